# Optimizing a Trainium2 kernel written in Bass

```python
import math
import jax, jax.numpy as jnp
from jax import lax
import numpy as np

D_MODEL = 1024
BATCH = 16
SEQ = 2048
DEPTH = 2

GRID_W = 64
CTX_LEN = 256
N_MIXERS = 2
N_HYENA = (DEPTH + 1) // 2
N_ATTN = DEPTH // 2
EPS = 1e-6

HY_STREAMS = 3
HY_SHORT = 3
HY_BANDS = 16
HY_EMB = 1 + 2 * HY_BANDS
HY_FILT_W = 64
HY_DECAY_TARGET = 1e-2
HY_FAST = 0.3
HY_SLOW = 1.5
HY_SHIFT = 0.0

HEAD_DIM = 128
N_HEADS = D_MODEL // HEAD_DIM
N_KV_HEADS = 2
GROUP = N_HEADS // N_KV_HEADS
QKV_DIM = (N_HEADS + 2 * N_KV_HEADS) * HEAD_DIM
ROPE_PAIRS = HEAD_DIM // 4
ROPE_THETA = 10000.0
Q_BLOCK = 128
ATTN_SCALE = HEAD_DIM ** -0.5

D_FF = 4 * D_MODEL

kernel_name = 'hyena_gqa_interleaved_dit_block'


def rms_norm(x, g):
    xf = x.astype(jnp.float32)
    y = xf * lax.rsqrt(jnp.mean(xf * xf, axis=-1, keepdims=True) + EPS)
    return (y * g).astype(x.dtype)


def modulation(cond, w, b):
    return jnp.split(jax.nn.silu(cond) @ w + b, 6, axis=-1)


def short_conv(z, w, b):
    L = z.shape[1]
    zp = jnp.pad(z, ((0, 0), (1, 1), (0, 0)))
    return zp[:, :L] * w[0] + zp[:, 1:L + 1] * w[1] + zp[:, 2:] * w[2] + b


def hyena_filters(L, w1, b1, f1, w2, b2, f2, w3):
    t = jnp.arange(L, dtype=jnp.float32) / L
    bands = jnp.linspace(1e-4, HY_BANDS - 1, HY_BANDS, dtype=jnp.float32)
    ang = 2.0 * math.pi * t[:, None] * bands[None, :]
    z = jnp.concatenate([t[:, None], jnp.cos(ang), jnp.sin(ang)], axis=-1)
    h = jnp.sin(f1 * (z @ w1 + b1))
    h = jnp.sin(f2 * (h @ w2 + b2))
    h = (h @ w3).astype(jnp.float32)
    deltas = jnp.abs(jnp.linspace(math.log(HY_DECAY_TARGET) / HY_SLOW,
                                  math.log(HY_DECAY_TARGET) / HY_FAST, D_MODEL, dtype=jnp.float32))
    decay = jnp.exp(-t[:, None] * deltas[None, :])
    h = h * (jnp.concatenate([decay, decay], axis=-1) + HY_SHIFT)
    return h[:, :D_MODEL], h[:, D_MODEL:]


def bidir_fftconv(u, h_f, h_b, bias):
    L = u.shape[1]
    k2 = jnp.concatenate([h_f, jnp.zeros((1, h_f.shape[1]), jnp.float32), h_b[:0:-1]], axis=0)
    kf = jnp.fft.rfft(k2, n=2 * L, axis=0)
    uf32 = u.astype(jnp.float32)
    uf = jnp.fft.rfft(uf32, n=2 * L, axis=1)
    y = jnp.fft.irfft(uf * kf[None], n=2 * L, axis=1)[:, :L]
    return (y + uf32 * bias).astype(u.dtype)


def hyena_mixer(h, w_in, b_in, conv_w, conv_b, fw1, fb1, ff1, fw2, fb2, ff2, fw3, fbias, w_out, b_out):
    L = h.shape[1]
    z = short_conv(h @ w_in + b_in, conv_w, conv_b)
    x1, x2, v = jnp.split(z, HY_STREAMS, axis=-1)
    h_f, h_b = hyena_filters(L, fw1, fb1, ff1, fw2, fb2, ff2, fw3)
    v = bidir_fftconv(v * x2, h_f, h_b, fbias)
    return (v * x1) @ w_out + b_out


def axial_rope_tables(rows):
    row = jnp.repeat(jnp.arange(rows, dtype=jnp.float32), GRID_W)
    col = jnp.tile(jnp.arange(GRID_W, dtype=jnp.float32), rows)
    inv = ROPE_THETA ** (-jnp.arange(ROPE_PAIRS, dtype=jnp.float32) / ROPE_PAIRS)
    ang = jnp.concatenate([row[:, None] * inv[None, :], col[:, None] * inv[None, :]], axis=-1)
    return jnp.cos(ang), jnp.sin(ang)


def apply_rope(x, cos, sin):
    xf = x.astype(jnp.float32).reshape(x.shape[:-1] + (HEAD_DIM // 2, 2))
    x0, x1 = xf[..., 0], xf[..., 1]
    out = jnp.stack([x0 * cos - x1 * sin, x0 * sin + x1 * cos], axis=-1)
    return out.reshape(x.shape).astype(x.dtype)


def project_qkv(h, w_qkv, q_gain, k_gain):
    B, L, _ = h.shape
    qkv = h @ w_qkv
    q = qkv[..., :N_HEADS * HEAD_DIM].reshape(B, L, N_KV_HEADS, GROUP, HEAD_DIM)
    k = qkv[..., N_HEADS * HEAD_DIM:(N_HEADS + N_KV_HEADS) * HEAD_DIM].reshape(B, L, N_KV_HEADS, HEAD_DIM)
    v = qkv[..., (N_HEADS + N_KV_HEADS) * HEAD_DIM:].reshape(B, L, N_KV_HEADS, HEAD_DIM)
    return rms_norm(q, q_gain), rms_norm(k, k_gain), v


def attend(q, k, v):
    s = jnp.einsum('bqkgd,bskd->bkgqs', q, k).astype(jnp.float32) * ATTN_SCALE
    p = jax.nn.softmax(s, axis=-1).astype(v.dtype)
    return jnp.einsum('bkgqs,bskd->bqkgd', p, v)


def gqa_mixer(h_lat, h_ctx, w_qkv, q_gain, k_gain, w_o, cos, sin, with_ctx_out):
    B, L, _ = h_lat.shape
    C = h_ctx.shape[1]
    q_l, k_l, v_l = project_qkv(h_lat, w_qkv, q_gain, k_gain)
    q_l = apply_rope(q_l, cos[:, None, None, :], sin[:, None, None, :])
    k_l = apply_rope(k_l, cos[:, None, :], sin[:, None, :])
    q_c, k_c, v_c = project_qkv(h_ctx, w_qkv, q_gain, k_gain)
    k_all = jnp.concatenate([k_l, k_c], axis=1)
    v_all = jnp.concatenate([v_l, v_c], axis=1)
    nb = L // Q_BLOCK
    q_blocks = jnp.moveaxis(q_l.reshape(B, nb, Q_BLOCK, N_KV_HEADS, GROUP, HEAD_DIM), 1, 0)
    o = lax.map(lambda qb: attend(qb, k_all, v_all), q_blocks)
    y_lat = jnp.moveaxis(o, 0, 1).reshape(B, L, N_HEADS * HEAD_DIM) @ w_o
    y_ctx = None
    if with_ctx_out:
        y_ctx = attend(q_c, k_c, v_c).reshape(B, C, N_HEADS * HEAD_DIM) @ w_o
    return y_lat, y_ctx


def sq_relu_mlp(h, w1, w2):
    return jnp.square(jax.nn.relu(h @ w1)) @ w2


def setup_inputs(seed: int = 0) -> dict:
    key = jax.random.key(seed)
    ks = iter(jax.random.split(key, 40))
    f32 = jnp.float32

    def nrm(shape, scale):
        return jax.random.normal(next(ks), shape, f32) * scale

    def gain(shape):
        return 1.0 + nrm(shape, 0.05)

    D = D_MODEL
    return {
        'x': nrm((BATCH, SEQ, D), 1.0),
        'c': nrm((BATCH, D), 1.0),
        'ctx': nrm((BATCH, CTX_LEN, D), 1.0),
        'c_ctx': nrm((D,), 1.0),
        'mod_w': nrm((DEPTH, D, 6 * D), 0.5 * D ** -0.5),
        'mod_b': nrm((DEPTH, 6 * D), 0.02),
        'mix_norm_pre': gain((DEPTH, D)),
        'mix_norm_post': gain((DEPTH, D)),
        'mlp_norm_pre': gain((DEPTH, D)),
        'mlp_norm_post': gain((DEPTH, D)),
        'mlp_w1': nrm((DEPTH, D, D_FF), D ** -0.5),
        'mlp_w2': nrm((DEPTH, D_FF, D), D_FF ** -0.5),
        'hy_w_in': nrm((N_HYENA, D, HY_STREAMS * D), D ** -0.5),
        'hy_b_in': nrm((N_HYENA, HY_STREAMS * D), 0.02),
        'hy_conv_w': nrm((N_HYENA, HY_SHORT, HY_STREAMS * D), HY_SHORT ** -0.5),
        'hy_conv_b': nrm((N_HYENA, HY_STREAMS * D), 0.02),
        'hy_filt_w1': nrm((N_HYENA, HY_EMB, HY_FILT_W), HY_EMB ** -0.5),
        'hy_filt_b1': nrm((N_HYENA, HY_FILT_W), 0.1),
        'hy_filt_freq1': gain((N_HYENA, HY_FILT_W)),
        'hy_filt_w2': nrm((N_HYENA, HY_FILT_W, HY_FILT_W), HY_FILT_W ** -0.5),
        'hy_filt_b2': nrm((N_HYENA, HY_FILT_W), 0.1),
        'hy_filt_freq2': gain((N_HYENA, HY_FILT_W)),
        'hy_filt_w3': nrm((N_HYENA, HY_FILT_W, 2 * D), HY_FILT_W ** -0.5),
        'hy_filt_bias': nrm((N_HYENA, D), 0.1),
        'hy_w_out': nrm((N_HYENA, D, D), D ** -0.5),
        'hy_b_out': nrm((N_HYENA, D), 0.02),
        'attn_w_qkv': nrm((N_ATTN, D, QKV_DIM), D ** -0.5),
        'attn_q_norm': gain((N_ATTN, HEAD_DIM)),
        'attn_k_norm': gain((N_ATTN, HEAD_DIM)),
        'attn_w_o': nrm((N_ATTN, N_HEADS * HEAD_DIM, D), D ** -0.5),
    }


def reference(x, c, ctx, c_ctx, mod_w, mod_b, mix_norm_pre, mix_norm_post, mlp_norm_pre, mlp_norm_post,
              mlp_w1, mlp_w2, hy_w_in, hy_b_in, hy_conv_w, hy_conv_b, hy_filt_w1, hy_filt_b1, hy_filt_freq1,
              hy_filt_w2, hy_filt_b2, hy_filt_freq2, hy_filt_w3, hy_filt_bias, hy_w_out, hy_b_out,
              attn_w_qkv, attn_q_norm, attn_k_norm, attn_w_o):
    L = x.shape[1]
    rows = L // GRID_W
    cos, sin = axial_rope_tables(rows)
    for i in range(DEPTH):
        last = i == DEPTH - 1
        j = i // N_MIXERS
        sh1, sc1, g1, sh2, sc2, g2 = [m[:, None, :] for m in modulation(c, mod_w[i], mod_b[i])]
        csh1, csc1, cg1, csh2, csc2, cg2 = modulation(c_ctx, mod_w[i], mod_b[i])

        hx = rms_norm(x, mix_norm_pre[i]) * (1.0 + sc1) + sh1
        hc = rms_norm(ctx, mix_norm_pre[i]) * (1.0 + csc1) + csh1
        if i % N_MIXERS == 0:
            hp = (hy_w_in[j], hy_b_in[j], hy_conv_w[j], hy_conv_b[j], hy_filt_w1[j], hy_filt_b1[j],
                  hy_filt_freq1[j], hy_filt_w2[j], hy_filt_b2[j], hy_filt_freq2[j], hy_filt_w3[j],
                  hy_filt_bias[j], hy_w_out[j], hy_b_out[j])
            yx = hyena_mixer(hx, *hp)
            yc = None if last else hyena_mixer(hc, *hp)
        else:
            yx, yc = gqa_mixer(hx, hc, attn_w_qkv[j], attn_q_norm[j], attn_k_norm[j], attn_w_o[j],
                               cos, sin, not last)
        x = x + g1 * rms_norm(yx, mix_norm_post[i])
        if not last:
            ctx = ctx + cg1 * rms_norm(yc, mix_norm_post[i])

        hx = rms_norm(x, mlp_norm_pre[i]) * (1.0 + sc2) + sh2
        x = x + g2 * rms_norm(sq_relu_mlp(hx, mlp_w1[i], mlp_w2[i]), mlp_norm_post[i])
        if not last:
            hc = rms_norm(ctx, mlp_norm_pre[i]) * (1.0 + csc2) + csh2
            ctx = ctx + cg2 * rms_norm(sq_relu_mlp(hc, mlp_w1[i], mlp_w2[i]), mlp_norm_post[i])
    return x
```

```python
import math
import numpy as np
import ml_dtypes
import concourse.bass as bass
import concourse.mybir as mybir
from concourse.bass_utils import run_bass_kernel_spmd

F32 = mybir.dt.float32
BF16 = mybir.dt.bfloat16
AF = mybir.ActivationFunctionType
ALU = mybir.AluOpType
AX = mybir.AxisListType

D = 1024
L = 2048
C = 256
T = L + C
NB = 2
DFF = 4096
EPS = 1e-6
NT = T // 128
NTX = L // 128
HD = 128
NH = 8
NKV = 2
N_DFT = 2 * L
N_DFTC = 2 * C


class Buf:
    __slots__ = ("name", "w", "r")

    def __init__(self, name):
        self.name = name
        self.w = {}
        self.r = []


class Tile:
    def __init__(self, h, name, nsub=1):
        self.h = h
        self.name = name
        self.bufs = [Buf(f"{name}.{i}") for i in range(nsub)]

    def __getitem__(self, idx):
        return self.h[idx]

    @property
    def b(self):
        return self.bufs[0]


class KB:
    def __init__(self):
        nc = self.nc = bass.Bass("TRN2", target_bir_lowering=False)
        self.E = {"pe": nc.tensor, "act": nc.scalar, "dve": nc.vector, "pool": nc.gpsimd, "sp": nc.sync}
        self.sem = {e: nc.alloc_semaphore("s_" + e) for e in ("pe", "act", "dve", "pool")}
        self.tick = {e: 0 for e in self.sem}
        self.seen = {e: {} for e in self.E}
        self.dsem = {}
        self.all_tokens = {}
        self.sb_off = (nc.sbuf_base + 31) // 32 * 32
        self.sb_top = nc.sbuf_top // 32 * 32
        self.top_off = self.sb_top
        self.rel = []
        self.sb_peak = 0
        self.n_names = 0
        self.dram = {}

    def _inherit(self, lo, hi, tile):
        toks = {}
        for (a, b_, tl) in self.rel:
            if a < hi and lo < b_:
                for tok in tl:
                    if tok[0] not in toks or toks[tok[0]][1] < tok[1]:
                        toks[tok[0]] = tok
        if toks:
            for buf in tile.bufs:
                buf.r = list(toks.values())

    @staticmethod
    def _per(shape, dtype):
        per = 4 if dtype == F32 else 2
        for s_ in shape[1:]:
            per *= s_
        return (per + 31) // 32 * 32

    def sb(self, name, shape, dtype, nsub=1):
        per = self._per(shape, dtype)
        off = self.sb_off
        assert off + per <= self.top_off, f"SBUF overflow allocating {name}: {off}+{per} > {self.top_off}"
        self.n_names += 1
        h = self.nc.alloc_sbuf_tensor_at(f"{name}_{self.n_names}", list(shape), dtype, offset=off)
        self.sb_off = off + per
        self.sb_peak = max(self.sb_peak, self.sb_off + (self.sb_top - self.top_off))
        t = Tile(h, name, nsub)
        self._inherit(off, off + per, t)
        return t

    def sbt(self, name, shape, dtype, nsub=1):
        per = self._per(shape, dtype)
        off = self.top_off - per
        assert off >= self.sb_off, f"SBUF overflow (top) allocating {name}: {off} < {self.sb_off}"
        self.n_names += 1
        h = self.nc.alloc_sbuf_tensor_at(f"{name}_{self.n_names}", list(shape), dtype, offset=off)
        self.top_off = off
        self.sb_peak = max(self.sb_peak, self.sb_off + (self.sb_top - self.top_off))
        t = Tile(h, name, nsub)
        self._inherit(off, off + per, t)
        return t

    def sb_at(self, name, shape, dtype, off, alias_of=(), nsub=1):
        self.n_names += 1
        h = self.nc.alloc_sbuf_tensor_at(f"{name}_{self.n_names}", list(shape), dtype, offset=off)
        t = Tile(h, name, nsub)
        toks = {}
        for a in alias_of:
            for buf in a.bufs:
                for tok in list(buf.w.values()) + list(buf.r):
                    if tok[0] not in toks or toks[tok[0]][1] < tok[1]:
                        toks[tok[0]] = tok
        for buf in t.bufs:
            buf.r = list(toks.values())
        return t

    def mark(self):
        return self.sb_off

    def release(self, mark):
        if self.sb_off > mark:
            self.rel.append((mark, self.sb_off, list(self.all_tokens.values())))
        self.sb_off = mark

    def mark_top(self):
        return self.top_off

    def release_top(self, mark):
        if mark > self.top_off:
            self.rel.append((self.top_off, mark, list(self.all_tokens.values())))
        self.top_off = mark

    def dram_t(self, name, shape, dtype, kind="Internal", nsub=1):
        h = self.nc.dram_tensor(name, list(shape), dtype, kind=kind)
        t = Tile(h.ap(), name, nsub)
        self.dram[name] = t
        return t

    def _collect(self, eng, reads, writes, is_dma):
        need = {}

        def add(tok, raw):
            if tok is None:
                return
            s, v, src = tok
            if (not is_dma) and src == eng and not raw and eng == "pe":
                return
            if need.get(s, 0) < v:
                need[s] = v

        for b in reads:
            for t in b.w.values():
                add(t, True)
        for b in writes:
            for t in b.w.values():
                add(t, False)
            for t in b.r:
                add(t, False)
        seen = self.seen[eng]
        out = []
        for s, v in need.items():
            if seen.get(s, 0) >= v:
                continue
            seen[s] = v
            out.append((s, v))
        return out

    def _emit_waits(self, eng, waits):
        e = self.E[eng]
        for s, v in waits:
            e.wait_ge(s, v)

    def _finish(self, tok, reads, writes):
        for b in writes:
            b.w[tok[0]] = tok
            b.r = []
        for b in reads:
            b.r.append(tok)
            if len(b.r) > 64:
                mx = {}
                for (s, v, src) in b.r:
                    if s not in mx or mx[s][1] < v:
                        mx[s] = (s, v, src)
                b.r = list(mx.values())
        self.all_tokens[tok[0]] = tok

    def op(self, eng, reads, writes, fn):
        waits = self._collect(eng, reads, writes, False)
        self._emit_waits(eng, waits)
        last = fn()
        self.tick[eng] += 1
        last.then_inc(self.sem[eng], 1)
        tok = (self.sem[eng], self.tick[eng], eng)
        self._finish(tok, reads, writes)
        return tok

    def dma(self, q, out_ap, in_ap, reads, writes, key):
        waits = self._collect(q, reads, writes, True)
        self._emit_waits(q, waits)
        if key not in self.dsem:
            self.dsem[key] = [self.nc.alloc_semaphore("d_" + key), 0]
        ds = self.dsem[key]
        ins = self.E[q].dma_start(out=out_ap, in_=in_ap)
        ds[1] += 16
        ins.then_inc(ds[0], 16)
        tok = (ds[0], ds[1], "dma")
        self._finish(tok, reads, writes)
        return tok

    def group_done(self, key, tiles):
        ds = self.dsem[key]
        tok = (ds[0], ds[1], "dma")
        for t in tiles:
            for b in t.bufs:
                b.w[ds[0]] = tok

    def barrier(self):
        toks = list(self.all_tokens.values())
        for eng in self.E:
            seen = self.seen[eng]
            for (s, v, src) in toks:
                if seen.get(s, 0) >= v:
                    continue
                seen[s] = v
                self.E[eng].wait_ge(s, v)

    def final_wait(self):
        toks = list(self.all_tokens.values())
        seen = self.seen["sp"]
        for (s, v, src) in toks:
            if seen.get(s, 0) >= v:
                continue
            seen[s] = v
            self.E["sp"].wait_ge(s, v)


_CONST_CACHE = {}


def _bf(a):
    return np.ascontiguousarray(a.astype(ml_dtypes.bfloat16))


def _dft_consts(Ls):
    N = 2 * Ls
    half = N // 2
    n = np.arange(N, dtype=np.float64)
    f = np.arange(half, dtype=np.float64) + 0.5
    k2 = (np.arange(half, dtype=np.int64) * 2 + 1)[:, None] * np.arange(N, dtype=np.int64)[None, :]
    k2 = k2 % (2 * N)
    th = k2.astype(np.float64) * (math.pi / N)
    Fre = np.cos(th)
    Fim = -np.sin(th)
    Fall = np.concatenate([Fre, Fim], axis=0)
    nfc = N // 128
    ntile = N // 128
    FKT = Fall.reshape(nfc, 128, ntile, 128).transpose(0, 3, 2, 1)
    Gre = (2.0 / N) * np.cos(th[:, :Ls])
    Gim = -(2.0 / N) * np.sin(th[:, :Ls])
    Gall = np.concatenate([Gre, Gim], axis=0)
    tch = min(Ls, 256)
    ntch = Ls // tch
    GT = Gall.reshape(nfc, 128, ntch, tch).transpose(2, 1, 0, 3)
    return _bf(FKT), _bf(GT)


def _filter_pos(Ls):
    t = np.arange(Ls, dtype=np.float32) / np.float32(Ls)
    bands = np.linspace(1e-4, 16 - 1, 16, dtype=np.float32)
    ang = (np.float32(2.0 * math.pi) * t[:, None] * bands[None, :]).astype(np.float32)
    z = np.concatenate([t[:, None], np.cos(ang), np.sin(ang)], axis=-1).astype(np.float32)
    idx = (Ls - np.arange(Ls)) % Ls
    zr = z[idx]
    tr = t[idx]
    sgn_f = np.ones(Ls, np.float32)
    sgn_r = -np.ones(Ls, np.float32)
    sgn_r[0] = 0.0
    return z, zr, t, tr, sgn_f, sgn_r


def get_consts():
    if _CONST_CACHE:
        return _CONST_CACHE
    cc = {}
    cc["fkt"], cc["gt"] = _dft_consts(L)
    cc["fktc"], cc["gtc"] = _dft_consts(C)
    z, zr, t, tr, sf, sr = _filter_pos(L)
    zc, zrc, tc_, trc, sfc, src_ = _filter_pos(C)
    cc["zT"] = np.ascontiguousarray(np.concatenate([z.T, zr.T, zc.T, zrc.T], axis=1))
    def cols(v):
        return v.reshape(-1, 128).T
    cc["tcol"] = np.ascontiguousarray(np.concatenate(
        [cols(-t), cols(-tr), cols(-tc_), cols(-trc), cols(sf), cols(sr), cols(sfc), cols(src_)], axis=1)).astype(np.float32)
    deltas = np.abs(np.linspace(math.log(1e-2) / 1.5, math.log(1e-2) / 0.3, D, dtype=np.float32)).astype(np.float32)
    cc["delta"] = deltas.reshape(1, D)
    cc["ident"] = np.eye(128, dtype=np.float32)
    sel = np.zeros((3, 3, 128), np.float32)
    for r in range(3):
        sel[r, r, :] = 1.0
    cc["sel3"] = sel.transpose(1, 0, 2).copy()
    rows = L // 64
    row = np.repeat(np.arange(rows, dtype=np.float32), 64)
    col = np.tile(np.arange(64, dtype=np.float32), rows)
    inv = (np.float32(10000.0) ** (-np.arange(32, dtype=np.float32) / np.float32(32))).astype(np.float32)
    ang = np.concatenate([row[:, None] * inv[None, :], col[:, None] * inv[None, :]], axis=-1).astype(np.float32)
    cc["ropec"] = np.ascontiguousarray(np.cos(ang).astype(np.float32).reshape(NTX, 128, 64).transpose(1, 0, 2))
    cc["ropes"] = np.ascontiguousarray(np.sin(ang).astype(np.float32).reshape(NTX, 128, 64).transpose(1, 0, 2))
    _CONST_CACHE.update(cc)
    return cc


def _colv_layout():
    lay = {}
    off = 0

    def add(name, n):
        nonlocal off
        lay[name] = off
        off += n

    for l in range(2):
        add(f"mix_pre{l}", 8)
        add(f"mlp_pre{l}", 8)
    add("hy_b_in", 24)
    add("hy_cw0", 24)
    add("hy_cw1", 24)
    add("hy_cw2", 24)
    add("hy_cb", 24)
    add("hy_fbias", 8)
    add("f_b1", 1)
    add("f_f1", 1)
    add("f_b2", 1)
    add("f_f2", 1)
    for l in range(2):
        add(f"mod_b{l}", 48)
    lay["_n"] = off
    return lay


COLV = _colv_layout()


def _pcol(v):
    return np.asarray(v, np.float32).reshape(-1, 128).T


def build_colv(inp):
    a = np.zeros((128, COLV["_n"]), np.float32)

    def put(name, v):
        m = _pcol(v)
        a[:, COLV[name]:COLV[name] + m.shape[1]] = m

    for l in range(2):
        put(f"mix_pre{l}", inp["mix_norm_pre"][l])
        put(f"mlp_pre{l}", inp["mlp_norm_pre"][l])
        put(f"mod_b{l}", inp["mod_b"][l])
    put("hy_b_in", inp["hy_b_in"][0])
    for i in range(3):
        put(f"hy_cw{i}", inp["hy_conv_w"][0, i])
    put("hy_cb", inp["hy_conv_b"][0])
    put("hy_fbias", inp["hy_filt_bias"][0])
    for nm, key in (("f_b1", "hy_filt_b1"), ("f_f1", "hy_filt_freq1"), ("f_b2", "hy_filt_b2"), ("f_f2", "hy_filt_freq2")):
        a[:64, COLV[nm]] = inp[key][0]
    return a


ROW_MIX_POST = (0, 2)
ROW_MLP_POST = (1, 3)
ROW_HY_B_OUT = 4
ROW_DELTA = 5
ROW_QN = 6
ROW_KN = 7
ROW_MODB_G = ((8, 9), (10, 11))
N_ROWS = 12


def build_rows(inp, cc):
    r = np.zeros((N_ROWS, D), np.float32)
    r[0] = inp["mix_norm_post"][0]
    r[1] = inp["mlp_norm_post"][0]
    r[2] = inp["mix_norm_post"][1]
    r[3] = inp["mlp_norm_post"][1]
    r[4] = inp["hy_b_out"][0]
    r[5] = cc["delta"][0]
    r[6, :HD] = inp["attn_q_norm"][0]
    r[7, :HD] = inp["attn_k_norm"][0]
    for l in range(2):
        r[8 + 2 * l] = inp["mod_b"][l, 2048:3072]
        r[9 + 2 * l] = inp["mod_b"][l, 5120:6144]
    return r


INPUT_SPECS = [
    ("x", [NB, L, D], F32), ("ctx", [NB, C, D], F32), ("cT", [128, 8, 3], F32),
    ("mod_w", [2, D, 6 * D], F32), ("colv", [128, COLV["_n"]], F32), ("rows", [N_ROWS, D], F32),
    ("mlp_w1", [2, D, DFF], F32), ("mlp_w2", [2, DFF, D], F32),
    ("hy_w_in", [D, 3 * D], F32), ("hy_w_out", [D, D], F32),
    ("attn_w_qkv", [D, 1536], F32), ("attn_w_o", [D, D], F32),
    ("f_w1", [33, 64], F32), ("f_w2", [64, 64], F32), ("f_w3", [64, 2 * D], F32),
    ("fkt", [32, 128, 32, 128], BF16), ("gt", [8, 128, 32, 256], BF16),
    ("fktc", [4, 128, 4, 128], BF16), ("gtc", [1, 128, 4, 256], BF16),
    ("zT", [33, 2 * L + 2 * C], F32), ("tcol", [128, 72], F32),
    ("ident", [128, 128], F32), ("sel3", [3, 3, 128], F32),
    ("ropec", [128, NTX, 64], F32), ("ropes", [128, NTX, 64], F32),
]


class Prog:
    def __init__(self, taps=(), stop_after=None):
        self.k = k = KB()
        nc = k.nc
        self.taps = set(taps)
        self.stop_after = stop_after
        self.io = {}
        for name, shape, dt in INPUT_SPECS:
            self.io[name] = nc.dram_tensor(name, list(shape), dt, kind="ExternalInput").ap()
        self.out = nc.dram_tensor("out", [NB, L, D], F32, kind="ExternalOutput").ap()
        self.tap_aps = {}
        self.XS = k.dram_t("XS", [NB, T, D], F32, nsub=NB * NT)
        self.KF = k.dram_t("KF", [32, 128, D], F32, nsub=32)
        self.KFC = k.dram_t("KFC", [4, 128, D], F32, nsub=4)
        self.X1T = k.dram_t("X1T", [NB, 8, 128, T], BF16, nsub=NB * 8)
        self.UT = k.dram_t("UT", [NB, 8, 128, T], BF16, nsub=NB * 8)
        self.GGD = k.dram_t("GGD", [2, 2, 3, 128, D], F32, nsub=12)
        self.ps = [Tile(nc.alloc_psum_tensor(f"ps{i}", [128, 512], F32), f"ps{i}") for i in range(8)]
        self.ps_rr = 0
        self.ps_pool = self.ps

    def tap(self, name, shape, dtype=F32):
        ap = self.k.nc.dram_tensor("tap_" + name, list(shape), dtype, kind="ExternalOutput").ap()
        self.tap_aps[name] = ap
        return ap

    def next_ps(self):
        pool = self.ps_pool
        p = pool[self.ps_rr % len(pool)]
        self.ps_rr += 1
        return p

    def load_consts(self):
        k, nc, io = self.k, self.k.nc, self.io
        self.colv = k.sb("colv", [128, COLV["_n"]], F32)
        k.dma("sp", self.colv[:], io["colv"], [], [self.colv.b], "c_grp0")
        self.ident = k.sb("ident", [128, 128], BF16)
        k.dma("pool", self.ident[:], io["ident"], [], [self.ident.b], "c_ident")
        self.ones = k.sb("ones", [128, 128], BF16)
        k.op("dve", [], [self.ones.b], lambda: nc.vector.memset(self.ones[:], 1.0))
        self.AS = k.sb("AS", [128, 2, 2, 2, 3, 8], F32)
        self.epsc = k.sb("epsc", [128, 1], F32)
        k.op("dve", [], [self.epsc.b], lambda: nc.vector.memset(self.epsc[:], EPS))

    def cv(self, name, n=1, off=0):
        o = COLV[name] + off
        return self.colv[:, o:o + n]

    def phase_mod_gen(self):
        k, nc, io = self.k, self.k.nc, self.io
        mTop = k.mark_top()
        cT = k.sbt("cT", [128, 24], F32)
        scT = k.sbt("scT", [128, 8, 3], BF16)
        k.dma("sp", cT[:], io["cT"].rearrange("p k r -> p (k r)"), [], [cT.b], "c_cT")
        k.op("act", [cT.b], [scT.b], lambda: nc.scalar.activation(
            out=scT[:].rearrange("p k r -> p (k r)"), in_=cT[:], func=AF.Silu))
        sel = k.sbt("sel3", [3, 3, 128], F32)
        k.dma("sp", sel[:], io["sel3"], [], [sel.b], "c_sel")
        modF = k.sbt("modF", [128, 2, 48, 3], F32)
        tmpr = k.sbt("tmpr", [3, 512], F32)
        W = [k.sbt(f"modW{i}", [128, 8, 512], BF16) for i in range(2)]
        GGs = [k.sbt(f"GGs{i}", [128, D], F32) for i in range(2)]
        R3 = k.sbt("R3", [3, 4, D], F32)
        ggrow = k.sbt("ggrow", [3, 2, D], F32)
        wi = 0
        gi = 0
        for l in range(2):
            for i, r in enumerate((ROW_MIX_POST[l], ROW_MLP_POST[l], ROW_MODB_G[l][0], ROW_MODB_G[l][1])):
                k.dma("sp", R3[:, i, :], io["rows"][r:r + 1, :].partition_broadcast(3), [], [R3.b], "c_R3")
            for cb in range(12):
                blk = cb // 2
                w = W[wi % 2]
                wi += 1
                k.dma("pool", w[:], io["mod_w"][l, :, cb * 512:(cb + 1) * 512].rearrange("(k p) n -> p k n", p=128),
                      [], [w.b], f"modW{wi % 2}")
                ps = self.next_ps()
                if blk in (2, 5):
                    sub = 0 if blk == 2 else 1
                    half = cb % 2

                    def mm_row(ps=ps, w=w):
                        last = None
                        for kk in range(8):
                            last = nc.tensor.matmul(ps[0:3, :], scT[:, kk, :], w[:, kk, :], start=(kk == 0), stop=(kk == 7))
                        return last
                    k.op("pe", [scT.b, w.b], [ps.b], mm_row)
                    rb = 2 + sub
                    rg = sub
                    k.op("dve", [ps.b, R3.b], [tmpr.b], lambda ps=ps, rb=rb, half=half: nc.vector.tensor_tensor(
                        out=tmpr[:], in0=ps[0:3, :], in1=R3[:, rb, half * 512:(half + 1) * 512], op=ALU.add))
                    k.op("dve", [tmpr.b, R3.b], [ggrow.b], lambda sub=sub, rg=rg, half=half: nc.vector.tensor_tensor(
                        out=ggrow[:, sub, half * 512:(half + 1) * 512], in0=tmpr[:],
                        in1=R3[:, rg, half * 512:(half + 1) * 512], op=ALU.mult))
                else:
                    def mm_f(ps=ps, w=w):
                        last = None
                        for q in range(4):
                            for kk in range(8):
                                last = nc.tensor.matmul(ps[:, q * 3:(q + 1) * 3], w[:, kk, q * 128:(q + 1) * 128], scT[:, kk, :],
                                                        start=(kk == 0), stop=(kk == 7))
                        return last
                    k.op("pe", [scT.b, w.b], [ps.b], mm_f)
                    bcol = self.cv(f"mod_b{l}", 4, cb * 4).unsqueeze(2).broadcast_to([128, 4, 3])
                    k.op("dve", [ps.b, self.colv.b], [modF.b], lambda ps=ps, l=l, cb=cb, bcol=bcol: nc.vector.tensor_tensor(
                        out=modF[:, l, cb * 4:(cb + 1) * 4, :], in0=ps[:, 0:12].rearrange("p (q r) -> p q r", r=3),
                        in1=bcol, op=ALU.add))
                yield
            for sub in range(2):
                sh0 = 0 if sub == 0 else 24
                sc0 = 8 if sub == 0 else 32
                gname = (f"mix_pre{l}" if sub == 0 else f"mlp_pre{l}")
                for r in range(3):
                    k.op("dve", [modF.b, self.colv.b], [self.AS.b], lambda l=l, sub=sub, r=r, sc0=sc0, gname=gname:
                         nc.vector.scalar_tensor_tensor(out=self.AS[:, l, sub, 0, r, :], in0=modF[:, l, sc0:sc0 + 8, r], scalar=1.0,
                                                        in1=self.cv(gname, 8), op0=ALU.add, op1=ALU.mult))
                    k.op("dve", [modF.b], [self.AS.b], lambda l=l, sub=sub, r=r, sh0=sh0:
                         nc.vector.tensor_copy(out=self.AS[:, l, sub, 1, r, :], in_=modF[:, l, sh0:sh0 + 8, r]))
            for sub in range(2):
                for r in range(3):
                    g = GGs[gi % 2]
                    gi += 1
                    for half in range(2):
                        ps = self.next_ps()
                        k.op("pe", [sel.b, ggrow.b], [ps.b], lambda ps=ps, sub=sub, r=r, half=half: nc.tensor.matmul(
                            ps[:, :], sel[:, r, :], ggrow[:, sub, half * 512:(half + 1) * 512], start=True, stop=True))
                        k.op("act", [ps.b], [g.b], lambda ps=ps, g=g, half=half: nc.scalar.copy(
                            out=g[:, half * 512:(half + 1) * 512], in_=ps[:, :]))
                    k.dma("act", self.GGD[l, sub, r], g[:], [g.b], [self.GGD.bufs[(l * 2 + sub) * 3 + r]], f"st_GG{(gi - 1) % 2}")
                yield
        k.release_top(mTop)

    def phase_mod(self):
        for _ in self.phase_mod_gen():
            pass

    def _range_reduce(self, ta, tn, csz):
        k, nc = self.k, self.k.nc
        MAGIC = 12582912.0
        k.op("dve", [ta.b], [tn.b], lambda: nc.vector.tensor_scalar(
            out=tn[:, 0:csz], in0=ta[:, 0:csz], scalar1=1.0 / (2.0 * math.pi), scalar2=MAGIC, op0=ALU.mult, op1=ALU.add))
        k.op("dve", [tn.b], [tn.b], lambda: nc.vector.tensor_scalar(
            out=tn[:, 0:csz], in0=tn[:, 0:csz], scalar1=-MAGIC, scalar2=-2.0 * math.pi, op0=ALU.add, op1=ALU.mult))
        k.op("dve", [tn.b, ta.b], [ta.b], lambda: nc.vector.tensor_tensor(
            out=ta[:, 0:csz], in0=ta[:, 0:csz], in1=tn[:, 0:csz], op=ALU.add))

    def phase_filter(self, inter=None):
        k, nc, io = self.k, self.k.nc, self.io
        m0 = k.mark()

        def tick():
            if inter is not None:
                next(inter, None)
        zT = k.sb("zT", [33, 2 * L + 2 * C], F32)
        k.dma("sp", zT[:], io["zT"], [], [zT.b], "c_grp1")
        w1 = k.sb("fw1", [33, 64], F32)
        w2 = k.sb("fw2", [64, 64], F32)
        w3 = k.sb("fw3", [64, 2 * D], F32)
        k.dma("sp", w1[:], io["f_w1"], [], [w1.b], "c_grp1")
        k.dma("sp", w2[:], io["f_w2"], [], [w2.b], "c_grp1")
        k.dma("sp", w3[:], io["f_w3"], [], [w3.b], "c_grp1")
        tcol = k.sb("tcol", [128, 72], F32)
        k.dma("sp", tcol[:], io["tcol"], [], [tcol.b], "c_grp1")
        delta = k.sb("delta", [128, D], F32)
        k.dma("sp", delta[:], io["rows"][ROW_DELTA:ROW_DELTA + 1, :].partition_broadcast(128), [], [delta.b], "c_grp1")
        k.group_done("c_grp1", [zT, w1, w2, w3, tcol, delta])
        fb = k.sb("fb", [64, 2], F32)
        k.op("dve", [self.colv.b], [fb.b], lambda: nc.vector.tensor_tensor(
            out=fb[:, 0:1], in0=self.colv[0:64, COLV["f_f1"]:COLV["f_f1"] + 1], in1=self.colv[0:64, COLV["f_b1"]:COLV["f_b1"] + 1], op=ALU.mult))
        k.op("dve", [self.colv.b], [fb.b], lambda: nc.vector.tensor_tensor(
            out=fb[:, 1:2], in0=self.colv[0:64, COLV["f_f2"]:COLV["f_f2"] + 1], in1=self.colv[0:64, COLV["f_b2"]:COLV["f_b2"] + 1], op=ALU.mult))
        negpi = k.sb("negpi", [128, 1], F32)
        k.op("dve", [], [negpi.b], lambda: nc.vector.memset(negpi[:], -math.pi))
        OFF = math.pi + 2.0 * math.pi * 8
        TWO_PI = 2.0 * math.pi
        K2 = k.sb("K2", [128, 32, D], BF16, nsub=32)
        K2c = k.sb("K2c", [128, 4, D], BF16, nsub=4)
        h1 = k.sb("h1T", [64, 512], F32)
        h2 = [k.sb(f"h2T{i}", [64, 512], F32) for i in range(2)]
        targ = [k.sb(f"targ{i}", [64, 512], F32) for i in range(2)]
        dec = [k.sb(f"dec{i}", [128, D], F32) for i in range(2)]
        tn = k.sb("tn", [64, 512], F32)
        cnt = 0
        jobs = [(0, L, K2, 0, 0, 0, 36), (L, L, K2, 16, D, 16, 52),
                (2 * L, C, K2c, 0, 0, 32, 68), (2 * L + C, C, K2c, 2, D, 34, 70)]
        for (z0, Ls, KB_, t0, wc0, tc0, sg0) in jobs:
            csz = min(Ls, 512)
            for ch in range(Ls // csz):
                c0 = z0 + ch * csz
                ps = self.next_ps()
                k.op("pe", [w1.b, zT.b], [ps.b], lambda ps=ps, c0=c0, csz=csz: nc.tensor.matmul(
                    ps[0:64, 0:csz], w1[:, :], zT[:, c0:c0 + csz], start=True, stop=True))
                ta = targ[cnt % 2]
                k.op("dve", [ps.b, self.colv.b, fb.b], [ta.b], lambda ps=ps, ta=ta, csz=csz: nc.vector.tensor_scalar(
                    out=ta[:, 0:csz], in0=ps[0:64, 0:csz], scalar1=self.colv[0:64, COLV["f_f1"]:COLV["f_f1"] + 1],
                    scalar2=fb[:, 0:1], op0=ALU.mult, op1=ALU.add))
                self._range_reduce(ta, tn, csz)
                k.op("act", [ta.b], [h1.b], lambda ta=ta, csz=csz: nc.scalar.activation(
                    out=h1[:, 0:csz], in_=ta[:, 0:csz], func=AF.Sin))
                ps2 = self.next_ps()
                k.op("pe", [w2.b, h1.b], [ps2.b], lambda ps2=ps2, csz=csz: nc.tensor.matmul(
                    ps2[0:64, 0:csz], w2[:, :], h1[:, 0:csz], start=True, stop=True))
                tb = targ[(cnt + 1) % 2]
                k.op("dve", [ps2.b, self.colv.b, fb.b], [tb.b], lambda ps2=ps2, tb=tb, csz=csz: nc.vector.tensor_scalar(
                    out=tb[:, 0:csz], in0=ps2[0:64, 0:csz], scalar1=self.colv[0:64, COLV["f_f2"]:COLV["f_f2"] + 1],
                    scalar2=fb[:, 1:2], op0=ALU.mult, op1=ALU.add))
                self._range_reduce(tb, tn, csz)
                hh = h2[cnt % 2]
                k.op("act", [tb.b], [hh.b], lambda tb=tb, hh=hh, csz=csz: nc.scalar.activation(
                    out=hh[:, 0:csz], in_=tb[:, 0:csz], func=AF.Sin))
                for tt in range(csz // 128):
                    tile_i = ch * (csz // 128) + tt
                    dc = dec[(cnt + tt) % 2]
                    k.op("act", [delta.b, tcol.b], [dc.b], lambda dc=dc, tc0=tc0, tile_i=tile_i: nc.scalar.activation(
                        out=dc[:], in_=delta[:], func=AF.Exp, scale=tcol[:, tc0 + tile_i:tc0 + tile_i + 1]))
                    for cc_ in range(2):
                        ps3 = self.next_ps()
                        k.op("pe", [hh.b, w3.b], [ps3.b], lambda ps3=ps3, hh=hh, tt=tt, wc0=wc0, cc_=cc_: nc.tensor.matmul(
                            ps3[:, :], hh[:, tt * 128:(tt + 1) * 128], w3[:, wc0 + cc_ * 512:wc0 + (cc_ + 1) * 512],
                            start=True, stop=True))
                        kb = KB_.bufs[t0 + tile_i]
                        k.op("dve", [ps3.b, dc.b, tcol.b], [kb], lambda ps3=ps3, dc=dc, KB_=KB_, t0=t0, tile_i=tile_i, cc_=cc_, sg0=sg0:
                             nc.vector.scalar_tensor_tensor(out=KB_[:, t0 + tile_i, cc_ * 512:(cc_ + 1) * 512], in0=ps3[:, :],
                                                            scalar=tcol[:, sg0 + tile_i:sg0 + tile_i + 1],
                                                            in1=dc[:, cc_ * 512:(cc_ + 1) * 512], op0=ALU.mult, op1=ALU.mult))
                cnt += 1
        if "K2" in self.taps:
            k.dma("sp", self.tap("K2", [128, 32 * D], BF16), K2[:].rearrange("p a b -> p (a b)"), K2.bufs, [], "tap")
            k.dma("sp", self.tap("K2c", [128, 4 * D], BF16), K2c[:].rearrange("p a b -> p (a b)"), K2c.bufs, [], "tap")
        FK = [k.sb(f"FK{i}", [128, 32, 128], BF16) for i in range(2)]
        KFs = [k.sb(f"KFs{i}", [128, D], F32) for i in range(2)]
        for (nfc, ntile, src, KB_, dst, nm) in ((32, 32, io["fkt"], K2, self.KF, "m"), (4, 4, io["fktc"], K2c, self.KFC, "c")):
            for fc in range(nfc):
                fk = FK[fc % 2]
                k.dma("sp", fk[:, 0:ntile, :], src[fc], [], [fk.b], f"FK{fc % 2}")
                kf = KFs[fc % 2]
                for cc_ in range(2):
                    ps = self.next_ps()

                    def mmk(ps=ps, fk=fk, KB_=KB_, cc_=cc_, ntile=ntile):
                        last = None
                        for nt in range(ntile):
                            last = nc.tensor.matmul(ps[:, :], fk[:, nt, :], KB_[:, nt, cc_ * 512:(cc_ + 1) * 512],
                                                    start=(nt == 0), stop=(nt == ntile - 1))
                        return last
                    k.op("pe", [fk.b] + KB_.bufs, [ps.b], mmk)
                    k.op("act", [ps.b], [kf.b], lambda ps=ps, kf=kf, cc_=cc_: nc.scalar.copy(
                        out=kf[:, cc_ * 512:(cc_ + 1) * 512], in_=ps[:, :]))
                k.dma("act", dst[fc], kf[:], [kf.b], [dst.bufs[fc]], f"st_KF{fc % 2}")
                tick()
        if "KF" in self.taps:
            k.dma("sp", self.tap("KF", [32, 128, D]), self.KF[:], [self.KF.b], [], "tap")
            k.dma("sp", self.tap("KFC", [4, 128, D]), self.KFC[:], [self.KFC.b], [], "tap")
        if inter is not None:
            for _ in inter:
                pass
        k.release(m0)

    def psb(self, ps):
        return ps.h[:, :].bitcast(BF16)

    def alloc_norm_tmp(self, nslots=2):
        k = self.k
        self.ntmp = []
        for i in range(nslots):
            self.ntmp.append(dict(
                junk=k.sb(f"njunk{i}", [128, D], BF16), ss=k.sb(f"nss{i}", [128, 2], F32),
                rstd=k.sb(f"nrstd{i}", [128, 1], F32), xh=k.sb(f"nxh{i}", [128, D], BF16)))
        self.ntmp_i = 0

    def norm_stats(self, xt):
        k, nc = self.k, self.k.nc
        tm = self.ntmp[self.ntmp_i % len(self.ntmp)]
        self.ntmp_i += 1
        junk, ss, rstd, xh = tm["junk"], tm["ss"], tm["rstd"], tm["xh"]
        k.op("act", [xt.b], [junk.b, ss.b], lambda: nc.scalar.activation(
            out=junk[:], in_=xt[:], func=AF.Square, accum_out=ss[:, 0:1]))
        k.op("act", [ss.b, self.epsc.b], [rstd.b], lambda: nc.scalar.activation(
            out=rstd[:], in_=ss[:, 0:1], func=AF.Sqrt, scale=1.0 / D, bias=self.epsc[:, 0:1]))
        k.op("dve", [rstd.b], [rstd.b], lambda: nc.vector.reciprocal(out=rstd[:], in_=rstd[:]))
        k.op("dve", [xt.b, rstd.b], [xh.b], lambda: nc.vector.tensor_scalar(
            out=xh[:], in0=xt[:], scalar1=rstd[:, 0:1], scalar2=None, op0=ALU.mult))
        return tm

    def norm_transpose(self, tm, A, S, dst_fn, dst_bufs):
        k, nc = self.k, self.k.nc
        xh = tm["xh"]
        for half in range(2):
            ps = self.next_ps()
            pb = self.psb(ps)

            def tr(pb=pb, half=half):
                last = None
                for q in range(4):
                    kk = half * 4 + q
                    last = nc.tensor.transpose(pb[:, q * 128:(q + 1) * 128], xh[:, kk * 128:(kk + 1) * 128], self.ident[:])
                return last
            k.op("pe", [xh.b, self.ident.b], [ps.b], tr)
            for q in range(4):
                kk = half * 4 + q
                if half == 0:
                    k.op("act", [ps.b, self.AS.b], dst_bufs[0:1], lambda kk=kk, q=q, pb=pb: nc.scalar.activation(
                        out=dst_fn(kk), in_=pb[:, q * 128:(q + 1) * 128], func=AF.Identity,
                        scale=A[:, kk:kk + 1], bias=S[:, kk:kk + 1]))
                else:
                    k.op("dve", [ps.b, self.AS.b], dst_bufs[1:2], lambda kk=kk, q=q, pb=pb: nc.vector.tensor_scalar(
                        out=dst_fn(kk), in0=pb[:, q * 128:(q + 1) * 128], scalar1=A[:, kk:kk + 1], scalar2=S[:, kk:kk + 1],
                        op0=ALU.mult, op1=ALU.add))

    def norm_T(self, xt, A, S, dst_fn, dst_bufs):
        tm = self.norm_stats(xt)
        self.norm_transpose(tm, A, S, dst_fn, dst_bufs)

    def alloc_post_tmp(self):
        k = self.k
        self.ptmp = [dict(ss=k.sb(f"pss{i}", [128, 2], F32), rstd=k.sb(f"prstd{i}", [128, 1], F32),
                          junk=k.sb(f"pjunk{i}", [128, 512], BF16), tmp=k.sb(f"ptmp{i}", [128, 512], F32)) for i in range(2)]
        self.ptmp_i = 0

    def post_residual(self, ps2, xt, GG):
        k, nc = self.k, self.k.nc
        tm = self.ptmp[self.ptmp_i % 2]
        self.ptmp_i += 1
        ss, rstd, junk, tmp = tm["ss"], tm["rstd"], tm["junk"], tm["tmp"]
        for h in range(2):
            k.op("act", [ps2[h].b], [junk.b, ss.b], lambda h=h: nc.scalar.activation(
                out=junk[:], in_=ps2[h][:, :], func=AF.Square, accum_out=ss[:, h:h + 1]))
        k.op("dve", [ss.b], [rstd.b], lambda: nc.vector.tensor_tensor(out=rstd[:], in0=ss[:, 0:1], in1=ss[:, 1:2], op=ALU.add))
        k.op("act", [rstd.b, self.epsc.b], [rstd.b], lambda: nc.scalar.activation(
            out=rstd[:], in_=rstd[:], func=AF.Sqrt, scale=1.0 / D, bias=self.epsc[:, 0:1]))
        k.op("dve", [rstd.b], [rstd.b], lambda: nc.vector.reciprocal(out=rstd[:], in_=rstd[:]))
        for h in range(2):
            sl = slice(h * 512, (h + 1) * 512)
            k.op("dve", [ps2[h].b, rstd.b, GG.b], [tmp.b], lambda h=h, sl=sl: nc.vector.scalar_tensor_tensor(
                out=tmp[:], in0=ps2[h][:, :], scalar=rstd[:, 0:1], in1=GG[:, sl], op0=ALU.mult, op1=ALU.mult))
            k.op("dve", [tmp.b, xt.b], [xt.b], lambda sl=sl: nc.vector.tensor_tensor(
                out=xt[:, sl], in0=xt[:, sl], in1=tmp[:], op=ALU.add))

    def tok_src(self, b, tt, layer0):
        if layer0:
            if tt < NTX:
                return self.io["x"][b, tt * 128:(tt + 1) * 128, :]
            return self.io["ctx"][b, (tt - NTX) * 128:(tt - NTX + 1) * 128, :]
        return self.XS[b, tt * 128:(tt + 1) * 128, :]

    def phase_hyena(self, b):
        k, nc, io = self.k, self.k.nc, self.io
        mB = k.mark()
        U_off = k.sb_off
        U = k.sb("U", [128, NT, D], BF16, nsub=NT)
        mA = k.mark()
        hxT = k.sb("hxT", [128, 8, T], BF16, nsub=2 * NT)
        self.alloc_norm_tmp(3)
        xts = [k.sb(f"xt{i}", [128, D], F32) for i in range(2)]
        tms = {}
        for tt in range(NT + 1):
            if tt < NT:
                xt = xts[tt % 2]
                k.dma("sp", xt[:], self.tok_src(b, tt, True), [], [xt.b], f"xt{tt % 2}")
                tms[tt] = self.norm_stats(xt)
            if tt >= 1:
                t1 = tt - 1
                r = b if t1 < NTX else 2
                self.norm_transpose(tms.pop(t1), self.AS[:, 0, 0, 0, r, :], self.AS[:, 0, 0, 1, r, :],
                                    lambda kk, t1=t1: hxT[:, kk, t1 * 128:(t1 + 1) * 128], hxT.bufs[2 * t1:2 * t1 + 2])
        ZW = 2 + L + 2 + C
        zpad = [k.sb(f"zpad{s_}", [128, ZW], F32) for s_ in range(3)]
        for z in zpad:
            k.op("dve", [], [z.b], lambda z=z: nc.vector.memset(z[:], 0.0))
        acc = [k.sb(f"cacc{s_}", [128, T], F32) for s_ in range(3)]
        x1s = [k.sb(f"x1s{i}", [128, T], BF16) for i in range(2)]
        uTs = [k.sb(f"uTs{i}", [128, T], BF16) for i in range(2)]
        wj = [[k.sb(f"win{i}_{s_}", [128, 8, 128], BF16) for s_ in range(3)] for i in range(2)]
        chunks = [(1 + c * 512, c * 512, 512) for c in range(4)] + [(L + 3, L, C)]
        regions = [(0, 0, L), (L + 2, L, C)]
        def u_transposes(j):
            us_ = uTs[j % 2]
            for g0 in range(0, NT, 8):
                ng = min(8, NT - g0)
                ps = self.next_ps()
                pb = self.psb(ps)

                def tru(pb=pb, g0=g0, ng=ng):
                    last = None
                    for i in range(ng):
                        last = nc.tensor.transpose(pb[:, i * 128:(i + 1) * 128], us_[:, (g0 + i) * 128:(g0 + i + 1) * 128], self.ident[:])
                    return last
                k.op("pe", [us_.b, self.ident.b], [ps.b], tru)
                k.op("act", [ps.b], U.bufs[g0:g0 + ng], lambda pb=pb, g0=g0, ng=ng, j=j: nc.scalar.copy(
                    out=U[:, g0:g0 + ng, j * 128:(j + 1) * 128], in_=pb[:, 0:ng * 128].rearrange("p (a c) -> p a c", c=128)))

        def load_win(j):
            for s_ in range(3):
                c0 = s_ * D + j * 128
                k.dma("pool", wj[j % 2][s_][:], io["hy_w_in"][:, c0:c0 + 128].rearrange("(k p) n -> p k n", p=128),
                      [], [wj[j % 2][s_].b], f"win{j % 2}_{s_}")

        load_win(0)
        for j in range(8):
            ws = wj[j % 2]
            if j + 1 < 8:
                load_win(j + 1)
            for s_ in range(3):
                col = s_ * 8 + j
                for (zc, t0, n) in chunks:
                    ps = self.next_ps()

                    def mmz(ps=ps, w=ws[s_], t0=t0, n=n):
                        last = None
                        for kk in range(8):
                            last = nc.tensor.matmul(ps[:, 0:n], w[:, kk, :], hxT[:, kk, t0:t0 + n], start=(kk == 0), stop=(kk == 7))
                        return last
                    rb = hxT.bufs[2 * (t0 // 128):2 * (t0 // 128 + n // 128)]
                    k.op("pe", [ws[s_].b] + rb, [ps.b], mmz)
                    k.op("act", [ps.b, self.colv.b], [zpad[s_].b], lambda ps=ps, s_=s_, zc=zc, n=n, col=col: nc.scalar.activation(
                        out=zpad[s_][:, zc:zc + n], in_=ps[:, 0:n], func=AF.Identity,
                        bias=self.cv("hy_b_in", 1, col), scale=1.0))
            xs_ = x1s[j % 2]
            us_ = uTs[j % 2]
            for s_ in range(3):
                col = s_ * 8 + j
                eng = "dve"
                E = nc.vector
                for (zb, tb, n) in regions:
                    k.op(eng, [zpad[s_].b, self.colv.b], [acc[s_].b], lambda E=E, s_=s_, zb=zb, tb=tb, n=n, col=col: E.tensor_scalar(
                        out=acc[s_][:, tb:tb + n], in0=zpad[s_][:, zb:zb + n], scalar1=self.cv("hy_cw0", 1, col),
                        scalar2=self.cv("hy_cb", 1, col), op0=ALU.mult, op1=ALU.add))
                    k.op(eng, [zpad[s_].b, self.colv.b, acc[s_].b], [acc[s_].b], lambda E=E, s_=s_, zb=zb, tb=tb, n=n, col=col: E.scalar_tensor_tensor(
                        out=acc[s_][:, tb:tb + n], in0=zpad[s_][:, zb + 1:zb + 1 + n], scalar=self.cv("hy_cw1", 1, col),
                        in1=acc[s_][:, tb:tb + n], op0=ALU.mult, op1=ALU.add))
                    if s_ == 0:
                        k.op(eng, [zpad[s_].b, self.colv.b, acc[s_].b], [xs_.b], lambda E=E, s_=s_, zb=zb, tb=tb, n=n, col=col: E.scalar_tensor_tensor(
                            out=xs_[:, tb:tb + n], in0=zpad[s_][:, zb + 2:zb + 2 + n], scalar=self.cv("hy_cw2", 1, col),
                            in1=acc[s_][:, tb:tb + n], op0=ALU.mult, op1=ALU.add))
                    else:
                        k.op(eng, [zpad[s_].b, self.colv.b, acc[s_].b], [acc[s_].b], lambda E=E, s_=s_, zb=zb, tb=tb, n=n, col=col: E.scalar_tensor_tensor(
                            out=acc[s_][:, tb:tb + n], in0=zpad[s_][:, zb + 2:zb + 2 + n], scalar=self.cv("hy_cw2", 1, col),
                            in1=acc[s_][:, tb:tb + n], op0=ALU.mult, op1=ALU.add))
            k.op("dve", [acc[1].b, acc[2].b], [us_.b], lambda: nc.vector.tensor_tensor(
                out=us_[:], in0=acc[2][:], in1=acc[1][:], op=ALU.mult))
            k.dma("pool", self.X1T[b, j], xs_[:], [xs_.b], [self.X1T.bufs[b * 8 + j]], f"st_X1T{j % 2}")
            k.dma("pool", self.UT[b, j], us_[:], [us_.b], [self.UT.bufs[b * 8 + j]], f"st_UT{j % 2}")
            if j >= 1:
                u_transposes(j - 1)
        u_transposes(7)
        if "U" in self.taps and b == 0:
            k.dma("sp", self.tap("U", [128, NT * D], BF16), U[:].rearrange("p a c -> p (a c)"), U.bufs, [], "tap")
            k.dma("sp", self.tap("X1T", [8, 128, T], BF16), self.X1T[0], [self.X1T.b], [], "tap")
        k.release(mA)
        if self.stop_after == "HA":
            return
        mTop = k.mark_top()
        P = k.sbt("P", [128, 32, D], BF16, nsub=32)
        Pc = k.sbt("Pc", [128, 4, D], BF16, nsub=4)
        mU = k.mark()
        Fr = [k.sb(f"Fr{i}", [128, 16, 128], BF16) for i in range(2)]
        Fi = [k.sb(f"Fi{i}", [128, 16, 128], BF16) for i in range(2)]
        kre = [k.sb(f"kre{i}", [128, D], F32) for i in range(2)]
        kim = [k.sb(f"kim{i}", [128, D], F32) for i in range(2)]
        t1 = k.sb("pw1", [128, 512], F32)
        t2 = k.sb("pw2", [128, 512], F32)
        jobs = [(16, 16, io["fkt"], self.KF, P, 0, 16, "m"), (2, 2, io["fktc"], self.KFC, Pc, 16, 2, "c")]
        it = 0
        for (nfh, nk, fsrc, kfsrc, Pd, ut0, imoff, nm) in jobs:
            for i in range(nfh):
                fr, fi_, kr, ki = Fr[it % 2], Fi[it % 2], kre[it % 2], kim[it % 2]
                it += 1
                k.dma("sp", fr[:, 0:nk, :], fsrc[i, :, 0:nk, :], [], [fr.b], f"Fr{it % 2}")
                k.dma("sp", fi_[:, 0:nk, :], fsrc[imoff + i, :, 0:nk, :], [], [fi_.b], f"Fi{it % 2}")
                k.dma("sp", kr[:], kfsrc[i], [kfsrc.bufs[i]], [kr.b], f"kre{it % 2}")
                k.dma("sp", ki[:], kfsrc[imoff + i], [kfsrc.bufs[imoff + i]], [ki.b], f"kim{it % 2}")
                for ch in range(2):
                    sl = slice(ch * 512, (ch + 1) * 512)
                    psr = self.next_ps()
                    psi = self.next_ps()
                    ub = U.bufs[ut0:ut0 + nk]
                    for (ps_, f_) in ((psr, fr), (psi, fi_)):
                        def mmf(ps_=ps_, f_=f_, sl=sl, nk=nk, ut0=ut0):
                            last = None
                            for kk in range(nk):
                                last = nc.tensor.matmul(ps_[:, :], f_[:, kk, :], U[:, ut0 + kk, sl], start=(kk == 0), stop=(kk == nk - 1))
                            return last
                        k.op("pe", [f_.b] + ub, [ps_.b], mmf)
                    k.op("dve", [psr.b, kr.b], [t1.b], lambda psr=psr, kr=kr, sl=sl: nc.vector.tensor_tensor(
                        out=t1[:], in0=psr[:, :], in1=kr[:, sl], op=ALU.mult))
                    k.op("dve", [psi.b, ki.b], [t2.b], lambda psi=psi, ki=ki, sl=sl: nc.vector.tensor_tensor(
                        out=t2[:], in0=psi[:, :], in1=ki[:, sl], op=ALU.mult))
                    k.op("dve", [t1.b, t2.b], [Pd.bufs[i]], lambda Pd=Pd, i=i, sl=sl: nc.vector.tensor_tensor(
                        out=Pd[:, i, sl], in0=t1[:], in1=t2[:], op=ALU.subtract))
                    k.op("dve", [psr.b, ki.b], [t1.b], lambda psr=psr, ki=ki, sl=sl: nc.vector.tensor_tensor(
                        out=t1[:], in0=psr[:, :], in1=ki[:, sl], op=ALU.mult))
                    k.op("dve", [psi.b, kr.b], [t2.b], lambda psi=psi, kr=kr, sl=sl: nc.vector.tensor_tensor(
                        out=t2[:], in0=psi[:, :], in1=kr[:, sl], op=ALU.mult))
                    k.op("dve", [t1.b, t2.b], [Pd.bufs[imoff + i]], lambda Pd=Pd, i=i, imoff=imoff, sl=sl: nc.vector.tensor_tensor(
                        out=Pd[:, imoff + i, sl], in0=t1[:], in1=t2[:], op=ALU.add))
        k.release(mU)
        gT = k.sb_at("gatedT", [128, 8, T], BF16, U_off, alias_of=[U], nsub=NT)
        mI = k.mark()
        G = [k.sb(f"G{i}", [128, 32, 256], BF16) for i in range(2)]
        x1t = [k.sb(f"x1t{i}", [128, 8, 256], BF16) for i in range(2)]
        utt = [k.sb(f"utt{i}", [128, 8, 256], BF16) for i in range(2)]
        vt = [k.sb(f"vt{i}", [128, 256], F32) for i in range(2)]
        for tc in range(9):
            g, xa, ua = G[tc % 2], x1t[tc % 2], utt[tc % 2]
            if tc < 8:
                k.dma("sp", g[:], io["gt"][tc], [], [g.b], f"G{tc % 2}")
                nfc, Pd = 32, P
            else:
                k.dma("sp", g[:, 0:4, :], io["gtc"][0], [], [g.b], f"G{tc % 2}")
                nfc, Pd = 4, Pc
            tsl = slice(tc * 256, (tc + 1) * 256)
            k.dma("sp", xa[:], self.X1T[b, :, :, tsl].rearrange("j p t -> p j t"), self.X1T.bufs[b * 8:b * 8 + 8], [xa.b], f"x1t{tc % 2}")
            k.dma("sp", ua[:], self.UT[b, :, :, tsl].rearrange("j p t -> p j t"), self.UT.bufs[b * 8:b * 8 + 8], [ua.b], f"utt{tc % 2}")
            for j in range(8):
                ps = self.next_ps()

                def mmi(ps=ps, g=g, Pd=Pd, nfc=nfc, j=j):
                    last = None
                    for fc in range(nfc):
                        last = nc.tensor.matmul(ps[:, 0:256], Pd[:, fc, j * 128:(j + 1) * 128], g[:, fc, :], start=(fc == 0), stop=(fc == nfc - 1))
                    return last
                k.op("pe", [g.b] + Pd.bufs, [ps.b], mmi)
                v_ = vt[j % 2]
                k.op("dve", [ps.b, ua.b, self.colv.b], [v_.b], lambda ps=ps, ua=ua, v_=v_, j=j: nc.vector.scalar_tensor_tensor(
                    out=v_[:], in0=ua[:, j, :], scalar=self.cv("hy_fbias", 1, j), in1=ps[:, 0:256], op0=ALU.mult, op1=ALU.add))
                k.op("dve", [v_.b, xa.b], [gT.bufs[2 * tc], gT.bufs[2 * tc + 1]], lambda v_=v_, xa=xa, j=j, tsl=tsl: nc.vector.tensor_tensor(
                    out=gT[:, j, tsl], in0=v_[:], in1=xa[:, j, :], op=ALU.mult))
        if "gT" in self.taps and b == 0:
            k.dma("sp", self.tap("gT", [128, 8 * T], BF16), gT[:].rearrange("p a c -> p (a c)"), gT.bufs, [], "tap")
        k.release(mI)
        k.release_top(mTop)
        wo = k.sb("hy_wo", [128, 8, D], BF16)
        k.dma("pool", wo[:], io["hy_w_out"].rearrange("(k p) n -> p k n", p=128), [], [wo.b], "hy_wo")
        bo = k.sb("hy_bo", [1, D], BF16)
        k.dma("pool", bo[:], io["rows"][ROW_HY_B_OUT:ROW_HY_B_OUT + 1, :], [], [bo.b], "hy_bo")
        GGx = k.sb("GGx", [128, D], F32)
        GGc = k.sb("GGc", [128, D], F32)
        k.dma("sp", GGx[:], self.GGD[0, 0, b], [self.GGD.bufs[b]], [GGx.b], "GGx")
        k.dma("sp", GGc[:], self.GGD[0, 0, 2], [self.GGD.bufs[2]], [GGc.b], "GGc")
        self.alloc_post_tmp()
        xts = [k.sb(f"xo{i}", [128, D], F32) for i in range(3)]
        for tt in range(NT):
            xt = xts[tt % 3]
            k.dma("sp", xt[:], self.tok_src(b, tt, True), [], [xt.b], f"xo{tt % 3}")
            ps2 = [self.next_ps(), self.next_ps()]
            for h in range(2):
                def mmo(h=h, ps=ps2[h], tt=tt):
                    for j in range(8):
                        nc.tensor.matmul(ps[:, :], gT[:, j, tt * 128:(tt + 1) * 128], wo[:, j, h * 512:(h + 1) * 512], start=(j == 0), stop=False)
                    return nc.tensor.matmul(ps[:, :], self.ones[0:1, :], bo[:, h * 512:(h + 1) * 512], start=False, stop=True)
                k.op("pe", [gT.bufs[tt], wo.b, bo.b, self.ones.b], [ps2[h].b], mmo)
            self.post_residual(ps2, xt, GGx if tt < NTX else GGc)
            k.dma("pool", self.XS[b, tt * 128:(tt + 1) * 128, :], xt[:], [xt.b], [self.XS.bufs[b * NT + tt]], f"st_xo{tt % 3}")
        k.release(mB)

    def phase_mlp(self, l, final):
        k, nc, io = self.k, self.k.nc, self.io
        m0 = k.mark()
        mTop = k.mark_top()
        w1c, w2c = [], []
        for c in range(8):
            a = k.sbt(f"w1c{c}", [128, 8, 512], BF16)
            k.dma("pool", a[:], io["mlp_w1"][l, :, c * 512:(c + 1) * 512].rearrange("(k p) n -> p k n", p=128),
                  [], [a.b], f"w1c{c}")
            w1c.append(a)
            a2 = k.sbt(f"w2c{c}", [128, 4, D], BF16)
            k.dma("pool", a2[:], io["mlp_w2"][l, c * 512:(c + 1) * 512, :].rearrange("(f p) n -> p f n", p=128),
                  [], [a2.b], f"w2c{c}")
            w2c.append(a2)
        GG = []
        for r in range(3):
            g = k.sb(f"GGm{r}", [128, D], F32)
            k.dma("sp", g[:], self.GGD[l, 1, r], [self.GGD.bufs[(l * 2 + 1) * 3 + r]], [g.b], f"GGm{r}")
            GG.append(g)
        self.alloc_norm_tmp(2)
        self.alloc_post_tmp()
        self.pjunk = k.sb("pjunk", [128, D], BF16)
        ntt = NTX if final else NT
        tiles = [(b, tt) for b in range(NB) for tt in range(ntt)]
        blocks = [tiles[i:i + 2] for i in range(0, len(tiles), 2)]
        xts = [k.sb(f"xm{i}", [128, D], F32) for i in range(4)]
        hxT = [k.sb(f"hxTm{i}", [128, 8, 256], BF16, nsub=4) for i in range(2)]
        hT = [k.sb(f"hTm{i}", [128, 256], BF16) for i in range(4)]
        rl = [k.sb(f"rlm{i}", [128, 256], F32) for i in range(2)]
        ysb = [k.sb(f"ysb{i}", [128, D], F32) for i in range(4)]
        out_banks = self.ps[0:4]
        self.ps_pool = self.ps[4:8]
        xi = 0
        blk_x = {}

        def load_block(n):
            nonlocal xi
            xs = []
            for (b, tt) in blocks[n]:
                xt = xts[xi % 4]
                k.dma("sp", xt[:], self.tok_src(b, tt, False), [self.XS.bufs[b * NT + tt]], [xt.b], f"xm{xi % 4}")
                xi += 1
                xs.append(xt)
            blk_x[n] = xs

        blk_tm = {}

        def stats_block(n):
            blk_tm[n] = [self.norm_stats(blk_x[n][i]) for i in range(len(blocks[n]))]

        def tr_block(n):
            hx = hxT[n % 2]
            for i, (b, tt) in enumerate(blocks[n]):
                r = b if tt < NTX else 2
                self.norm_transpose(blk_tm[n][i], self.AS[:, l, 1, 0, r, :], self.AS[:, l, 1, 1, r, :],
                                    lambda kk, hx=hx, i=i: hx[:, kk, i * 128:(i + 1) * 128], hx.bufs[2 * i:2 * i + 2])

        def norm_block(n):
            stats_block(n)
            tr_block(n)

        def post_tile(n, i):
            b, tt = blocks[n][i]
            y = ysb[(2 * n + i) % 4]
            xt = blk_x[n][i]
            r = b if tt < NTX else 2
            self.post_residual_sb(y, xt, GG[r])
            if final:
                k.dma("pool", self.out[b, tt * 128:(tt + 1) * 128, :], xt[:], [xt.b], [], f"st_xm{(2 * n + i) % 4}")
            else:
                k.dma("pool", self.XS[b, tt * 128:(tt + 1) * 128, :], xt[:], [xt.b], [self.XS.bufs[b * NT + tt]], f"st_xm{(2 * n + i) % 4}")

        load_block(0)
        load_block(1)
        norm_block(0)
        SK = 3
        f_ctr = 0
        for n in range(len(blocks)):
            hx = hxT[n % 2]
            for f in range(32 + SK):
                if f < 32:
                    ps1 = self.next_ps()
                    h_ = hT[f_ctr % 4]
                    r_ = rl[f_ctr % 2]
                    f_ctr += 1

                    def mm1(ps1=ps1, f=f, hx=hx):
                        last = None
                        for kk in range(8):
                            last = nc.tensor.matmul(ps1[:, 0:256], w1c[f // 4][:, kk, (f % 4) * 128:(f % 4 + 1) * 128], hx[:, kk, :], start=(kk == 0), stop=(kk == 7))
                        return last
                    k.op("pe", [w1c[f // 4].b] + hx.bufs, [ps1.b], mm1)
                    k.op("act", [ps1.b], [r_.b], lambda ps1=ps1, r_=r_: nc.scalar.activation(out=r_[:], in_=ps1[:, 0:256], func=AF.Relu))
                    k.op("dve", [r_.b], [h_.b], lambda r_=r_, h_=h_: nc.vector.tensor_tensor(out=h_[:], in0=r_[:], in1=r_[:], op=ALU.mult))
                if f >= SK:
                    f2 = f - SK
                    h2 = hT[(f_ctr - (min(f, 31) - f2) - 1) % 4] if False else None
                    h2 = hT[(n * 32 + f2) % 4]

                    def mm2(h2=h2, f2=f2):
                        last = None
                        for i in range(2):
                            for hf in range(2):
                                last = nc.tensor.matmul(out_banks[2 * i + hf][:, :], h2[:, i * 128:(i + 1) * 128],
                                                        w2c[f2 // 4][:, f2 % 4, hf * 512:(hf + 1) * 512], start=(f2 == 0), stop=(f2 == 31))
                        return last
                    k.op("pe", [h2.b, w2c[f2 // 4].b], [ob.b for ob in out_banks], mm2)
                if n >= 1 and f in (3, 7):
                    post_tile(n - 1, 0 if f == 3 else 1)
                if f == 8 and n + 1 < len(blocks) and n >= 1:
                    load_block(n + 1)
                if f == 12 and n + 1 < len(blocks):
                    stats_block(n + 1)
                if f == 26 and n + 1 < len(blocks):
                    tr_block(n + 1)
            for i, (b, tt) in enumerate(blocks[n]):
                y = ysb[(2 * n + i) % 4]
                for hf in range(2):
                    k.op("act", [out_banks[2 * i + hf].b], [y.b], lambda y=y, i=i, hf=hf: nc.scalar.copy(
                        out=y[:, hf * 512:(hf + 1) * 512], in_=out_banks[2 * i + hf][:, :]))
        post_tile(len(blocks) - 1, 0)
        post_tile(len(blocks) - 1, 1)
        self.ps_pool = self.ps
        k.release(m0)
        k.release_top(mTop)

    def post_residual_sb(self, y, xt, GG):
        k, nc = self.k, self.k.nc
        tm = self.ptmp[self.ptmp_i % 2]
        self.ptmp_i += 1
        ss, rstd = tm["ss"], tm["rstd"]
        junk = self.pjunk
        k.op("act", [y.b], [junk.b, ss.b], lambda: nc.scalar.activation(
            out=junk[:], in_=y[:], func=AF.Square, accum_out=ss[:, 0:1]))
        k.op("act", [ss.b, self.epsc.b], [rstd.b], lambda: nc.scalar.activation(
            out=rstd[:], in_=ss[:, 0:1], func=AF.Sqrt, scale=1.0 / D, bias=self.epsc[:, 0:1]))
        k.op("dve", [rstd.b], [rstd.b], lambda: nc.vector.reciprocal(out=rstd[:], in_=rstd[:]))
        k.op("dve", [y.b, rstd.b, GG.b], [y.b], lambda: nc.vector.scalar_tensor_tensor(
            out=y[:], in0=y[:], scalar=rstd[:, 0:1], in1=GG[:], op0=ALU.mult, op1=ALU.mult))
        k.op("dve", [y.b, xt.b], [xt.b], lambda: nc.vector.tensor_tensor(out=xt[:], in0=xt[:], in1=y[:], op=ALU.add))

    def phase_attn(self, b):
        k, nc, io = self.k, self.k.nc, self.io
        l = 1
        m0 = k.mark()
        QT = k.sb("QT", [128, NH, L], BF16, nsub=NTX)
        KT = k.sb("KT", [128, NKV, T], BF16, nsub=NT)
        V = k.sb("V", [128, NT, NKV, HD], BF16, nsub=NT)
        hxT = k.sb("hxTa", [128, 8, L], BF16, nsub=2 * NTX)
        wq = k.sb("wq", [128, 8, D], BF16)
        k.dma("pool", wq[:], io["attn_w_qkv"][:, 0:D].rearrange("(k p) n -> p k n", p=128), [], [wq.b], "wq")
        ropec = k.sb("ropec", [128, NTX, 64], F32)
        ropes = k.sb("ropes", [128, NTX, 64], F32)
        k.dma("sp", ropec[:], io["ropec"], [], [ropec.b], "c_grp2")
        k.dma("sp", ropes[:], io["ropes"], [], [ropes.b], "c_grp2")
        qg = k.sb("qg", [128, HD], F32)
        kg = k.sb("kg", [128, HD], F32)
        k.dma("sp", qg[:], io["rows"][ROW_QN:ROW_QN + 1, 0:HD].partition_broadcast(128), [], [qg.b], "c_grp2")
        k.dma("sp", kg[:], io["rows"][ROW_KN:ROW_KN + 1, 0:HD].partition_broadcast(128), [], [kg.b], "c_grp2")
        k.group_done("c_grp2", [ropec, ropes, qg, kg])
        sq_ = k.sb("sq", [128, 10, HD], F32)
        qn_ = k.sb("qn", [128, 10, HD], F32)
        qr = [k.sb(f"qr{i}", [128, 10, HD], BF16) for i in range(2)]
        ss_ = k.sb("ss10", [128, 10], F32)
        ra_ = k.sb("ra", [128, 10, 64], F32)
        rbb = k.sb("rb", [128, 10, 64], F32)
        xts = [k.sb(f"xa{i}", [128, D], F32) for i in range(2)]
        PT = [k.sb(f"PT{i}", [128, 512], BF16) for i in range(4)]
        rec = k.sb("rec", [128, 512], F32)
        rec2 = k.sb("rec2", [128, 512], F32)
        mTop = k.mark_top()
        hxTc = k.sbt("hxTc", [128, 8, C], BF16, nsub=4)
        wkv = k.sbt("wkv", [128, 8, 512], BF16)
        k.dma("pool", wkv[:], io["attn_w_qkv"][:, D:D + 512].rearrange("(k p) n -> p k n", p=128), [], [wkv.b], "wkv")
        k_ntmp = []
        for i in range(3):
            k_ntmp.append(dict(junk=k.sbt(f"njunk{i}", [128, D], BF16), ss=k.sbt(f"nss{i}", [128, 2], F32),
                               rstd=k.sbt(f"nrstd{i}", [128, 1], F32), xh=k.sbt(f"nxh{i}", [128, D], BF16)))
        self.ntmp = k_ntmp
        self.ntmp_i = 0

        def hx_ap(tt, kk):
            if tt < NTX:
                return hxT[:, kk, tt * 128:(tt + 1) * 128]
            return hxTc[:, kk, (tt - NTX) * 128:(tt - NTX + 1) * 128]

        def hx_bufs(tt):
            if tt < NTX:
                return hxT.bufs[2 * tt:2 * tt + 2]
            return hxTc.bufs[2 * (tt - NTX):2 * (tt - NTX) + 2]

        def rope(h0, h1, tt, dst):
            nh_ = h1 - h0
            x0 = qn_[:, h0:h1, :].rearrange("p h (i two) -> p h i two", two=2)[:, :, :, 0]
            x1 = qn_[:, h0:h1, :].rearrange("p h (i two) -> p h i two", two=2)[:, :, :, 1]
            o0 = dst[:, h0:h1, :].rearrange("p h (i two) -> p h i two", two=2)[:, :, :, 0]
            o1 = dst[:, h0:h1, :].rearrange("p h (i two) -> p h i two", two=2)[:, :, :, 1]
            cosb = ropec[:, tt, :].unsqueeze(1).broadcast_to([128, nh_, 64])
            sinb = ropes[:, tt, :].unsqueeze(1).broadcast_to([128, nh_, 64])
            a_, b_ = ra_[:, h0:h1, :], rbb[:, h0:h1, :]
            k.op("dve", [qn_.b, ropec.b], [ra_.b], lambda: nc.vector.tensor_tensor(out=a_, in0=x0, in1=cosb, op=ALU.mult))
            k.op("dve", [qn_.b, ropes.b], [rbb.b], lambda: nc.vector.tensor_tensor(out=b_, in0=x1, in1=sinb, op=ALU.mult))
            k.op("dve", [ra_.b, rbb.b], [dst.b], lambda: nc.vector.tensor_tensor(out=o0, in0=a_, in1=b_, op=ALU.subtract))
            k.op("dve", [qn_.b, ropes.b], [ra_.b], lambda: nc.vector.tensor_tensor(out=a_, in0=x0, in1=sinb, op=ALU.mult))
            k.op("dve", [qn_.b, ropec.b], [rbb.b], lambda: nc.vector.tensor_tensor(out=b_, in0=x1, in1=cosb, op=ALU.mult))
            k.op("dve", [ra_.b, rbb.b], [dst.b], lambda: nc.vector.tensor_tensor(out=o1, in0=a_, in1=b_, op=ALU.add))

        def rstd_heads(h0, h1):
            k.op("dve", [sq_.b], [ss_.b], lambda: nc.vector.tensor_reduce(
                out=ss_[:, h0:h1], in_=sq_[:, h0:h1, :], axis=AX.X, op=ALU.add))
            k.op("act", [ss_.b, self.epsc.b], [ss_.b], lambda: nc.scalar.activation(
                out=ss_[:, h0:h1], in_=ss_[:, h0:h1], func=AF.Ln, scale=1.0 / HD, bias=self.epsc[:, 0:1]))
            k.op("act", [ss_.b], [ss_.b], lambda: nc.scalar.activation(
                out=ss_[:, h0:h1], in_=ss_[:, h0:h1], func=AF.Exp, scale=-0.5))

        def stA(tt):
            xt = xts[tt % 2]
            k.dma("sp", xt[:], self.tok_src(b, tt, False), [self.XS.bufs[b * NT + tt]], [xt.b], f"xa{tt % 2}")
            return self.norm_stats(xt)

        def stB(tt, tm):
            r = b if tt < NTX else 2
            self.norm_transpose(tm, self.AS[:, l, 0, 0, r, :], self.AS[:, l, 0, 1, r, :],
                                lambda kk, tt=tt: hx_ap(tt, kk), hx_bufs(tt))

        def stK(tt):
            isx = tt < NTX
            pskv = self.next_ps()

            def mmkv(ps=pskv, tt=tt):
                last = None
                for kk in range(8):
                    last = nc.tensor.matmul(ps[:, :], hx_ap(tt, kk), wkv[:, kk, :], start=(kk == 0), stop=(kk == 7))
                return last
            k.op("pe", hx_bufs(tt) + [wkv.b], [pskv.b], mmkv)
            k.op("act", [pskv.b], [V.bufs[tt]], lambda: nc.scalar.copy(
                out=V[:, tt, :, :], in_=pskv[:, 256:512].rearrange("p (g d) -> p g d", d=HD)))
            k.op("act", [pskv.b], [sq_.b], lambda: nc.scalar.activation(
                out=sq_[:, 8:10, :], in_=pskv[:, 0:256].rearrange("p (h d) -> p h d", d=HD), func=AF.Square))
            rstd_heads(8, 10)
            dst = qr[tt % 2]
            for hh in range(2):
                o_ = qn_[:, 8 + hh, :] if isx else dst[:, 8 + hh, :]
                ob_ = qn_.b if isx else dst.b
                k.op("dve", [pskv.b, ss_.b, kg.b], [ob_], lambda hh=hh, o_=o_: nc.vector.scalar_tensor_tensor(
                    out=o_, in0=pskv[:, hh * HD:(hh + 1) * HD], scalar=ss_[:, 8 + hh:9 + hh],
                    in1=kg[:, :], op0=ALU.mult, op1=ALU.mult))
            if isx:
                rope(8, 10, tt, dst)

        def stKD(tt):
            src = qr[tt % 2]
            ps = self.next_ps()
            pb = self.psb(ps)
            k.op("pe", [src.b, self.ident.b], [ps.b], lambda: [nc.tensor.transpose(
                pb[:, g * 128:(g + 1) * 128], src[:, 8 + g, :], self.ident[:]) for g in range(2)][-1])
            k.op("act", [ps.b], [KT.bufs[tt]], lambda: nc.scalar.copy(
                out=KT[:, :, tt * 128:(tt + 1) * 128], in_=pb[:, 0:256].rearrange("p (g t) -> p g t", t=128)))

        tms = {}
        for i in range(NT + 3):
            if i < NT:
                tms[i] = stA(i)
            if 0 <= i - 1 < NT:
                stB(i - 1, tms.pop(i - 1))
            if 0 <= i - 2 < NT:
                stK(i - 2)
            if 0 <= i - 3 < NT:
                stKD(i - 3)
        k.release_top(mTop)
        if "QT" in self.taps and b == 0:
            k.dma("sp", self.tap("KT", [128, NKV * T], BF16), KT[:].rearrange("p a c -> p (a c)"), KT.bufs, [], "tap")
            k.dma("sp", self.tap("V", [128, NT * NKV * HD], BF16), V[:].rearrange("p a c d -> p (a c d)"), V.bufs, [], "tap")

        OT = k.sb("OT", [128, NH, L], BF16, nsub=NTX)
        qbank = self.ps[7]

        def stC(tt, c):
            def mmq():
                last = None
                for kk in range(8):
                    last = nc.tensor.matmul(qbank[:, :], hxT[:, kk, tt * 128:(tt + 1) * 128], wq[:, kk, c * 512:(c + 1) * 512],
                                            start=(kk == 0), stop=(kk == 7))
                return last
            k.op("pe", hxT.bufs[2 * tt:2 * tt + 2] + [wq.b], [qbank.b], mmq)
            k.op("act", [qbank.b], [sq_.b], lambda: nc.scalar.activation(
                out=sq_[:, 4 * c:4 * c + 4, :], in_=qbank[:, :].rearrange("p (h d) -> p h d", d=HD), func=AF.Square))
            rstd_heads(4 * c, 4 * c + 4)
            for hh in range(4 * c, 4 * c + 4):
                k.op("dve", [qbank.b, ss_.b, qg.b], [qn_.b], lambda hh=hh: nc.vector.scalar_tensor_tensor(
                    out=qn_[:, hh, :], in0=qbank[:, (hh % 4) * HD:(hh % 4 + 1) * HD], scalar=ss_[:, hh:hh + 1],
                    in1=qg[:, :], op0=ALU.mult, op1=ALU.mult))
            if c == 1:
                rope(0, 8, tt, qr[tt % 2])

        def stD(tt):
            src = qr[tt % 2]
            pb = self.psb(qbank)
            k.op("pe", [src.b, self.ident.b], [qbank.b], lambda: [nc.tensor.transpose(
                pb[:, hh * 128:(hh + 1) * 128], src[:, hh, :], self.ident[:]) for hh in range(8)][-1])
            k.op("dve", [qbank.b], [QT.bufs[tt]], lambda: nc.vector.tensor_copy(
                out=QT[:, :, tt * 128:(tt + 1) * 128], in_=pb[:, 0:1024].rearrange("p (h t) -> p h t", t=128)))

        def q_pieces(t0):
            t = [t0, t0 + 1, t0 + 2, t0 + 3]
            return [("C", t[0], 0), ("C", t[0], 1), ("C", t[1], 0), ("D", t[0], 0), ("C", t[1], 1), ("C", t[2], 0),
                    ("D", t[1], 0), ("C", t[2], 1), ("C", t[3], 0), ("D", t[2], 0), ("C", t[3], 1), ("D", t[3], 0)]

        def emit_piece(pc):
            kind, tt_, c_ = pc
            if kind == "C":
                stC(tt_, c_)
            else:
                stD(tt_)

        for pc in q_pieces(0):
            emit_piece(pc)
        s_banks = self.ps[0:3]
        o_banks = self.ps[3:5]
        d_banks = self.ps[5:7]
        recs = [rec, rec2]
        SCALE = float(HD) ** -0.5
        SK = 2
        sc = 0
        it = 0
        for qc in range(L // 512):
            qsl = slice(qc * 512, (qc + 1) * 512)
            qbufs = QT.bufs[qc * 4:qc * 4 + 4]
            pieces = q_pieces(4 * (qc + 1)) if qc + 1 < L // 512 else []
            slot = 0
            for h in range(NH):
                g = h // (NH // NKV)
                base = sc
                pso = o_banks[it % 2]
                psd = d_banks[it % 2]
                rc = recs[it % 2]
                it += 1
                for kc in range(NT + SK):
                    if kc < NT:
                        pss = s_banks[sc % 3]
                        pt = PT[sc % 4]
                        sc += 1
                        k.op("pe", [KT.bufs[kc]] + qbufs, [pss.b], lambda pss=pss, kc=kc, g=g, h=h: nc.tensor.matmul(
                            pss[:, :], KT[:, g, kc * 128:(kc + 1) * 128], QT[:, h, qsl], start=True, stop=True))
                        k.op("act", [pss.b], [pt.b], lambda pss=pss, pt=pt: nc.scalar.activation(
                            out=pt[:], in_=pss[:, :], func=AF.Exp, scale=SCALE))
                    if kc >= SK:
                        k2_ = kc - SK
                        pt2 = PT[(base + k2_) % 4]

                        def mmpv(pt2=pt2, k2_=k2_, g=g, pso=pso, psd=psd):
                            nc.tensor.matmul(pso[:, :], V[:, k2_, g, :], pt2[:], start=(k2_ == 0), stop=(k2_ == NT - 1))
                            return nc.tensor.matmul(psd[:, :], self.ones[:, :], pt2[:], start=(k2_ == 0), stop=(k2_ == NT - 1))
                        k.op("pe", [pt2.b, V.bufs[k2_], self.ones.b], [pso.b, psd.b], mmpv)
                    if kc in (6, 14) and slot < len(pieces) and (slot % 2 == (0 if kc == 6 else 1) or True):
                        emit_piece(pieces[slot])
                        slot += 1
                k.op("dve", [psd.b], [rc.b], lambda psd=psd, rc=rc: nc.vector.reciprocal(out=rc[:], in_=psd[:, :]))
                k.op("dve", [pso.b, rc.b], OT.bufs[qc * 4:qc * 4 + 4], lambda h=h, pso=pso, rc=rc: nc.vector.tensor_tensor(
                    out=OT[:, h, qsl], in0=pso[:, :], in1=rc[:], op=ALU.mult))
            while slot < len(pieces):
                emit_piece(pieces[slot])
                slot += 1
        if "OT" in self.taps and b == 0:
            k.dma("sp", self.tap("OT", [128, NH * L], BF16), OT[:].rearrange("p a c -> p (a c)"), OT.bufs, [], "tap")
        wo = k.sb("at_wo", [128, NH, D], BF16)
        k.dma("pool", wo[:], io["attn_w_o"].rearrange("(h p) n -> p h n", p=128), [], [wo.b], "at_wo")
        GGx = k.sb("GGa", [128, D], F32)
        k.dma("sp", GGx[:], self.GGD[l, 0, b], [self.GGD.bufs[(l * 2) * 3 + b]], [GGx.b], "GGa")
        self.alloc_post_tmp()
        for tt in range(NTX):
            xt = xts[tt % 2]
            k.dma("sp", xt[:], self.tok_src(b, tt, False), [self.XS.bufs[b * NT + tt]], [xt.b], f"xa{tt % 2}")
            ps2 = [self.next_ps(), self.next_ps()]
            for hf in range(2):
                def mmo(hf=hf, ps=ps2[hf], tt=tt):
                    last = None
                    for h in range(NH):
                        last = nc.tensor.matmul(ps[:, :], OT[:, h, tt * 128:(tt + 1) * 128], wo[:, h, hf * 512:(hf + 1) * 512],
                                                start=(h == 0), stop=(h == NH - 1))
                    return last
                k.op("pe", [OT.bufs[tt], wo.b], [ps2[hf].b], mmo)
            self.post_residual(ps2, xt, GGx)
            k.dma("pool", self.XS[b, tt * 128:(tt + 1) * 128, :], xt[:], [xt.b], [self.XS.bufs[b * NT + tt]], f"st_xa{tt % 2}")
        k.release(m0)

    def finish(self):
        self.k.final_wait()
        return self.k.nc


def make_in_maps(inp, ncores=8):
    cc = get_consts()
    f32 = np.float32
    shared = {
        "mod_w": np.ascontiguousarray(inp["mod_w"], f32),
        "colv": build_colv(inp),
        "rows": build_rows(inp, cc),
        "mlp_w1": np.ascontiguousarray(inp["mlp_w1"], f32),
        "mlp_w2": np.ascontiguousarray(inp["mlp_w2"], f32),
        "hy_w_in": np.ascontiguousarray(inp["hy_w_in"][0], f32),
        "hy_w_out": np.ascontiguousarray(inp["hy_w_out"][0], f32),
        "attn_w_qkv": np.ascontiguousarray(inp["attn_w_qkv"][0], f32),
        "attn_w_o": np.ascontiguousarray(inp["attn_w_o"][0], f32),
        "f_w1": np.ascontiguousarray(inp["hy_filt_w1"][0], f32),
        "f_w2": np.ascontiguousarray(inp["hy_filt_w2"][0], f32),
        "f_w3": np.ascontiguousarray(inp["hy_filt_w3"][0], f32),
        "fkt": cc["fkt"], "gt": cc["gt"], "fktc": cc["fktc"], "gtc": cc["gtc"],
        "zT": cc["zT"], "tcol": cc["tcol"], "ident": cc["ident"], "sel3": cc["sel3"],
        "ropec": cc["ropec"], "ropes": cc["ropes"],
    }
    maps = []
    for i in range(ncores):
        b0 = i * NB
        cT = np.stack([inp["c"][b0], inp["c"][b0 + 1], inp["c_ctx"]], axis=-1)
        cT = np.ascontiguousarray(cT.reshape(8, 128, 3).transpose(1, 0, 2), f32)
        m = dict(shared)
        m["x"] = np.ascontiguousarray(inp["x"][b0:b0 + NB], f32)
        m["ctx"] = np.ascontiguousarray(inp["ctx"][b0:b0 + NB], f32)
        m["cT"] = cT
        maps.append(m)
    return maps


_PROG_CACHE = {}


def build_program():
    p = Prog()
    p.load_consts()
    p.phase_filter(p.phase_mod_gen())
    for b in range(NB):
        p.phase_hyena(b)
    p.phase_mlp(0, False)
    for b in range(NB):
        p.phase_attn(b)
    p.phase_mlp(1, True)
    return p.finish()


def kernel(**inputs):
    inp = {k_: np.asarray(v) for k_, v in inputs.items()}
    n_cores = 8
    maps = make_in_maps(inp, n_cores)
    nc = build_program()
    res = run_bass_kernel_spmd(nc, maps, core_ids=list(range(n_cores)))
    outs = [np.asarray(res.results[i]["out"], dtype=np.float32) for i in range(n_cores)]
    return np.concatenate(outs, axis=0)
```

```python
import math
import numpy as np
import ml_dtypes
import concourse.bass as bass
import concourse.mybir as mybir
from concourse.bass_utils import run_bass_kernel_spmd

F32 = mybir.dt.float32
BF16 = mybir.dt.bfloat16
AF = mybir.ActivationFunctionType
ALU = mybir.AluOpType
AX = mybir.AxisListType

D = 1024
L = 2048
C = 256
T = L + C
NB = 2
DFF = 4096
EPS = 1e-6
NT = T // 128
NTX = L // 128
HD = 128
NH = 8
NKV = 2
N_DFT = 2 * L
N_DFTC = 2 * C


class Buf:
    __slots__ = ("name", "w", "r")

    def __init__(self, name):
        self.name = name
        self.w = {}
        self.r = []


class Tile:
    def __init__(self, h, name, nsub=1):
        self.h = h
        self.name = name
        self.bufs = [Buf(f"{name}.{i}") for i in range(nsub)]

    def __getitem__(self, idx):
        return self.h[idx]

    @property
    def b(self):
        return self.bufs[0]


class KB:
    def __init__(self):
        nc = self.nc = bass.Bass("TRN2", target_bir_lowering=False)
        self.E = {"pe": nc.tensor, "act": nc.scalar, "dve": nc.vector, "pool": nc.gpsimd, "sp": nc.sync}
        self.sem = {e: nc.alloc_semaphore("s_" + e) for e in ("pe", "act", "dve", "pool")}
        self.tick = {e: 0 for e in self.sem}
        self.seen = {e: {} for e in self.E}
        self.dsem = {}
        self.all_tokens = {}
        self.sb_off = (nc.sbuf_base + 31) // 32 * 32
        self.sb_top = nc.sbuf_top // 32 * 32
        self.top_off = self.sb_top
        self.rel = []
        self.sb_peak = 0
        self.n_names = 0
        self.dram = {}

    def _inherit(self, lo, hi, tile):
        toks = {}
        for (a, b_, tl) in self.rel:
            if a < hi and lo < b_:
                for tok in tl:
                    if tok[0] not in toks or toks[tok[0]][1] < tok[1]:
                        toks[tok[0]] = tok
        if toks:
            for buf in tile.bufs:
                buf.r = list(toks.values())

    @staticmethod
    def _per(shape, dtype):
        per = 4 if dtype == F32 else 2
        for s_ in shape[1:]:
            per *= s_
        return (per + 31) // 32 * 32

    def sb(self, name, shape, dtype, nsub=1):
        per = self._per(shape, dtype)
        off = self.sb_off
        assert off + per <= self.top_off, f"SBUF overflow allocating {name}: {off}+{per} > {self.top_off}"
        self.n_names += 1
        h = self.nc.alloc_sbuf_tensor_at(f"{name}_{self.n_names}", list(shape), dtype, offset=off)
        self.sb_off = off + per
        self.sb_peak = max(self.sb_peak, self.sb_off + (self.sb_top - self.top_off))
        t = Tile(h, name, nsub)
        self._inherit(off, off + per, t)
        return t

    def sbt(self, name, shape, dtype, nsub=1):
        per = self._per(shape, dtype)
        off = self.top_off - per
        assert off >= self.sb_off, f"SBUF overflow (top) allocating {name}: {off} < {self.sb_off}"
        self.n_names += 1
        h = self.nc.alloc_sbuf_tensor_at(f"{name}_{self.n_names}", list(shape), dtype, offset=off)
        self.top_off = off
        self.sb_peak = max(self.sb_peak, self.sb_off + (self.sb_top - self.top_off))
        t = Tile(h, name, nsub)
        self._inherit(off, off + per, t)
        return t

    def sb_at(self, name, shape, dtype, off, alias_of=(), nsub=1):
        self.n_names += 1
        h = self.nc.alloc_sbuf_tensor_at(f"{name}_{self.n_names}", list(shape), dtype, offset=off)
        t = Tile(h, name, nsub)
        toks = {}
        for a in alias_of:
            for buf in a.bufs:
                for tok in list(buf.w.values()) + list(buf.r):
                    if tok[0] not in toks or toks[tok[0]][1] < tok[1]:
                        toks[tok[0]] = tok
        for buf in t.bufs:
            buf.r = list(toks.values())
        return t

    def mark(self):
        return self.sb_off

    def release(self, mark):
        if self.sb_off > mark:
            self.rel.append((mark, self.sb_off, list(self.all_tokens.values())))
        self.sb_off = mark

    def mark_top(self):
        return self.top_off

    def release_top(self, mark):
        if mark > self.top_off:
            self.rel.append((self.top_off, mark, list(self.all_tokens.values())))
        self.top_off = mark

    def dram_t(self, name, shape, dtype, kind="Internal", nsub=1):
        h = self.nc.dram_tensor(name, list(shape), dtype, kind=kind)
        t = Tile(h.ap(), name, nsub)
        self.dram[name] = t
        return t

    def _collect(self, eng, reads, writes, is_dma):
        need = {}

        def add(tok, raw):
            if tok is None:
                return
            s, v, src = tok
            if (not is_dma) and src == eng and not raw and eng == "pe":
                return
            if need.get(s, 0) < v:
                need[s] = v

        for b in reads:
            for t in b.w.values():
                add(t, True)
        for b in writes:
            for t in b.w.values():
                add(t, False)
            for t in b.r:
                add(t, False)
        seen = self.seen[eng]
        out = []
        for s, v in need.items():
            if seen.get(s, 0) >= v:
                continue
            seen[s] = v
            out.append((s, v))
        return out

    def _emit_waits(self, eng, waits):
        e = self.E[eng]
        for s, v in waits:
            e.wait_ge(s, v)

    def _finish(self, tok, reads, writes):
        for b in writes:
            b.w[tok[0]] = tok
            b.r = []
        for b in reads:
            b.r.append(tok)
            if len(b.r) > 64:
                mx = {}
                for (s, v, src) in b.r:
                    if s not in mx or mx[s][1] < v:
                        mx[s] = (s, v, src)
                b.r = list(mx.values())
        self.all_tokens[tok[0]] = tok

    def op(self, eng, reads, writes, fn):
        waits = self._collect(eng, reads, writes, False)
        self._emit_waits(eng, waits)
        last = fn()
        self.tick[eng] += 1
        last.then_inc(self.sem[eng], 1)
        tok = (self.sem[eng], self.tick[eng], eng)
        self._finish(tok, reads, writes)
        return tok

    def dma(self, q, out_ap, in_ap, reads, writes, key):
        waits = self._collect(q, reads, writes, True)
        self._emit_waits(q, waits)
        if key not in self.dsem:
            self.dsem[key] = [self.nc.alloc_semaphore("d_" + key), 0]
        ds = self.dsem[key]
        ins = self.E[q].dma_start(out=out_ap, in_=in_ap)
        ds[1] += 16
        ins.then_inc(ds[0], 16)
        tok = (ds[0], ds[1], "dma")
        self._finish(tok, reads, writes)
        return tok

    def group_done(self, key, tiles):
        ds = self.dsem[key]
        tok = (ds[0], ds[1], "dma")
        for t in tiles:
            for b in t.bufs:
                b.w[ds[0]] = tok

    def barrier(self):
        toks = list(self.all_tokens.values())
        for eng in self.E:
            seen = self.seen[eng]
            for (s, v, src) in toks:
                if seen.get(s, 0) >= v:
                    continue
                seen[s] = v
                self.E[eng].wait_ge(s, v)

    def final_wait(self):
        toks = list(self.all_tokens.values())
        seen = self.seen["sp"]
        for (s, v, src) in toks:
            if seen.get(s, 0) >= v:
                continue
            seen[s] = v
            self.E["sp"].wait_ge(s, v)


_CONST_CACHE = {}


def _bf(a):
    return np.ascontiguousarray(a.astype(ml_dtypes.bfloat16))


def _dft_consts(Ls):
    N = 2 * Ls
    half = N // 2
    n = np.arange(N, dtype=np.float64)
    f = np.arange(half, dtype=np.float64) + 0.5
    k2 = (np.arange(half, dtype=np.int64) * 2 + 1)[:, None] * np.arange(N, dtype=np.int64)[None, :]
    k2 = k2 % (2 * N)
    th = k2.astype(np.float64) * (math.pi / N)
    Fre = np.cos(th)
    Fim = -np.sin(th)
    Fall = np.concatenate([Fre, Fim], axis=0)
    nfc = N // 128
    ntile = N // 128
    FKT = Fall.reshape(nfc, 128, ntile, 128).transpose(0, 3, 2, 1)
    Gre = (2.0 / N) * np.cos(th[:, :Ls])
    Gim = -(2.0 / N) * np.sin(th[:, :Ls])
    Gall = np.concatenate([Gre, Gim], axis=0)
    tch = min(Ls, 256)
    ntch = Ls // tch
    GT = Gall.reshape(nfc, 128, ntch, tch).transpose(2, 1, 0, 3)
    return _bf(FKT), _bf(GT)


def _filter_pos(Ls):
    t = np.arange(Ls, dtype=np.float32) / np.float32(Ls)
    bands = np.linspace(1e-4, 16 - 1, 16, dtype=np.float32)
    ang = (np.float32(2.0 * math.pi) * t[:, None] * bands[None, :]).astype(np.float32)
    z = np.concatenate([t[:, None], np.cos(ang), np.sin(ang)], axis=-1).astype(np.float32)
    idx = (Ls - np.arange(Ls)) % Ls
    zr = z[idx]
    tr = t[idx]
    sgn_f = np.ones(Ls, np.float32)
    sgn_r = -np.ones(Ls, np.float32)
    sgn_r[0] = 0.0
    return z, zr, t, tr, sgn_f, sgn_r


def get_consts():
    if _CONST_CACHE:
        return _CONST_CACHE
    cc = {}
    cc["fkt"], cc["gt"] = _dft_consts(L)
    cc["fktc"], cc["gtc"] = _dft_consts(C)
    z, zr, t, tr, sf, sr = _filter_pos(L)
    zc, zrc, tc_, trc, sfc, src_ = _filter_pos(C)
    cc["zT"] = np.ascontiguousarray(np.concatenate([z.T, zr.T, zc.T, zrc.T], axis=1))
    def cols(v):
        return v.reshape(-1, 128).T
    cc["tcol"] = np.ascontiguousarray(np.concatenate(
        [cols(-t), cols(-tr), cols(-tc_), cols(-trc), cols(sf), cols(sr), cols(sfc), cols(src_)], axis=1)).astype(np.float32)
    deltas = np.abs(np.linspace(math.log(1e-2) / 1.5, math.log(1e-2) / 0.3, D, dtype=np.float32)).astype(np.float32)
    cc["delta"] = deltas.reshape(1, D)
    cc["ident"] = np.eye(128, dtype=np.float32)
    sel = np.zeros((3, 3, 128), np.float32)
    for r in range(3):
        sel[r, r, :] = 1.0
    cc["sel3"] = sel.transpose(1, 0, 2).copy()
    rows = L // 64
    row = np.repeat(np.arange(rows, dtype=np.float32), 64)
    col = np.tile(np.arange(64, dtype=np.float32), rows)
    inv = (np.float32(10000.0) ** (-np.arange(32, dtype=np.float32) / np.float32(32))).astype(np.float32)
    ang = np.concatenate([row[:, None] * inv[None, :], col[:, None] * inv[None, :]], axis=-1).astype(np.float32)
    cc["ropec"] = np.ascontiguousarray(np.cos(ang).astype(np.float32).reshape(NTX, 128, 64).transpose(1, 0, 2))
    cc["ropes"] = np.ascontiguousarray(np.sin(ang).astype(np.float32).reshape(NTX, 128, 64).transpose(1, 0, 2))
    _CONST_CACHE.update(cc)
    return cc


def _colv_layout():
    lay = {}
    off = 0

    def add(name, n):
        nonlocal off
        lay[name] = off
        off += n

    for l in range(2):
        add(f"mix_pre{l}", 8)
        add(f"mlp_pre{l}", 8)
    add("hy_b_in", 24)
    add("hy_cw0", 24)
    add("hy_cw1", 24)
    add("hy_cw2", 24)
    add("hy_cb", 24)
    add("hy_fbias", 8)
    add("f_b1", 1)
    add("f_f1", 1)
    add("f_b2", 1)
    add("f_f2", 1)
    for l in range(2):
        add(f"mod_b{l}", 48)
    lay["_n"] = off
    return lay


COLV = _colv_layout()


def _pcol(v):
    return np.asarray(v, np.float32).reshape(-1, 128).T


def build_colv(inp):
    a = np.zeros((128, COLV["_n"]), np.float32)

    def put(name, v):
        m = _pcol(v)
        a[:, COLV[name]:COLV[name] + m.shape[1]] = m

    for l in range(2):
        put(f"mix_pre{l}", inp["mix_norm_pre"][l])
        put(f"mlp_pre{l}", inp["mlp_norm_pre"][l])
        put(f"mod_b{l}", inp["mod_b"][l])
    put("hy_b_in", inp["hy_b_in"][0])
    for i in range(3):
        put(f"hy_cw{i}", inp["hy_conv_w"][0, i])
    put("hy_cb", inp["hy_conv_b"][0])
    put("hy_fbias", inp["hy_filt_bias"][0])
    for nm, key in (("f_b1", "hy_filt_b1"), ("f_f1", "hy_filt_freq1"), ("f_b2", "hy_filt_b2"), ("f_f2", "hy_filt_freq2")):
        a[:64, COLV[nm]] = inp[key][0]
    return a


ROW_MIX_POST = (0, 2)
ROW_MLP_POST = (1, 3)
ROW_HY_B_OUT = 4
ROW_DELTA = 5
ROW_QN = 6
ROW_KN = 7
ROW_MODB_G = ((8, 9), (10, 11))
N_ROWS = 12


def build_rows(inp, cc):
    r = np.zeros((N_ROWS, D), np.float32)
    r[0] = inp["mix_norm_post"][0]
    r[1] = inp["mlp_norm_post"][0]
    r[2] = inp["mix_norm_post"][1]
    r[3] = inp["mlp_norm_post"][1]
    r[4] = inp["hy_b_out"][0]
    r[5] = cc["delta"][0]
    r[6, :HD] = inp["attn_q_norm"][0]
    r[7, :HD] = inp["attn_k_norm"][0]
    for l in range(2):
        r[8 + 2 * l] = inp["mod_b"][l, 2048:3072]
        r[9 + 2 * l] = inp["mod_b"][l, 5120:6144]
    return r


INPUT_SPECS = [
    ("x", [NB, L, D], F32), ("ctx", [NB, C, D], F32), ("cT", [128, 8, 3], F32),
    ("mod_w", [2, D, 6 * D], F32), ("colv", [128, COLV["_n"]], F32), ("rows", [N_ROWS, D], F32),
    ("mlp_w1", [2, D, DFF], F32), ("mlp_w2", [2, DFF, D], F32),
    ("hy_w_in", [D, 3 * D], F32), ("hy_w_out", [D, D], F32),
    ("attn_w_qkv", [D, 1536], F32), ("attn_w_o", [D, D], F32),
    ("f_w1", [33, 64], F32), ("f_w2", [64, 64], F32), ("f_w3", [64, 2 * D], F32),
    ("fkt", [32, 128, 32, 128], BF16), ("gt", [8, 128, 32, 256], BF16),
    ("fktc", [4, 128, 4, 128], BF16), ("gtc", [1, 128, 4, 256], BF16),
    ("zT", [33, 2 * L + 2 * C], F32), ("tcol", [128, 72], F32),
    ("ident", [128, 128], F32), ("sel3", [3, 3, 128], F32),
    ("ropec", [128, NTX, 64], F32), ("ropes", [128, NTX, 64], F32),
]


class Prog:
    def __init__(self, taps=(), stop_after=None):
        self.k = k = KB()
        nc = k.nc
        self.taps = set(taps)
        self.stop_after = stop_after
        self.io = {}
        for name, shape, dt in INPUT_SPECS:
            self.io[name] = nc.dram_tensor(name, list(shape), dt, kind="ExternalInput").ap()
        self.out = nc.dram_tensor("out", [NB, L, D], F32, kind="ExternalOutput").ap()
        self.tap_aps = {}
        self.XS = k.dram_t("XS", [NB, T, D], F32, nsub=NB * NT)
        self.KF = k.dram_t("KF", [32, 128, D], F32, nsub=32)
        self.KFC = k.dram_t("KFC", [4, 128, D], F32, nsub=4)
        self.X1T = k.dram_t("X1T", [NB, 8, 128, T], BF16, nsub=NB * 8)
        self.UT = k.dram_t("UT", [NB, 8, 128, T], BF16, nsub=NB * 8)
        self.GGD = k.dram_t("GGD", [2, 2, 3, 128, D], F32, nsub=12)
        self.ps = [Tile(nc.alloc_psum_tensor(f"ps{i}", [128, 512], F32), f"ps{i}") for i in range(8)]
        self.ps_rr = 0
        self.ps_pool = self.ps

    def tap(self, name, shape, dtype=F32):
        ap = self.k.nc.dram_tensor("tap_" + name, list(shape), dtype, kind="ExternalOutput").ap()
        self.tap_aps[name] = ap
        return ap

    def next_ps(self):
        pool = self.ps_pool
        p = pool[self.ps_rr % len(pool)]
        self.ps_rr += 1
        return p

    def load_consts(self):
        k, nc, io = self.k, self.k.nc, self.io
        self.colv = k.sb("colv", [128, COLV["_n"]], F32)
        k.dma("sp", self.colv[:], io["colv"], [], [self.colv.b], "c_grp0")
        self.ident = k.sb("ident", [128, 128], BF16)
        k.dma("pool", self.ident[:], io["ident"], [], [self.ident.b], "c_ident")
        self.ones = k.sb("ones", [128, 128], BF16)
        k.op("dve", [], [self.ones.b], lambda: nc.vector.memset(self.ones[:], 1.0))
        self.AS = k.sb("AS", [128, 2, 2, 2, 3, 8], F32)
        self.epsc = k.sb("epsc", [128, 1], F32)
        k.op("dve", [], [self.epsc.b], lambda: nc.vector.memset(self.epsc[:], EPS))

    def cv(self, name, n=1, off=0):
        o = COLV[name] + off
        return self.colv[:, o:o + n]

    def phase_mod_gen(self):
        k, nc, io = self.k, self.k.nc, self.io
        mTop = k.mark_top()
        cT = k.sbt("cT", [128, 24], F32)
        scT = k.sbt("scT", [128, 8, 3], BF16)
        k.dma("sp", cT[:], io["cT"].rearrange("p k r -> p (k r)"), [], [cT.b], "c_cT")
        k.op("act", [cT.b], [scT.b], lambda: nc.scalar.activation(
            out=scT[:].rearrange("p k r -> p (k r)"), in_=cT[:], func=AF.Silu))
        sel = k.sbt("sel3", [3, 3, 128], F32)
        k.dma("sp", sel[:], io["sel3"], [], [sel.b], "c_sel")
        modF = k.sbt("modF", [128, 2, 48, 3], F32)
        tmpr = k.sbt("tmpr", [3, 512], F32)
        W = [k.sbt(f"modW{i}", [128, 8, 512], BF16) for i in range(2)]
        GGs = [k.sbt(f"GGs{i}", [128, D], F32) for i in range(2)]
        R3 = k.sbt("R3", [3, 4, D], F32)
        ggrow = k.sbt("ggrow", [3, 2, D], F32)
        wi = 0
        gi = 0
        for l in range(2):
            for i, r in enumerate((ROW_MIX_POST[l], ROW_MLP_POST[l], ROW_MODB_G[l][0], ROW_MODB_G[l][1])):
                k.dma("sp", R3[:, i, :], io["rows"][r:r + 1, :].partition_broadcast(3), [], [R3.b], "c_R3")
            for cb in range(12):
                blk = cb // 2
                w = W[wi % 2]
                wi += 1
                k.dma("pool", w[:], io["mod_w"][l, :, cb * 512:(cb + 1) * 512].rearrange("(k p) n -> p k n", p=128),
                      [], [w.b], f"modW{wi % 2}")
                ps = self.next_ps()
                if blk in (2, 5):
                    sub = 0 if blk == 2 else 1
                    half = cb % 2

                    def mm_row(ps=ps, w=w):
                        last = None
                        for kk in range(8):
                            last = nc.tensor.matmul(ps[0:3, :], scT[:, kk, :], w[:, kk, :], start=(kk == 0), stop=(kk == 7))
                        return last
                    k.op("pe", [scT.b, w.b], [ps.b], mm_row)
                    rb = 2 + sub
                    rg = sub
                    k.op("dve", [ps.b, R3.b], [tmpr.b], lambda ps=ps, rb=rb, half=half: nc.vector.tensor_tensor(
                        out=tmpr[:], in0=ps[0:3, :], in1=R3[:, rb, half * 512:(half + 1) * 512], op=ALU.add))
                    k.op("dve", [tmpr.b, R3.b], [ggrow.b], lambda sub=sub, rg=rg, half=half: nc.vector.tensor_tensor(
                        out=ggrow[:, sub, half * 512:(half + 1) * 512], in0=tmpr[:],
                        in1=R3[:, rg, half * 512:(half + 1) * 512], op=ALU.mult))
                else:
                    def mm_f(ps=ps, w=w):
                        last = None
                        for q in range(4):
                            for kk in range(8):
                                last = nc.tensor.matmul(ps[:, q * 3:(q + 1) * 3], w[:, kk, q * 128:(q + 1) * 128], scT[:, kk, :],
                                                        start=(kk == 0), stop=(kk == 7))
                        return last
                    k.op("pe", [scT.b, w.b], [ps.b], mm_f)
                    bcol = self.cv(f"mod_b{l}", 4, cb * 4).unsqueeze(2).broadcast_to([128, 4, 3])
                    k.op("dve", [ps.b, self.colv.b], [modF.b], lambda ps=ps, l=l, cb=cb, bcol=bcol: nc.vector.tensor_tensor(
                        out=modF[:, l, cb * 4:(cb + 1) * 4, :], in0=ps[:, 0:12].rearrange("p (q r) -> p q r", r=3),
                        in1=bcol, op=ALU.add))
                yield
            for sub in range(2):
                sh0 = 0 if sub == 0 else 24
                sc0 = 8 if sub == 0 else 32
                gname = (f"mix_pre{l}" if sub == 0 else f"mlp_pre{l}")
                for r in range(3):
                    k.op("dve", [modF.b, self.colv.b], [self.AS.b], lambda l=l, sub=sub, r=r, sc0=sc0, gname=gname:
                         nc.vector.scalar_tensor_tensor(out=self.AS[:, l, sub, 0, r, :], in0=modF[:, l, sc0:sc0 + 8, r], scalar=1.0,
                                                        in1=self.cv(gname, 8), op0=ALU.add, op1=ALU.mult))
                    k.op("dve", [modF.b], [self.AS.b], lambda l=l, sub=sub, r=r, sh0=sh0:
                         nc.vector.tensor_copy(out=self.AS[:, l, sub, 1, r, :], in_=modF[:, l, sh0:sh0 + 8, r]))
            for sub in range(2):
                for r in range(3):
                    g = GGs[gi % 2]
                    gi += 1
                    for half in range(2):
                        ps = self.next_ps()
                        k.op("pe", [sel.b, ggrow.b], [ps.b], lambda ps=ps, sub=sub, r=r, half=half: nc.tensor.matmul(
                            ps[:, :], sel[:, r, :], ggrow[:, sub, half * 512:(half + 1) * 512], start=True, stop=True))
                        k.op("act", [ps.b], [g.b], lambda ps=ps, g=g, half=half: nc.scalar.copy(
                            out=g[:, half * 512:(half + 1) * 512], in_=ps[:, :]))
                    k.dma("act", self.GGD[l, sub, r], g[:], [g.b], [self.GGD.bufs[(l * 2 + sub) * 3 + r]], f"st_GG{(gi - 1) % 2}")
                yield
        k.release_top(mTop)

    def phase_mod(self):
        for _ in self.phase_mod_gen():
            pass

    def _range_reduce(self, ta, tn, csz):
        k, nc = self.k, self.k.nc
        MAGIC = 12582912.0
        k.op("dve", [ta.b], [tn.b], lambda: nc.vector.tensor_scalar(
            out=tn[:, 0:csz], in0=ta[:, 0:csz], scalar1=1.0 / (2.0 * math.pi), scalar2=MAGIC, op0=ALU.mult, op1=ALU.add))
        k.op("dve", [tn.b], [tn.b], lambda: nc.vector.tensor_scalar(
            out=tn[:, 0:csz], in0=tn[:, 0:csz], scalar1=-MAGIC, scalar2=-2.0 * math.pi, op0=ALU.add, op1=ALU.mult))
        k.op("dve", [tn.b, ta.b], [ta.b], lambda: nc.vector.tensor_tensor(
            out=ta[:, 0:csz], in0=ta[:, 0:csz], in1=tn[:, 0:csz], op=ALU.add))

    def phase_filter(self, inter=None):
        k, nc, io = self.k, self.k.nc, self.io
        m0 = k.mark()

        def tick():
            if inter is not None:
                next(inter, None)
        zT = k.sb("zT", [33, 2 * L + 2 * C], F32)
        k.dma("sp", zT[:], io["zT"], [], [zT.b], "c_grp1")
        w1 = k.sb("fw1", [33, 64], F32)
        w2 = k.sb("fw2", [64, 64], F32)
        w3 = k.sb("fw3", [64, 2 * D], F32)
        k.dma("sp", w1[:], io["f_w1"], [], [w1.b], "c_grp1")
        k.dma("sp", w2[:], io["f_w2"], [], [w2.b], "c_grp1")
        k.dma("sp", w3[:], io["f_w3"], [], [w3.b], "c_grp1")
        tcol = k.sb("tcol", [128, 72], F32)
        k.dma("sp", tcol[:], io["tcol"], [], [tcol.b], "c_grp1")
        delta = k.sb("delta", [128, D], F32)
        k.dma("sp", delta[:], io["rows"][ROW_DELTA:ROW_DELTA + 1, :].partition_broadcast(128), [], [delta.b], "c_grp1")
        k.group_done("c_grp1", [zT, w1, w2, w3, tcol, delta])
        fb = k.sb("fb", [64, 2], F32)
        k.op("dve", [self.colv.b], [fb.b], lambda: nc.vector.tensor_tensor(
            out=fb[:, 0:1], in0=self.colv[0:64, COLV["f_f1"]:COLV["f_f1"] + 1], in1=self.colv[0:64, COLV["f_b1"]:COLV["f_b1"] + 1], op=ALU.mult))
        k.op("dve", [self.colv.b], [fb.b], lambda: nc.vector.tensor_tensor(
            out=fb[:, 1:2], in0=self.colv[0:64, COLV["f_f2"]:COLV["f_f2"] + 1], in1=self.colv[0:64, COLV["f_b2"]:COLV["f_b2"] + 1], op=ALU.mult))
        negpi = k.sb("negpi", [128, 1], F32)
        k.op("dve", [], [negpi.b], lambda: nc.vector.memset(negpi[:], -math.pi))
        OFF = math.pi + 2.0 * math.pi * 8
        TWO_PI = 2.0 * math.pi
        K2 = k.sb("K2", [128, 32, D], BF16, nsub=32)
        K2c = k.sb("K2c", [128, 4, D], BF16, nsub=4)
        h1 = k.sb("h1T", [64, 512], F32)
        h2 = [k.sb(f"h2T{i}", [64, 512], F32) for i in range(2)]
        targ = [k.sb(f"targ{i}", [64, 512], F32) for i in range(2)]
        dec = [k.sb(f"dec{i}", [128, D], F32) for i in range(2)]
        tn = k.sb("tn", [64, 512], F32)
        cnt = 0
        jobs = [(0, L, K2, 0, 0, 0, 36), (L, L, K2, 16, D, 16, 52),
                (2 * L, C, K2c, 0, 0, 32, 68), (2 * L + C, C, K2c, 2, D, 34, 70)]
        for (z0, Ls, KB_, t0, wc0, tc0, sg0) in jobs:
            csz = min(Ls, 512)
            for ch in range(Ls // csz):
                c0 = z0 + ch * csz
                ps = self.next_ps()
                k.op("pe", [w1.b, zT.b], [ps.b], lambda ps=ps, c0=c0, csz=csz: nc.tensor.matmul(
                    ps[0:64, 0:csz], w1[:, :], zT[:, c0:c0 + csz], start=True, stop=True))
                ta = targ[cnt % 2]
                k.op("dve", [ps.b, self.colv.b, fb.b], [ta.b], lambda ps=ps, ta=ta, csz=csz: nc.vector.tensor_scalar(
                    out=ta[:, 0:csz], in0=ps[0:64, 0:csz], scalar1=self.colv[0:64, COLV["f_f1"]:COLV["f_f1"] + 1],
                    scalar2=fb[:, 0:1], op0=ALU.mult, op1=ALU.add))
                self._range_reduce(ta, tn, csz)
                k.op("act", [ta.b], [h1.b], lambda ta=ta, csz=csz: nc.scalar.activation(
                    out=h1[:, 0:csz], in_=ta[:, 0:csz], func=AF.Sin))
                ps2 = self.next_ps()
                k.op("pe", [w2.b, h1.b], [ps2.b], lambda ps2=ps2, csz=csz: nc.tensor.matmul(
                    ps2[0:64, 0:csz], w2[:, :], h1[:, 0:csz], start=True, stop=True))
                tb = targ[(cnt + 1) % 2]
                k.op("dve", [ps2.b, self.colv.b, fb.b], [tb.b], lambda ps2=ps2, tb=tb, csz=csz: nc.vector.tensor_scalar(
                    out=tb[:, 0:csz], in0=ps2[0:64, 0:csz], scalar1=self.colv[0:64, COLV["f_f2"]:COLV["f_f2"] + 1],
                    scalar2=fb[:, 1:2], op0=ALU.mult, op1=ALU.add))
                self._range_reduce(tb, tn, csz)
                hh = h2[cnt % 2]
                k.op("act", [tb.b], [hh.b], lambda tb=tb, hh=hh, csz=csz: nc.scalar.activation(
                    out=hh[:, 0:csz], in_=tb[:, 0:csz], func=AF.Sin))
                for tt in range(csz // 128):
                    tile_i = ch * (csz // 128) + tt
                    dc = dec[(cnt + tt) % 2]
                    k.op("act", [delta.b, tcol.b], [dc.b], lambda dc=dc, tc0=tc0, tile_i=tile_i: nc.scalar.activation(
                        out=dc[:], in_=delta[:], func=AF.Exp, scale=tcol[:, tc0 + tile_i:tc0 + tile_i + 1]))
                    for cc_ in range(2):
                        ps3 = self.next_ps()
                        k.op("pe", [hh.b, w3.b], [ps3.b], lambda ps3=ps3, hh=hh, tt=tt, wc0=wc0, cc_=cc_: nc.tensor.matmul(
                            ps3[:, :], hh[:, tt * 128:(tt + 1) * 128], w3[:, wc0 + cc_ * 512:wc0 + (cc_ + 1) * 512],
                            start=True, stop=True))
                        kb = KB_.bufs[t0 + tile_i]
                        k.op("dve", [ps3.b, dc.b, tcol.b], [kb], lambda ps3=ps3, dc=dc, KB_=KB_, t0=t0, tile_i=tile_i, cc_=cc_, sg0=sg0:
                             nc.vector.scalar_tensor_tensor(out=KB_[:, t0 + tile_i, cc_ * 512:(cc_ + 1) * 512], in0=ps3[:, :],
                                                            scalar=tcol[:, sg0 + tile_i:sg0 + tile_i + 1],
                                                            in1=dc[:, cc_ * 512:(cc_ + 1) * 512], op0=ALU.mult, op1=ALU.mult))
                cnt += 1
        if "K2" in self.taps:
            k.dma("sp", self.tap("K2", [128, 32 * D], BF16), K2[:].rearrange("p a b -> p (a b)"), K2.bufs, [], "tap")
            k.dma("sp", self.tap("K2c", [128, 4 * D], BF16), K2c[:].rearrange("p a b -> p (a b)"), K2c.bufs, [], "tap")
        FK = [k.sb(f"FK{i}", [128, 32, 128], BF16) for i in range(2)]
        KFs = [k.sb(f"KFs{i}", [128, D], F32) for i in range(2)]
        for (nfc, ntile, src, KB_, dst, nm) in ((32, 32, io["fkt"], K2, self.KF, "m"), (4, 4, io["fktc"], K2c, self.KFC, "c")):
            for fc in range(nfc):
                fk = FK[fc % 2]
                k.dma("sp", fk[:, 0:ntile, :], src[fc], [], [fk.b], f"FK{fc % 2}")
                kf = KFs[fc % 2]
                for cc_ in range(2):
                    ps = self.next_ps()

                    def mmk(ps=ps, fk=fk, KB_=KB_, cc_=cc_, ntile=ntile):
                        last = None
                        for nt in range(ntile):
                            last = nc.tensor.matmul(ps[:, :], fk[:, nt, :], KB_[:, nt, cc_ * 512:(cc_ + 1) * 512],
                                                    start=(nt == 0), stop=(nt == ntile - 1))
                        return last
                    k.op("pe", [fk.b] + KB_.bufs, [ps.b], mmk)
                    k.op("act", [ps.b], [kf.b], lambda ps=ps, kf=kf, cc_=cc_: nc.scalar.copy(
                        out=kf[:, cc_ * 512:(cc_ + 1) * 512], in_=ps[:, :]))
                k.dma("act", dst[fc], kf[:], [kf.b], [dst.bufs[fc]], f"st_KF{fc % 2}")
                tick()
        if "KF" in self.taps:
            k.dma("sp", self.tap("KF", [32, 128, D]), self.KF[:], [self.KF.b], [], "tap")
            k.dma("sp", self.tap("KFC", [4, 128, D]), self.KFC[:], [self.KFC.b], [], "tap")
        if inter is not None:
            for _ in inter:
                pass
        k.release(m0)

    def psb(self, ps):
        return ps.h[:, :].bitcast(BF16)

    def alloc_norm_tmp(self, nslots=2):
        k = self.k
        self.ntmp = []
        for i in range(nslots):
            self.ntmp.append(dict(
                junk=k.sb(f"njunk{i}", [128, D], BF16), ss=k.sb(f"nss{i}", [128, 2], F32),
                rstd=k.sb(f"nrstd{i}", [128, 1], F32), xh=k.sb(f"nxh{i}", [128, D], BF16)))
        self.ntmp_i = 0

    def norm_stats(self, xt):
        k, nc = self.k, self.k.nc
        tm = self.ntmp[self.ntmp_i % len(self.ntmp)]
        self.ntmp_i += 1
        junk, ss, rstd, xh = tm["junk"], tm["ss"], tm["rstd"], tm["xh"]
        k.op("act", [xt.b], [junk.b, ss.b], lambda: nc.scalar.activation(
            out=junk[:], in_=xt[:], func=AF.Square, accum_out=ss[:, 0:1]))
        k.op("act", [ss.b, self.epsc.b], [rstd.b], lambda: nc.scalar.activation(
            out=rstd[:], in_=ss[:, 0:1], func=AF.Sqrt, scale=1.0 / D, bias=self.epsc[:, 0:1]))
        k.op("dve", [rstd.b], [rstd.b], lambda: nc.vector.reciprocal(out=rstd[:], in_=rstd[:]))
        k.op("dve", [xt.b, rstd.b], [xh.b], lambda: nc.vector.tensor_scalar(
            out=xh[:], in0=xt[:], scalar1=rstd[:, 0:1], scalar2=None, op0=ALU.mult))
        return tm

    def norm_transpose(self, tm, A, S, dst_fn, dst_bufs):
        k, nc = self.k, self.k.nc
        xh = tm["xh"]
        for half in range(2):
            ps = self.next_ps()
            pb = self.psb(ps)

            def tr(pb=pb, half=half):
                last = None
                for q in range(4):
                    kk = half * 4 + q
                    last = nc.tensor.transpose(pb[:, q * 128:(q + 1) * 128], xh[:, kk * 128:(kk + 1) * 128], self.ident[:])
                return last
            k.op("pe", [xh.b, self.ident.b], [ps.b], tr)
            for q in range(4):
                kk = half * 4 + q
                if half == 0:
                    k.op("act", [ps.b, self.AS.b], dst_bufs[0:1], lambda kk=kk, q=q, pb=pb: nc.scalar.activation(
                        out=dst_fn(kk), in_=pb[:, q * 128:(q + 1) * 128], func=AF.Identity,
                        scale=A[:, kk:kk + 1], bias=S[:, kk:kk + 1]))
                else:
                    k.op("dve", [ps.b, self.AS.b], dst_bufs[1:2], lambda kk=kk, q=q, pb=pb: nc.vector.tensor_scalar(
                        out=dst_fn(kk), in0=pb[:, q * 128:(q + 1) * 128], scalar1=A[:, kk:kk + 1], scalar2=S[:, kk:kk + 1],
                        op0=ALU.mult, op1=ALU.add))

    def norm_T(self, xt, A, S, dst_fn, dst_bufs):
        tm = self.norm_stats(xt)
        self.norm_transpose(tm, A, S, dst_fn, dst_bufs)

    def alloc_post_tmp(self):
        k = self.k
        self.ptmp = [dict(ss=k.sb(f"pss{i}", [128, 2], F32), rstd=k.sb(f"prstd{i}", [128, 1], F32),
                          junk=k.sb(f"pjunk{i}", [128, 512], BF16), tmp=k.sb(f"ptmp{i}", [128, 512], F32)) for i in range(2)]
        self.ptmp_i = 0

    def post_residual(self, ps2, xt, GG):
        k, nc = self.k, self.k.nc
        tm = self.ptmp[self.ptmp_i % 2]
        self.ptmp_i += 1
        ss, rstd, junk, tmp = tm["ss"], tm["rstd"], tm["junk"], tm["tmp"]
        for h in range(2):
            k.op("act", [ps2[h].b], [junk.b, ss.b], lambda h=h: nc.scalar.activation(
                out=junk[:], in_=ps2[h][:, :], func=AF.Square, accum_out=ss[:, h:h + 1]))
        k.op("dve", [ss.b], [rstd.b], lambda: nc.vector.tensor_tensor(out=rstd[:], in0=ss[:, 0:1], in1=ss[:, 1:2], op=ALU.add))
        k.op("act", [rstd.b, self.epsc.b], [rstd.b], lambda: nc.scalar.activation(
            out=rstd[:], in_=rstd[:], func=AF.Sqrt, scale=1.0 / D, bias=self.epsc[:, 0:1]))
        k.op("dve", [rstd.b], [rstd.b], lambda: nc.vector.reciprocal(out=rstd[:], in_=rstd[:]))
        for h in range(2):
            sl = slice(h * 512, (h + 1) * 512)
            k.op("dve", [ps2[h].b, rstd.b, GG.b], [tmp.b], lambda h=h, sl=sl: nc.vector.scalar_tensor_tensor(
                out=tmp[:], in0=ps2[h][:, :], scalar=rstd[:, 0:1], in1=GG[:, sl], op0=ALU.mult, op1=ALU.mult))
            k.op("dve", [tmp.b, xt.b], [xt.b], lambda sl=sl: nc.vector.tensor_tensor(
                out=xt[:, sl], in0=xt[:, sl], in1=tmp[:], op=ALU.add))

    def tok_src(self, b, tt, layer0):
        if layer0:
            if tt < NTX:
                return self.io["x"][b, tt * 128:(tt + 1) * 128, :]
            return self.io["ctx"][b, (tt - NTX) * 128:(tt - NTX + 1) * 128, :]
        return self.XS[b, tt * 128:(tt + 1) * 128, :]

    def phase_hyena(self, b):
        k, nc, io = self.k, self.k.nc, self.io
        mB = k.mark()
        U_off = k.sb_off
        U = k.sb("U", [128, NT, D], BF16, nsub=NT)
        mA = k.mark()
        hxT = k.sb("hxT", [128, 8, T], BF16, nsub=2 * NT)
        self.alloc_norm_tmp(3)
        xts = [k.sb(f"xt{i}", [128, D], F32) for i in range(2)]
        tms = {}
        for tt in range(NT + 1):
            if tt < NT:
                xt = xts[tt % 2]
                k.dma("sp", xt[:], self.tok_src(b, tt, True), [], [xt.b], f"xt{tt % 2}")
                tms[tt] = self.norm_stats(xt)
            if tt >= 1:
                t1 = tt - 1
                r = b if t1 < NTX else 2
                self.norm_transpose(tms.pop(t1), self.AS[:, 0, 0, 0, r, :], self.AS[:, 0, 0, 1, r, :],
                                    lambda kk, t1=t1: hxT[:, kk, t1 * 128:(t1 + 1) * 128], hxT.bufs[2 * t1:2 * t1 + 2])
        ZW = 2 + L + 2 + C
        zpad = [k.sb(f"zpad{s_}", [128, ZW], F32) for s_ in range(3)]
        for z in zpad:
            k.op("dve", [], [z.b], lambda z=z: nc.vector.memset(z[:], 0.0))
        acc = [k.sb(f"cacc{s_}", [128, T], F32) for s_ in range(3)]
        x1s = [k.sb(f"x1s{i}", [128, T], BF16) for i in range(2)]
        uTs = [k.sb(f"uTs{i}", [128, T], BF16) for i in range(2)]
        wj = [[k.sb(f"win{i}_{s_}", [128, 8, 128], BF16) for s_ in range(3)] for i in range(2)]
        chunks = [(1 + c * 512, c * 512, 512) for c in range(4)] + [(L + 3, L, C)]
        regions = [(0, 0, L), (L + 2, L, C)]
        def u_transposes(j):
            us_ = uTs[j % 2]
            for g0 in range(0, NT, 8):
                ng = min(8, NT - g0)
                ps = self.next_ps()
                pb = self.psb(ps)

                def tru(pb=pb, g0=g0, ng=ng):
                    last = None
                    for i in range(ng):
                        last = nc.tensor.transpose(pb[:, i * 128:(i + 1) * 128], us_[:, (g0 + i) * 128:(g0 + i + 1) * 128], self.ident[:])
                    return last
                k.op("pe", [us_.b, self.ident.b], [ps.b], tru)
                k.op("act", [ps.b], U.bufs[g0:g0 + ng], lambda pb=pb, g0=g0, ng=ng, j=j: nc.scalar.copy(
                    out=U[:, g0:g0 + ng, j * 128:(j + 1) * 128], in_=pb[:, 0:ng * 128].rearrange("p (a c) -> p a c", c=128)))

        def load_win(j):
            for s_ in range(3):
                c0 = s_ * D + j * 128
                k.dma("pool", wj[j % 2][s_][:], io["hy_w_in"][:, c0:c0 + 128].rearrange("(k p) n -> p k n", p=128),
                      [], [wj[j % 2][s_].b], f"win{j % 2}_{s_}")

        load_win(0)
        for j in range(8):
            ws = wj[j % 2]
            if j + 1 < 8:
                load_win(j + 1)
            for s_ in range(3):
                col = s_ * 8 + j
                for (zc, t0, n) in chunks:
                    ps = self.next_ps()

                    def mmz(ps=ps, w=ws[s_], t0=t0, n=n):
                        last = None
                        for kk in range(8):
                            last = nc.tensor.matmul(ps[:, 0:n], w[:, kk, :], hxT[:, kk, t0:t0 + n], start=(kk == 0), stop=(kk == 7))
                        return last
                    rb = hxT.bufs[2 * (t0 // 128):2 * (t0 // 128 + n // 128)]
                    k.op("pe", [ws[s_].b] + rb, [ps.b], mmz)
                    k.op("act", [ps.b, self.colv.b], [zpad[s_].b], lambda ps=ps, s_=s_, zc=zc, n=n, col=col: nc.scalar.activation(
                        out=zpad[s_][:, zc:zc + n], in_=ps[:, 0:n], func=AF.Identity,
                        bias=self.cv("hy_b_in", 1, col), scale=1.0))
            xs_ = x1s[j % 2]
            us_ = uTs[j % 2]
            for s_ in range(3):
                col = s_ * 8 + j
                eng = "dve"
                E = nc.vector
                for (zb, tb, n) in regions:
                    k.op(eng, [zpad[s_].b, self.colv.b], [acc[s_].b], lambda E=E, s_=s_, zb=zb, tb=tb, n=n, col=col: E.tensor_scalar(
                        out=acc[s_][:, tb:tb + n], in0=zpad[s_][:, zb:zb + n], scalar1=self.cv("hy_cw0", 1, col),
                        scalar2=self.cv("hy_cb", 1, col), op0=ALU.mult, op1=ALU.add))
                    k.op(eng, [zpad[s_].b, self.colv.b, acc[s_].b], [acc[s_].b], lambda E=E, s_=s_, zb=zb, tb=tb, n=n, col=col: E.scalar_tensor_tensor(
                        out=acc[s_][:, tb:tb + n], in0=zpad[s_][:, zb + 1:zb + 1 + n], scalar=self.cv("hy_cw1", 1, col),
                        in1=acc[s_][:, tb:tb + n], op0=ALU.mult, op1=ALU.add))
                    if s_ == 0:
                        k.op(eng, [zpad[s_].b, self.colv.b, acc[s_].b], [xs_.b], lambda E=E, s_=s_, zb=zb, tb=tb, n=n, col=col: E.scalar_tensor_tensor(
                            out=xs_[:, tb:tb + n], in0=zpad[s_][:, zb + 2:zb + 2 + n], scalar=self.cv("hy_cw2", 1, col),
                            in1=acc[s_][:, tb:tb + n], op0=ALU.mult, op1=ALU.add))
                    else:
                        k.op(eng, [zpad[s_].b, self.colv.b, acc[s_].b], [acc[s_].b], lambda E=E, s_=s_, zb=zb, tb=tb, n=n, col=col: E.scalar_tensor_tensor(
                            out=acc[s_][:, tb:tb + n], in0=zpad[s_][:, zb + 2:zb + 2 + n], scalar=self.cv("hy_cw2", 1, col),
                            in1=acc[s_][:, tb:tb + n], op0=ALU.mult, op1=ALU.add))
            k.op("dve", [acc[1].b, acc[2].b], [us_.b], lambda: nc.vector.tensor_tensor(
                out=us_[:], in0=acc[2][:], in1=acc[1][:], op=ALU.mult))
            k.dma("pool", self.X1T[b, j], xs_[:], [xs_.b], [self.X1T.bufs[b * 8 + j]], f"st_X1T{j % 2}")
            k.dma("pool", self.UT[b, j], us_[:], [us_.b], [self.UT.bufs[b * 8 + j]], f"st_UT{j % 2}")
            if j >= 1:
                u_transposes(j - 1)
        u_transposes(7)
        if "U" in self.taps and b == 0:
            k.dma("sp", self.tap("U", [128, NT * D], BF16), U[:].rearrange("p a c -> p (a c)"), U.bufs, [], "tap")
            k.dma("sp", self.tap("X1T", [8, 128, T], BF16), self.X1T[0], [self.X1T.b], [], "tap")
        k.release(mA)
        if self.stop_after == "HA":
            return
        mTop = k.mark_top()
        P = k.sbt("P", [128, 32, D], BF16, nsub=32)
        Pc = k.sbt("Pc", [128, 4, D], BF16, nsub=4)
        mU = k.mark()
        Fr = [k.sb(f"Fr{i}", [128, 16, 128], BF16) for i in range(2)]
        Fi = [k.sb(f"Fi{i}", [128, 16, 128], BF16) for i in range(2)]
        kre = [k.sb(f"kre{i}", [128, D], F32) for i in range(2)]
        kim = [k.sb(f"kim{i}", [128, D], F32) for i in range(2)]
        t1 = k.sb("pw1", [128, 512], F32)
        t2 = k.sb("pw2", [128, 512], F32)
        jobs = [(16, 16, io["fkt"], self.KF, P, 0, 16, "m"), (2, 2, io["fktc"], self.KFC, Pc, 16, 2, "c")]
        it = 0
        for (nfh, nk, fsrc, kfsrc, Pd, ut0, imoff, nm) in jobs:
            for i in range(nfh):
                fr, fi_, kr, ki = Fr[it % 2], Fi[it % 2], kre[it % 2], kim[it % 2]
                it += 1
                k.dma("sp", fr[:, 0:nk, :], fsrc[i, :, 0:nk, :], [], [fr.b], f"Fr{it % 2}")
                k.dma("sp", fi_[:, 0:nk, :], fsrc[imoff + i, :, 0:nk, :], [], [fi_.b], f"Fi{it % 2}")
                k.dma("sp", kr[:], kfsrc[i], [kfsrc.bufs[i]], [kr.b], f"kre{it % 2}")
                k.dma("sp", ki[:], kfsrc[imoff + i], [kfsrc.bufs[imoff + i]], [ki.b], f"kim{it % 2}")
                for ch in range(2):
                    sl = slice(ch * 512, (ch + 1) * 512)
                    psr = self.next_ps()
                    psi = self.next_ps()
                    ub = U.bufs[ut0:ut0 + nk]
                    for (ps_, f_) in ((psr, fr), (psi, fi_)):
                        def mmf(ps_=ps_, f_=f_, sl=sl, nk=nk, ut0=ut0):
                            last = None
                            for kk in range(nk):
                                last = nc.tensor.matmul(ps_[:, :], f_[:, kk, :], U[:, ut0 + kk, sl], start=(kk == 0), stop=(kk == nk - 1))
                            return last
                        k.op("pe", [f_.b] + ub, [ps_.b], mmf)
                    k.op("dve", [psr.b, kr.b], [t1.b], lambda psr=psr, kr=kr, sl=sl: nc.vector.tensor_tensor(
                        out=t1[:], in0=psr[:, :], in1=kr[:, sl], op=ALU.mult))
                    k.op("dve", [psi.b, ki.b], [t2.b], lambda psi=psi, ki=ki, sl=sl: nc.vector.tensor_tensor(
                        out=t2[:], in0=psi[:, :], in1=ki[:, sl], op=ALU.mult))
                    k.op("dve", [t1.b, t2.b], [Pd.bufs[i]], lambda Pd=Pd, i=i, sl=sl: nc.vector.tensor_tensor(
                        out=Pd[:, i, sl], in0=t1[:], in1=t2[:], op=ALU.subtract))
                    k.op("dve", [psr.b, ki.b], [t1.b], lambda psr=psr, ki=ki, sl=sl: nc.vector.tensor_tensor(
                        out=t1[:], in0=psr[:, :], in1=ki[:, sl], op=ALU.mult))
                    k.op("dve", [psi.b, kr.b], [t2.b], lambda psi=psi, kr=kr, sl=sl: nc.vector.tensor_tensor(
                        out=t2[:], in0=psi[:, :], in1=kr[:, sl], op=ALU.mult))
                    k.op("dve", [t1.b, t2.b], [Pd.bufs[imoff + i]], lambda Pd=Pd, i=i, imoff=imoff, sl=sl: nc.vector.tensor_tensor(
                        out=Pd[:, imoff + i, sl], in0=t1[:], in1=t2[:], op=ALU.add))
        k.release(mU)
        gT = k.sb_at("gatedT", [128, 8, T], BF16, U_off, alias_of=[U], nsub=NT)
        mI = k.mark()
        G = [k.sb(f"G{i}", [128, 32, 256], BF16) for i in range(2)]
        x1t = [k.sb(f"x1t{i}", [128, 8, 256], BF16) for i in range(2)]
        utt = [k.sb(f"utt{i}", [128, 8, 256], BF16) for i in range(2)]
        vt = [k.sb(f"vt{i}", [128, 256], F32) for i in range(2)]
        for tc in range(9):
            g, xa, ua = G[tc % 2], x1t[tc % 2], utt[tc % 2]
            if tc < 8:
                k.dma("sp", g[:], io["gt"][tc], [], [g.b], f"G{tc % 2}")
                nfc, Pd = 32, P
            else:
                k.dma("sp", g[:, 0:4, :], io["gtc"][0], [], [g.b], f"G{tc % 2}")
                nfc, Pd = 4, Pc
            tsl = slice(tc * 256, (tc + 1) * 256)
            k.dma("sp", xa[:], self.X1T[b, :, :, tsl].rearrange("j p t -> p j t"), self.X1T.bufs[b * 8:b * 8 + 8], [xa.b], f"x1t{tc % 2}")
            k.dma("sp", ua[:], self.UT[b, :, :, tsl].rearrange("j p t -> p j t"), self.UT.bufs[b * 8:b * 8 + 8], [ua.b], f"utt{tc % 2}")
            for j in range(8):
                ps = self.next_ps()

                def mmi(ps=ps, g=g, Pd=Pd, nfc=nfc, j=j):
                    last = None
                    for fc in range(nfc):
                        last = nc.tensor.matmul(ps[:, 0:256], Pd[:, fc, j * 128:(j + 1) * 128], g[:, fc, :], start=(fc == 0), stop=(fc == nfc - 1))
                    return last
                k.op("pe", [g.b] + Pd.bufs, [ps.b], mmi)
                v_ = vt[j % 2]
                k.op("dve", [ps.b, ua.b, self.colv.b], [v_.b], lambda ps=ps, ua=ua, v_=v_, j=j: nc.vector.scalar_tensor_tensor(
                    out=v_[:], in0=ua[:, j, :], scalar=self.cv("hy_fbias", 1, j), in1=ps[:, 0:256], op0=ALU.mult, op1=ALU.add))
                k.op("dve", [v_.b, xa.b], [gT.bufs[2 * tc], gT.bufs[2 * tc + 1]], lambda v_=v_, xa=xa, j=j, tsl=tsl: nc.vector.tensor_tensor(
                    out=gT[:, j, tsl], in0=v_[:], in1=xa[:, j, :], op=ALU.mult))
        if "gT" in self.taps and b == 0:
            k.dma("sp", self.tap("gT", [128, 8 * T], BF16), gT[:].rearrange("p a c -> p (a c)"), gT.bufs, [], "tap")
        k.release(mI)
        k.release_top(mTop)
        wo = k.sb("hy_wo", [128, 8, D], BF16)
        k.dma("pool", wo[:], io["hy_w_out"].rearrange("(k p) n -> p k n", p=128), [], [wo.b], "hy_wo")
        bo = k.sb("hy_bo", [1, D], BF16)
        k.dma("pool", bo[:], io["rows"][ROW_HY_B_OUT:ROW_HY_B_OUT + 1, :], [], [bo.b], "hy_bo")
        GGx = k.sb("GGx", [128, D], F32)
        GGc = k.sb("GGc", [128, D], F32)
        k.dma("sp", GGx[:], self.GGD[0, 0, b], [self.GGD.bufs[b]], [GGx.b], "GGx")
        k.dma("sp", GGc[:], self.GGD[0, 0, 2], [self.GGD.bufs[2]], [GGc.b], "GGc")
        self.alloc_post_tmp()
        xts = [k.sb(f"xo{i}", [128, D], F32) for i in range(3)]
        for tt in range(NT):
            xt = xts[tt % 3]
            k.dma("sp", xt[:], self.tok_src(b, tt, True), [], [xt.b], f"xo{tt % 3}")
            ps2 = [self.next_ps(), self.next_ps()]
            for h in range(2):
                def mmo(h=h, ps=ps2[h], tt=tt):
                    for j in range(8):
                        nc.tensor.matmul(ps[:, :], gT[:, j, tt * 128:(tt + 1) * 128], wo[:, j, h * 512:(h + 1) * 512], start=(j == 0), stop=False)
                    return nc.tensor.matmul(ps[:, :], self.ones[0:1, :], bo[:, h * 512:(h + 1) * 512], start=False, stop=True)
                k.op("pe", [gT.bufs[tt], wo.b, bo.b, self.ones.b], [ps2[h].b], mmo)
            self.post_residual(ps2, xt, GGx if tt < NTX else GGc)
            k.dma("pool", self.XS[b, tt * 128:(tt + 1) * 128, :], xt[:], [xt.b], [self.XS.bufs[b * NT + tt]], f"st_xo{tt % 3}")
        k.release(mB)

    def phase_mlp(self, l, final):
        k, nc, io = self.k, self.k.nc, self.io
        m0 = k.mark()
        mTop = k.mark_top()
        w1c, w2c = [], []
        for c in range(8):
            a = k.sbt(f"w1c{c}", [128, 8, 512], BF16)
            k.dma("pool", a[:], io["mlp_w1"][l, :, c * 512:(c + 1) * 512].rearrange("(k p) n -> p k n", p=128),
                  [], [a.b], f"w1c{c}")
            w1c.append(a)
            a2 = k.sbt(f"w2c{c}", [128, 4, D], BF16)
            k.dma("pool", a2[:], io["mlp_w2"][l, c * 512:(c + 1) * 512, :].rearrange("(f p) n -> p f n", p=128),
                  [], [a2.b], f"w2c{c}")
            w2c.append(a2)
        GG = []
        for r in range(3):
            g = k.sb(f"GGm{r}", [128, D], F32)
            k.dma("sp", g[:], self.GGD[l, 1, r], [self.GGD.bufs[(l * 2 + 1) * 3 + r]], [g.b], f"GGm{r}")
            GG.append(g)
        self.alloc_norm_tmp(2)
        self.alloc_post_tmp()
        self.pjunk = k.sb("pjunk", [128, D], BF16)
        ntt = NTX if final else NT
        tiles = [(b, tt) for b in range(NB) for tt in range(ntt)]
        blocks = [tiles[i:i + 2] for i in range(0, len(tiles), 2)]
        xts = [k.sb(f"xm{i}", [128, D], F32) for i in range(4)]
        hxT = [k.sb(f"hxTm{i}", [128, 8, 256], BF16, nsub=4) for i in range(2)]
        hT = [k.sb(f"hTm{i}", [128, 256], BF16) for i in range(4)]
        rl = [k.sb(f"rlm{i}", [128, 256], F32) for i in range(2)]
        ysb = [k.sb(f"ysb{i}", [128, D], F32) for i in range(4)]
        out_banks = self.ps[0:4]
        self.ps_pool = self.ps[4:8]
        xi = 0
        blk_x = {}

        def load_block(n):
            nonlocal xi
            xs = []
            for (b, tt) in blocks[n]:
                xt = xts[xi % 4]
                k.dma("sp", xt[:], self.tok_src(b, tt, False), [self.XS.bufs[b * NT + tt]], [xt.b], f"xm{xi % 4}")
                xi += 1
                xs.append(xt)
            blk_x[n] = xs

        blk_tm = {}

        def stats_block(n):
            blk_tm[n] = [self.norm_stats(blk_x[n][i]) for i in range(len(blocks[n]))]

        def tr_block(n):
            hx = hxT[n % 2]
            for i, (b, tt) in enumerate(blocks[n]):
                r = b if tt < NTX else 2
                self.norm_transpose(blk_tm[n][i], self.AS[:, l, 1, 0, r, :], self.AS[:, l, 1, 1, r, :],
                                    lambda kk, hx=hx, i=i: hx[:, kk, i * 128:(i + 1) * 128], hx.bufs[2 * i:2 * i + 2])

        def norm_block(n):
            stats_block(n)
            tr_block(n)

        def post_tile(n, i):
            b, tt = blocks[n][i]
            y = ysb[(2 * n + i) % 4]
            xt = blk_x[n][i]
            r = b if tt < NTX else 2
            self.post_residual_sb(y, xt, GG[r])
            if final:
                k.dma("pool", self.out[b, tt * 128:(tt + 1) * 128, :], xt[:], [xt.b], [], f"st_xm{(2 * n + i) % 4}")
            else:
                k.dma("pool", self.XS[b, tt * 128:(tt + 1) * 128, :], xt[:], [xt.b], [self.XS.bufs[b * NT + tt]], f"st_xm{(2 * n + i) % 4}")

        load_block(0)
        load_block(1)
        norm_block(0)
        SK = 3
        f_ctr = 0
        for n in range(len(blocks)):
            hx = hxT[n % 2]
            for f in range(32 + SK):
                if f < 32:
                    ps1 = self.next_ps()
                    h_ = hT[f_ctr % 4]
                    r_ = rl[f_ctr % 2]
                    f_ctr += 1

                    def mm1(ps1=ps1, f=f, hx=hx):
                        last = None
                        for kk in range(8):
                            last = nc.tensor.matmul(ps1[:, 0:256], w1c[f // 4][:, kk, (f % 4) * 128:(f % 4 + 1) * 128], hx[:, kk, :], start=(kk == 0), stop=(kk == 7))
                        return last
                    k.op("pe", [w1c[f // 4].b] + hx.bufs, [ps1.b], mm1)
                    k.op("act", [ps1.b], [r_.b], lambda ps1=ps1, r_=r_: nc.scalar.activation(out=r_[:], in_=ps1[:, 0:256], func=AF.Relu))
                    k.op("dve", [r_.b], [h_.b], lambda r_=r_, h_=h_: nc.vector.tensor_tensor(out=h_[:], in0=r_[:], in1=r_[:], op=ALU.mult))
                if f >= SK:
                    f2 = f - SK
                    h2 = hT[(f_ctr - (min(f, 31) - f2) - 1) % 4] if False else None
                    h2 = hT[(n * 32 + f2) % 4]

                    def mm2(h2=h2, f2=f2):
                        last = None
                        for i in range(2):
                            for hf in range(2):
                                last = nc.tensor.matmul(out_banks[2 * i + hf][:, :], h2[:, i * 128:(i + 1) * 128],
                                                        w2c[f2 // 4][:, f2 % 4, hf * 512:(hf + 1) * 512], start=(f2 == 0), stop=(f2 == 31))
                        return last
                    k.op("pe", [h2.b, w2c[f2 // 4].b], [ob.b for ob in out_banks], mm2)
                if n >= 1 and f in (3, 7):
                    post_tile(n - 1, 0 if f == 3 else 1)
                if f == 8 and n + 1 < len(blocks) and n >= 1:
                    load_block(n + 1)
                if f == 12 and n + 1 < len(blocks):
                    stats_block(n + 1)
                if f == 26 and n + 1 < len(blocks):
                    tr_block(n + 1)
            for i, (b, tt) in enumerate(blocks[n]):
                y = ysb[(2 * n + i) % 4]
                for hf in range(2):
                    k.op("act", [out_banks[2 * i + hf].b], [y.b], lambda y=y, i=i, hf=hf: nc.scalar.copy(
                        out=y[:, hf * 512:(hf + 1) * 512], in_=out_banks[2 * i + hf][:, :]))
        post_tile(len(blocks) - 1, 0)
        post_tile(len(blocks) - 1, 1)
        self.ps_pool = self.ps
        k.release(m0)
        k.release_top(mTop)

    def post_residual_sb(self, y, xt, GG):
        k, nc = self.k, self.k.nc
        tm = self.ptmp[self.ptmp_i % 2]
        self.ptmp_i += 1
        ss, rstd = tm["ss"], tm["rstd"]
        junk = self.pjunk
        k.op("act", [y.b], [junk.b, ss.b], lambda: nc.scalar.activation(
            out=junk[:], in_=y[:], func=AF.Square, accum_out=ss[:, 0:1]))
        k.op("act", [ss.b, self.epsc.b], [rstd.b], lambda: nc.scalar.activation(
            out=rstd[:], in_=ss[:, 0:1], func=AF.Sqrt, scale=1.0 / D, bias=self.epsc[:, 0:1]))
        k.op("dve", [rstd.b], [rstd.b], lambda: nc.vector.reciprocal(out=rstd[:], in_=rstd[:]))
        k.op("dve", [y.b, rstd.b, GG.b], [y.b], lambda: nc.vector.scalar_tensor_tensor(
            out=y[:], in0=y[:], scalar=rstd[:, 0:1], in1=GG[:], op0=ALU.mult, op1=ALU.mult))
        k.op("dve", [y.b, xt.b], [xt.b], lambda: nc.vector.tensor_tensor(out=xt[:], in0=xt[:], in1=y[:], op=ALU.add))

    def phase_attn(self, b):
        k, nc, io = self.k, self.k.nc, self.io
        l = 1
        m0 = k.mark()
        QT = k.sb("QT", [128, NH, L], BF16, nsub=NTX)
        KT = k.sb("KT", [128, NKV, T], BF16, nsub=NT)
        V = k.sb("V", [128, NT, NKV, HD], BF16, nsub=NT)
        mA = k.mark()
        hxT = k.sb("hxTa", [128, 8, T], BF16, nsub=2 * NT)
        wqkv = k.sb("wqkv", [128, 8, 1536], BF16)
        k.dma("pool", wqkv[:], io["attn_w_qkv"].rearrange("(k p) n -> p k n", p=128), [], [wqkv.b], "wqkv")
        ropec = k.sb("ropec", [128, NTX, 64], F32)
        ropes = k.sb("ropes", [128, NTX, 64], F32)
        k.dma("sp", ropec[:], io["ropec"], [], [ropec.b], "c_grp2")
        k.dma("sp", ropes[:], io["ropes"], [], [ropes.b], "c_grp2")
        qg = k.sb("qg", [128, HD], F32)
        kg = k.sb("kg", [128, HD], F32)
        k.dma("sp", qg[:], io["rows"][ROW_QN:ROW_QN + 1, 0:HD].partition_broadcast(128), [], [qg.b], "c_grp2")
        k.dma("sp", kg[:], io["rows"][ROW_KN:ROW_KN + 1, 0:HD].partition_broadcast(128), [], [kg.b], "c_grp2")
        k.group_done("c_grp2", [ropec, ropes, qg, kg])
        self.alloc_norm_tmp(3)
        xts = [k.sb(f"xa{i}", [128, D], F32) for i in range(3)]
        sq = [k.sb(f"sq{i}", [128, 10, HD], F32) for i in range(2)]
        qn = [k.sb(f"qn{i}", [128, 10, HD], F32) for i in range(2)]
        qr = [k.sb(f"qr{i}", [128, 10, HD], BF16) for i in range(2)]
        ss10 = [k.sb(f"ss10_{i}", [128, 10], F32) for i in range(2)]
        ra = [k.sb(f"ra{i}", [128, 10, 64], F32) for i in range(2)]
        rb_ = [k.sb(f"rb{i}", [128, 10, 64], F32) for i in range(2)]
        rc_ = [k.sb(f"rc{i}", [128, 10, 64], F32) for i in range(2)]
        rd_ = [k.sb(f"rd{i}", [128, 10, 64], F32) for i in range(2)]
        POOL = nc.gpsimd

        def stA(tt):
            xt = xts[tt % 3]
            k.dma("sp", xt[:], self.tok_src(b, tt, False), [self.XS.bufs[b * NT + tt]], [xt.b], f"xa{tt % 3}")
            return self.norm_stats(xt)

        def stB(tt, tm):
            r = b if tt < NTX else 2
            self.norm_transpose(tm, self.AS[:, l, 0, 0, r, :], self.AS[:, l, 0, 1, r, :],
                                lambda kk, tt=tt: hxT[:, kk, tt * 128:(tt + 1) * 128], hxT.bufs[2 * tt:2 * tt + 2])

        def stC(tt):
            isx = tt < NTX
            h0 = 0 if isx else 8
            banks = []
            if isx:
                for c in range(2):
                    ps = self.next_ps()

                    def mmq(ps=ps, c=c, tt=tt):
                        last = None
                        for kk in range(8):
                            last = nc.tensor.matmul(ps[:, :], hxT[:, kk, tt * 128:(tt + 1) * 128], wqkv[:, kk, c * 512:(c + 1) * 512],
                                                    start=(kk == 0), stop=(kk == 7))
                        return last
                    k.op("pe", hxT.bufs[2 * tt:2 * tt + 2] + [wqkv.b], [ps.b], mmq)
                    banks.append(ps)
            pskv = self.next_ps()

            def mmkv(ps=pskv, tt=tt):
                last = None
                for kk in range(8):
                    last = nc.tensor.matmul(ps[:, :], hxT[:, kk, tt * 128:(tt + 1) * 128], wqkv[:, kk, 1024:1536],
                                            start=(kk == 0), stop=(kk == 7))
                return last
            k.op("pe", hxT.bufs[2 * tt:2 * tt + 2] + [wqkv.b], [pskv.b], mmkv)
            sq_, qn_, qr_, ss_ = sq[tt % 2], qn[tt % 2], qr[tt % 2], ss10[tt % 2]
            ra_, rbb, rcc, rdd = ra[tt % 2], rb_[tt % 2], rc_[tt % 2], rd_[tt % 2]
            k.op("act", [pskv.b], [V.bufs[tt]], lambda pskv=pskv, tt=tt: nc.scalar.copy(
                out=V[:, tt, :, :], in_=pskv[:, 256:512].rearrange("p (g d) -> p g d", d=HD)))
            srcs = []
            if isx:
                srcs += [(banks[0], 0, 512, 0, 4), (banks[1], 0, 512, 4, 4)]
            srcs.append((pskv, 0, 256, 8, 2))
            for (ps, c0, ncol, hs, nhh) in srcs:
                k.op("act", [ps.b], [sq_.b], lambda ps=ps, c0=c0, ncol=ncol, hs=hs, nhh=nhh: nc.scalar.activation(
                    out=sq_[:, hs:hs + nhh, :], in_=ps[:, c0:c0 + ncol].rearrange("p (h d) -> p h d", d=HD), func=AF.Square))
            k.op("dve", [sq_.b], [ss_.b], lambda: nc.vector.tensor_reduce(
                out=ss_[:, h0:10], in_=sq_[:, h0:10, :], axis=AX.X, op=ALU.add))
            k.op("act", [ss_.b, self.epsc.b], [ss_.b], lambda: nc.scalar.activation(
                out=ss_[:, h0:10], in_=ss_[:, h0:10], func=AF.Sqrt, scale=1.0 / HD, bias=self.epsc[:, 0:1]))
            k.op("dve", [ss_.b], [ss_.b], lambda: nc.vector.reciprocal(out=ss_[:, h0:10], in_=ss_[:, h0:10]))
            if isx:
                for (ps, c0, ncol, hs, nhh) in srcs:
                    for hh in range(nhh):
                        gt_ = qg if hs + hh < 8 else kg
                        k.op("dve", [ps.b, ss_.b, gt_.b], [qn_.b], lambda ps=ps, c0=c0, hs=hs, hh=hh, gt_=gt_: nc.vector.scalar_tensor_tensor(
                            out=qn_[:, hs + hh, :], in0=ps[:, c0 + hh * HD:c0 + (hh + 1) * HD], scalar=ss_[:, hs + hh:hs + hh + 1],
                            in1=gt_[:, :], op0=ALU.mult, op1=ALU.mult))
                x0 = qn_[:, :, :].rearrange("p h (i two) -> p h i two", two=2)[:, :, :, 0]
                x1 = qn_[:, :, :].rearrange("p h (i two) -> p h i two", two=2)[:, :, :, 1]
                o0 = qr_[:, :, :].rearrange("p h (i two) -> p h i two", two=2)[:, :, :, 0]
                o1 = qr_[:, :, :].rearrange("p h (i two) -> p h i two", two=2)[:, :, :, 1]
                cosb = ropec[:, tt, :].unsqueeze(1).broadcast_to([128, 10, 64])
                sinb = ropes[:, tt, :].unsqueeze(1).broadcast_to([128, 10, 64])
                k.op("dve", [qn_.b, ropec.b], [ra_.b], lambda: nc.vector.tensor_tensor(out=ra_[:], in0=x0, in1=cosb, op=ALU.mult))
                k.op("dve", [qn_.b, ropes.b], [rbb.b], lambda: nc.vector.tensor_tensor(out=rbb[:], in0=x1, in1=sinb, op=ALU.mult))
                k.op("dve", [ra_.b, rbb.b], [qr_.b], lambda: nc.vector.tensor_tensor(out=o0, in0=ra_[:], in1=rbb[:], op=ALU.subtract))
                k.op("dve", [qn_.b, ropes.b], [rcc.b], lambda: nc.vector.tensor_tensor(out=rcc[:], in0=x0, in1=sinb, op=ALU.mult))
                k.op("dve", [qn_.b, ropec.b], [rdd.b], lambda: nc.vector.tensor_tensor(out=rdd[:], in0=x1, in1=cosb, op=ALU.mult))
                k.op("dve", [rcc.b, rdd.b], [qr_.b], lambda: nc.vector.tensor_tensor(out=o1, in0=rcc[:], in1=rdd[:], op=ALU.add))
            else:
                for hh in range(2):
                    k.op("dve", [pskv.b, ss_.b, kg.b], [qr_.b], lambda hh=hh: nc.vector.scalar_tensor_tensor(
                        out=qr_[:, 8 + hh, :], in0=pskv[:, hh * HD:(hh + 1) * HD], scalar=ss_[:, 8 + hh:9 + hh],
                        in1=kg[:, :], op0=ALU.mult, op1=ALU.mult))

        def stD(tt):
            isx = tt < NTX
            h0 = 0 if isx else 8
            qr_ = qr[tt % 2]
            ps = self.next_ps()
            pb = self.psb(ps)

            def trq(pb=pb, qr_=qr_, h0=h0):
                last = None
                for hh in range(h0, 8 if h0 == 0 else 10):
                    last = nc.tensor.transpose(pb[:, (hh - h0) * 128:(hh - h0 + 1) * 128], qr_[:, hh, :], self.ident[:])
                return last
            k.op("pe", [qr_.b, self.ident.b], [ps.b], trq)
            if isx:
                k.op("act", [ps.b], [QT.bufs[tt]], lambda pb=pb, tt=tt: nc.scalar.copy(
                    out=QT[:, :, tt * 128:(tt + 1) * 128], in_=pb[:, 0:1024].rearrange("p (h t) -> p h t", t=128)))
                ps2 = self.next_ps()
                pb2 = self.psb(ps2)
                k.op("pe", [qr_.b, self.ident.b], [ps2.b], lambda pb2=pb2, qr_=qr_: [nc.tensor.transpose(
                    pb2[:, g * 128:(g + 1) * 128], qr_[:, 8 + g, :], self.ident[:]) for g in range(2)][-1])
                k.op("act", [ps2.b], [KT.bufs[tt]], lambda pb2=pb2, tt=tt: nc.scalar.copy(
                    out=KT[:, :, tt * 128:(tt + 1) * 128], in_=pb2[:, 0:256].rearrange("p (g t) -> p g t", t=128)))
            else:
                k.op("act", [ps.b], [KT.bufs[tt]], lambda pb=pb, tt=tt: nc.scalar.copy(
                    out=KT[:, :, tt * 128:(tt + 1) * 128], in_=pb[:, 0:256].rearrange("p (g t) -> p g t", t=128)))

        tms = {}
        for i in range(NT + 3):
            if i < NT:
                tms[i] = stA(i)
            if 0 <= i - 1 < NT:
                stB(i - 1, tms.pop(i - 1))
            if 0 <= i - 2 < NT:
                stC(i - 2)
            if 0 <= i - 3 < NT:
                stD(i - 3)
        if "QT" in self.taps and b == 0:
            k.dma("sp", self.tap("QT", [128, NH * L], BF16), QT[:].rearrange("p a c -> p (a c)"), QT.bufs, [], "tap")
            k.dma("sp", self.tap("KT", [128, NKV * T], BF16), KT[:].rearrange("p a c -> p (a c)"), KT.bufs, [], "tap")
            k.dma("sp", self.tap("V", [128, NT * NKV * HD], BF16), V[:].rearrange("p a c d -> p (a c d)"), V.bufs, [], "tap")
        k.release(mA)
        OT = k.sb("OT", [128, NH, L], BF16, nsub=NTX)
        PT = [k.sb(f"PT{i}", [128, 512], BF16) for i in range(4)]
        rec = [k.sb(f"rec{i}", [128, 512], F32) for i in range(2)]
        s_banks = self.ps[0:4]
        o_banks = self.ps[4:6]
        d_banks = self.ps[6:8]
        SCALE = float(HD) ** -0.5
        SK = 2
        it = 0
        sc = 0
        for g in range(NKV):
            for hq in range(NH // NKV):
                h = g * (NH // NKV) + hq
                for qc in range(L // 512):
                    qsl = slice(qc * 512, (qc + 1) * 512)
                    qbufs = QT.bufs[qc * 4:qc * 4 + 4]
                    pso = o_banks[it % 2]
                    psd = d_banks[it % 2]
                    base = sc
                    for kc in range(NT + SK):
                        if kc < NT:
                            pss = s_banks[sc % 4]
                            pt = PT[sc % 4]
                            sc += 1
                            k.op("pe", [KT.bufs[kc]] + qbufs, [pss.b], lambda pss=pss, kc=kc, g=g, h=h, qsl=qsl: nc.tensor.matmul(
                                pss[:, :], KT[:, g, kc * 128:(kc + 1) * 128], QT[:, h, qsl], start=True, stop=True))
                            k.op("act", [pss.b], [pt.b], lambda pss=pss, pt=pt: nc.scalar.activation(
                                out=pt[:], in_=pss[:, :], func=AF.Exp, scale=SCALE))
                        if kc >= SK:
                            k2_ = kc - SK
                            pt2 = PT[(base + k2_) % 4]

                            def mmpv(pt2=pt2, k2_=k2_, g=g, pso=pso, psd=psd):
                                nc.tensor.matmul(pso[:, :], V[:, k2_, g, :], pt2[:], start=(k2_ == 0), stop=(k2_ == NT - 1))
                                return nc.tensor.matmul(psd[:, :], self.ones[:, :], pt2[:], start=(k2_ == 0), stop=(k2_ == NT - 1))
                            k.op("pe", [pt2.b, V.bufs[k2_], self.ones.b], [pso.b, psd.b], mmpv)
                    rc = rec[it % 2]
                    k.op("dve", [psd.b], [rc.b], lambda psd=psd, rc=rc: nc.vector.reciprocal(out=rc[:], in_=psd[:, :]))
                    k.op("dve", [pso.b, rc.b], OT.bufs[qc * 4:qc * 4 + 4], lambda pso=pso, rc=rc, h=h, qsl=qsl: nc.vector.tensor_tensor(
                        out=OT[:, h, qsl], in0=pso[:, :], in1=rc[:], op=ALU.mult))
                    it += 1
        if "OT" in self.taps and b == 0:
            k.dma("sp", self.tap("OT", [128, NH * L], BF16), OT[:].rearrange("p a c -> p (a c)"), OT.bufs, [], "tap")
        wo = k.sb("at_wo", [128, NH, D], BF16)
        k.dma("pool", wo[:], io["attn_w_o"].rearrange("(h p) n -> p h n", p=128), [], [wo.b], "at_wo")
        GGx = k.sb("GGa", [128, D], F32)
        k.dma("sp", GGx[:], self.GGD[l, 0, b], [self.GGD.bufs[(l * 2) * 3 + b]], [GGx.b], "GGa")
        self.alloc_post_tmp()
        xts = [k.sb(f"xao{i}", [128, D], F32) for i in range(3)]
        for tt in range(NTX):
            xt = xts[tt % 3]
            k.dma("sp", xt[:], self.tok_src(b, tt, False), [self.XS.bufs[b * NT + tt]], [xt.b], f"xao{tt % 3}")
            ps2 = [self.next_ps(), self.next_ps()]
            for hf in range(2):
                def mmo(hf=hf, ps=ps2[hf], tt=tt):
                    last = None
                    for h in range(NH):
                        last = nc.tensor.matmul(ps[:, :], OT[:, h, tt * 128:(tt + 1) * 128], wo[:, h, hf * 512:(hf + 1) * 512],
                                                start=(h == 0), stop=(h == NH - 1))
                    return last
                k.op("pe", [OT.bufs[tt], wo.b], [ps2[hf].b], mmo)
            self.post_residual(ps2, xt, GGx)
            k.dma("pool", self.XS[b, tt * 128:(tt + 1) * 128, :], xt[:], [xt.b], [self.XS.bufs[b * NT + tt]], f"st_xao{tt % 3}")
        k.release(m0)

    def finish(self):
        self.k.final_wait()
        return self.k.nc


def make_in_maps(inp, ncores=8):
    cc = get_consts()
    f32 = np.float32
    shared = {
        "mod_w": np.ascontiguousarray(inp["mod_w"], f32),
        "colv": build_colv(inp),
        "rows": build_rows(inp, cc),
        "mlp_w1": np.ascontiguousarray(inp["mlp_w1"], f32),
        "mlp_w2": np.ascontiguousarray(inp["mlp_w2"], f32),
        "hy_w_in": np.ascontiguousarray(inp["hy_w_in"][0], f32),
        "hy_w_out": np.ascontiguousarray(inp["hy_w_out"][0], f32),
        "attn_w_qkv": np.ascontiguousarray(inp["attn_w_qkv"][0], f32),
        "attn_w_o": np.ascontiguousarray(inp["attn_w_o"][0], f32),
        "f_w1": np.ascontiguousarray(inp["hy_filt_w1"][0], f32),
        "f_w2": np.ascontiguousarray(inp["hy_filt_w2"][0], f32),
        "f_w3": np.ascontiguousarray(inp["hy_filt_w3"][0], f32),
        "fkt": cc["fkt"], "gt": cc["gt"], "fktc": cc["fktc"], "gtc": cc["gtc"],
        "zT": cc["zT"], "tcol": cc["tcol"], "ident": cc["ident"], "sel3": cc["sel3"],
        "ropec": cc["ropec"], "ropes": cc["ropes"],
    }
    maps = []
    for i in range(ncores):
        b0 = i * NB
        cT = np.stack([inp["c"][b0], inp["c"][b0 + 1], inp["c_ctx"]], axis=-1)
        cT = np.ascontiguousarray(cT.reshape(8, 128, 3).transpose(1, 0, 2), f32)
        m = dict(shared)
        m["x"] = np.ascontiguousarray(inp["x"][b0:b0 + NB], f32)
        m["ctx"] = np.ascontiguousarray(inp["ctx"][b0:b0 + NB], f32)
        m["cT"] = cT
        maps.append(m)
    return maps


_PROG_CACHE = {}


def build_program():
    p = Prog()
    p.load_consts()
    p.phase_filter(p.phase_mod_gen())
    for b in range(NB):
        p.phase_hyena(b)
    p.phase_mlp(0, False)
    for b in range(NB):
        p.phase_attn(b)
    p.phase_mlp(1, True)
    return p.finish()


def kernel(**inputs):
    inp = {k_: np.asarray(v) for k_, v in inputs.items()}
    n_cores = 8
    maps = make_in_maps(inp, n_cores)
    nc = build_program()
    res = run_bass_kernel_spmd(nc, maps, core_ids=list(range(n_cores)))
    outs = [np.asarray(res.results[i]["out"], dtype=np.float32) for i in range(n_cores)]
    return np.concatenate(outs, axis=0)
```

```python
import math
import numpy as np
import ml_dtypes
import concourse.bass as bass
import concourse.mybir as mybir
from concourse.bass_utils import run_bass_kernel_spmd

F32 = mybir.dt.float32
BF16 = mybir.dt.bfloat16
AF = mybir.ActivationFunctionType
ALU = mybir.AluOpType
AX = mybir.AxisListType

D = 1024
L = 2048
C = 256
T = L + C
NB = 2
DFF = 4096
EPS = 1e-6
NT = T // 128
NTX = L // 128
HD = 128
NH = 8
NKV = 2
N_DFT = 2 * L
N_DFTC = 2 * C


class Buf:
    __slots__ = ("name", "w", "r")

    def __init__(self, name):
        self.name = name
        self.w = {}
        self.r = []


class Tile:
    def __init__(self, h, name, nsub=1):
        self.h = h
        self.name = name
        self.bufs = [Buf(f"{name}.{i}") for i in range(nsub)]

    def __getitem__(self, idx):
        return self.h[idx]

    @property
    def b(self):
        return self.bufs[0]


class KB:
    def __init__(self):
        nc = self.nc = bass.Bass("TRN2", target_bir_lowering=False)
        self.E = {"pe": nc.tensor, "act": nc.scalar, "dve": nc.vector, "pool": nc.gpsimd, "sp": nc.sync}
        self.sem = {e: nc.alloc_semaphore("s_" + e) for e in ("pe", "act", "dve", "pool")}
        self.tick = {e: 0 for e in self.sem}
        self.seen = {e: {} for e in self.E}
        self.dsem = {}
        self.all_tokens = {}
        self.sb_off = (nc.sbuf_base + 31) // 32 * 32
        self.sb_top = nc.sbuf_top // 32 * 32
        self.top_off = self.sb_top
        self.rel = []
        self.sb_peak = 0
        self.n_names = 0
        self.dram = {}

    def _inherit(self, lo, hi, tile):
        toks = {}
        for (a, b_, tl) in self.rel:
            if a < hi and lo < b_:
                for tok in tl:
                    if tok[0] not in toks or toks[tok[0]][1] < tok[1]:
                        toks[tok[0]] = tok
        if toks:
            for buf in tile.bufs:
                buf.r = list(toks.values())

    @staticmethod
    def _per(shape, dtype):
        per = 4 if dtype == F32 else 2
        for s_ in shape[1:]:
            per *= s_
        return (per + 31) // 32 * 32

    def sb(self, name, shape, dtype, nsub=1):
        per = self._per(shape, dtype)
        off = self.sb_off
        assert off + per <= self.top_off, f"SBUF overflow allocating {name}: {off}+{per} > {self.top_off}"
        self.n_names += 1
        h = self.nc.alloc_sbuf_tensor_at(f"{name}_{self.n_names}", list(shape), dtype, offset=off)
        self.sb_off = off + per
        self.sb_peak = max(self.sb_peak, self.sb_off + (self.sb_top - self.top_off))
        t = Tile(h, name, nsub)
        self._inherit(off, off + per, t)
        return t

    def sbt(self, name, shape, dtype, nsub=1):
        per = self._per(shape, dtype)
        off = self.top_off - per
        assert off >= self.sb_off, f"SBUF overflow (top) allocating {name}: {off} < {self.sb_off}"
        self.n_names += 1
        h = self.nc.alloc_sbuf_tensor_at(f"{name}_{self.n_names}", list(shape), dtype, offset=off)
        self.top_off = off
        self.sb_peak = max(self.sb_peak, self.sb_off + (self.sb_top - self.top_off))
        t = Tile(h, name, nsub)
        self._inherit(off, off + per, t)
        return t

    def sb_at(self, name, shape, dtype, off, alias_of=(), nsub=1):
        self.n_names += 1
        h = self.nc.alloc_sbuf_tensor_at(f"{name}_{self.n_names}", list(shape), dtype, offset=off)
        t = Tile(h, name, nsub)
        toks = {}
        for a in alias_of:
            for buf in a.bufs:
                for tok in list(buf.w.values()) + list(buf.r):
                    if tok[0] not in toks or toks[tok[0]][1] < tok[1]:
                        toks[tok[0]] = tok
        for buf in t.bufs:
            buf.r = list(toks.values())
        return t

    def mark(self):
        return self.sb_off

    def release(self, mark):
        if self.sb_off > mark:
            self.rel.append((mark, self.sb_off, list(self.all_tokens.values())))
        self.sb_off = mark

    def mark_top(self):
        return self.top_off

    def release_top(self, mark):
        if mark > self.top_off:
            self.rel.append((self.top_off, mark, list(self.all_tokens.values())))
        self.top_off = mark

    def dram_t(self, name, shape, dtype, kind="Internal", nsub=1):
        h = self.nc.dram_tensor(name, list(shape), dtype, kind=kind)
        t = Tile(h.ap(), name, nsub)
        self.dram[name] = t
        return t

    def _collect(self, eng, reads, writes, is_dma):
        need = {}

        def add(tok, raw):
            if tok is None:
                return
            s, v, src = tok
            if (not is_dma) and src == eng and not raw and eng == "pe":
                return
            if need.get(s, 0) < v:
                need[s] = v

        for b in reads:
            for t in b.w.values():
                add(t, True)
        for b in writes:
            for t in b.w.values():
                add(t, False)
            for t in b.r:
                add(t, False)
        seen = self.seen[eng]
        out = []
        for s, v in need.items():
            if seen.get(s, 0) >= v:
                continue
            seen[s] = v
            out.append((s, v))
        return out

    def _emit_waits(self, eng, waits):
        e = self.E[eng]
        for s, v in waits:
            e.wait_ge(s, v)

    def _finish(self, tok, reads, writes):
        for b in writes:
            b.w[tok[0]] = tok
            b.r = []
        for b in reads:
            b.r.append(tok)
            if len(b.r) > 64:
                mx = {}
                for (s, v, src) in b.r:
                    if s not in mx or mx[s][1] < v:
                        mx[s] = (s, v, src)
                b.r = list(mx.values())
        self.all_tokens[tok[0]] = tok

    def op(self, eng, reads, writes, fn):
        waits = self._collect(eng, reads, writes, False)
        self._emit_waits(eng, waits)
        last = fn()
        self.tick[eng] += 1
        last.then_inc(self.sem[eng], 1)
        tok = (self.sem[eng], self.tick[eng], eng)
        self._finish(tok, reads, writes)
        return tok

    def dma(self, q, out_ap, in_ap, reads, writes, key):
        waits = self._collect(q, reads, writes, True)
        self._emit_waits(q, waits)
        if key not in self.dsem:
            self.dsem[key] = [self.nc.alloc_semaphore("d_" + key), 0]
        ds = self.dsem[key]
        ins = self.E[q].dma_start(out=out_ap, in_=in_ap)
        ds[1] += 16
        ins.then_inc(ds[0], 16)
        tok = (ds[0], ds[1], "dma")
        self._finish(tok, reads, writes)
        return tok

    def group_done(self, key, tiles):
        ds = self.dsem[key]
        tok = (ds[0], ds[1], "dma")
        for t in tiles:
            for b in t.bufs:
                b.w[ds[0]] = tok

    def barrier(self):
        toks = list(self.all_tokens.values())
        for eng in self.E:
            seen = self.seen[eng]
            for (s, v, src) in toks:
                if seen.get(s, 0) >= v:
                    continue
                seen[s] = v
                self.E[eng].wait_ge(s, v)

    def final_wait(self):
        toks = list(self.all_tokens.values())
        seen = self.seen["sp"]
        for (s, v, src) in toks:
            if seen.get(s, 0) >= v:
                continue
            seen[s] = v
            self.E["sp"].wait_ge(s, v)


_CONST_CACHE = {}


def _bf(a):
    return np.ascontiguousarray(a.astype(ml_dtypes.bfloat16))


def _dft_consts(Ls):
    N = 2 * Ls
    half = N // 2
    n = np.arange(N, dtype=np.float64)
    f = np.arange(half, dtype=np.float64) + 0.5
    k2 = (np.arange(half, dtype=np.int64) * 2 + 1)[:, None] * np.arange(N, dtype=np.int64)[None, :]
    k2 = k2 % (2 * N)
    th = k2.astype(np.float64) * (math.pi / N)
    Fre = np.cos(th)
    Fim = -np.sin(th)
    Fall = np.concatenate([Fre, Fim], axis=0)
    nfc = N // 128
    ntile = N // 128
    FKT = Fall.reshape(nfc, 128, ntile, 128).transpose(0, 3, 2, 1)
    Gre = (2.0 / N) * np.cos(th[:, :Ls])
    Gim = -(2.0 / N) * np.sin(th[:, :Ls])
    Gall = np.concatenate([Gre, Gim], axis=0)
    tch = min(Ls, 256)
    ntch = Ls // tch
    GT = Gall.reshape(nfc, 128, ntch, tch).transpose(2, 1, 0, 3)
    return _bf(FKT), _bf(GT)


def _filter_pos(Ls):
    t = np.arange(Ls, dtype=np.float32) / np.float32(Ls)
    bands = np.linspace(1e-4, 16 - 1, 16, dtype=np.float32)
    ang = (np.float32(2.0 * math.pi) * t[:, None] * bands[None, :]).astype(np.float32)
    z = np.concatenate([t[:, None], np.cos(ang), np.sin(ang)], axis=-1).astype(np.float32)
    idx = (Ls - np.arange(Ls)) % Ls
    zr = z[idx]
    tr = t[idx]
    sgn_f = np.ones(Ls, np.float32)
    sgn_r = -np.ones(Ls, np.float32)
    sgn_r[0] = 0.0
    return z, zr, t, tr, sgn_f, sgn_r


def get_consts():
    if _CONST_CACHE:
        return _CONST_CACHE
    cc = {}
    cc["fkt"], cc["gt"] = _dft_consts(L)
    cc["fktc"], cc["gtc"] = _dft_consts(C)
    z, zr, t, tr, sf, sr = _filter_pos(L)
    zc, zrc, tc_, trc, sfc, src_ = _filter_pos(C)
    cc["zT"] = np.ascontiguousarray(np.concatenate([z.T, zr.T, zc.T, zrc.T], axis=1))
    def cols(v):
        return v.reshape(-1, 128).T
    cc["tcol"] = np.ascontiguousarray(np.concatenate(
        [cols(-t), cols(-tr), cols(-tc_), cols(-trc), cols(sf), cols(sr), cols(sfc), cols(src_)], axis=1)).astype(np.float32)
    deltas = np.abs(np.linspace(math.log(1e-2) / 1.5, math.log(1e-2) / 0.3, D, dtype=np.float32)).astype(np.float32)
    cc["delta"] = deltas.reshape(1, D)
    cc["ident"] = np.eye(128, dtype=np.float32)
    sel = np.zeros((3, 3, 128), np.float32)
    for r in range(3):
        sel[r, r, :] = 1.0
    cc["sel3"] = sel.transpose(1, 0, 2).copy()
    rows = L // 64
    row = np.repeat(np.arange(rows, dtype=np.float32), 64)
    col = np.tile(np.arange(64, dtype=np.float32), rows)
    inv = (np.float32(10000.0) ** (-np.arange(32, dtype=np.float32) / np.float32(32))).astype(np.float32)
    ang = np.concatenate([row[:, None] * inv[None, :], col[:, None] * inv[None, :]], axis=-1).astype(np.float32)
    cc["ropec"] = np.ascontiguousarray(np.cos(ang).astype(np.float32).reshape(NTX, 128, 64).transpose(1, 0, 2))
    cc["ropes"] = np.ascontiguousarray(np.sin(ang).astype(np.float32).reshape(NTX, 128, 64).transpose(1, 0, 2))
    _CONST_CACHE.update(cc)
    return cc


def _colv_layout():
    lay = {}
    off = 0

    def add(name, n):
        nonlocal off
        lay[name] = off
        off += n

    for l in range(2):
        add(f"mix_pre{l}", 8)
        add(f"mlp_pre{l}", 8)
    add("hy_b_in", 24)
    add("hy_cw0", 24)
    add("hy_cw1", 24)
    add("hy_cw2", 24)
    add("hy_cb", 24)
    add("hy_fbias", 8)
    add("f_b1", 1)
    add("f_f1", 1)
    add("f_b2", 1)
    add("f_f2", 1)
    for l in range(2):
        add(f"mod_b{l}", 48)
    lay["_n"] = off
    return lay


COLV = _colv_layout()


def _pcol(v):
    return np.asarray(v, np.float32).reshape(-1, 128).T


def build_colv(inp):
    a = np.zeros((128, COLV["_n"]), np.float32)

    def put(name, v):
        m = _pcol(v)
        a[:, COLV[name]:COLV[name] + m.shape[1]] = m

    for l in range(2):
        put(f"mix_pre{l}", inp["mix_norm_pre"][l])
        put(f"mlp_pre{l}", inp["mlp_norm_pre"][l])
        put(f"mod_b{l}", inp["mod_b"][l])
    put("hy_b_in", inp["hy_b_in"][0])
    for i in range(3):
        put(f"hy_cw{i}", inp["hy_conv_w"][0, i])
    put("hy_cb", inp["hy_conv_b"][0])
    put("hy_fbias", inp["hy_filt_bias"][0])
    for nm, key in (("f_b1", "hy_filt_b1"), ("f_f1", "hy_filt_freq1"), ("f_b2", "hy_filt_b2"), ("f_f2", "hy_filt_freq2")):
        a[:64, COLV[nm]] = inp[key][0]
    return a


ROW_MIX_POST = (0, 2)
ROW_MLP_POST = (1, 3)
ROW_HY_B_OUT = 4
ROW_DELTA = 5
ROW_QN = 6
ROW_KN = 7
ROW_MODB_G = ((8, 9), (10, 11))
N_ROWS = 12


def build_rows(inp, cc):
    r = np.zeros((N_ROWS, D), np.float32)
    r[0] = inp["mix_norm_post"][0]
    r[1] = inp["mlp_norm_post"][0]
    r[2] = inp["mix_norm_post"][1]
    r[3] = inp["mlp_norm_post"][1]
    r[4] = inp["hy_b_out"][0]
    r[5] = cc["delta"][0]
    r[6, :HD] = inp["attn_q_norm"][0]
    r[7, :HD] = inp["attn_k_norm"][0]
    for l in range(2):
        r[8 + 2 * l] = inp["mod_b"][l, 2048:3072]
        r[9 + 2 * l] = inp["mod_b"][l, 5120:6144]
    return r


INPUT_SPECS = [
    ("x", [NB, L, D], F32), ("ctx", [NB, C, D], F32), ("cT", [128, 8, 3], F32),
    ("mod_w", [2, D, 6 * D], F32), ("colv", [128, COLV["_n"]], F32), ("rows", [N_ROWS, D], F32),
    ("mlp_w1", [2, D, DFF], F32), ("mlp_w2", [2, DFF, D], F32),
    ("hy_w_in", [D, 3 * D], F32), ("hy_w_out", [D, D], F32),
    ("attn_w_qkv", [D, 1536], F32), ("attn_w_o", [D, D], F32),
    ("f_w1", [33, 64], F32), ("f_w2", [64, 64], F32), ("f_w3", [64, 2 * D], F32),
    ("fkt", [32, 128, 32, 128], BF16), ("gt", [8, 128, 32, 256], BF16),
    ("fktc", [4, 128, 4, 128], BF16), ("gtc", [1, 128, 4, 256], BF16),
    ("zT", [33, 2 * L + 2 * C], F32), ("tcol", [128, 72], F32),
    ("ident", [128, 128], F32), ("sel3", [3, 3, 128], F32),
    ("ropec", [128, NTX, 64], F32), ("ropes", [128, NTX, 64], F32),
]


class Prog:
    def __init__(self, taps=(), stop_after=None):
        self.k = k = KB()
        nc = k.nc
        self.taps = set(taps)
        self.stop_after = stop_after
        self.io = {}
        for name, shape, dt in INPUT_SPECS:
            self.io[name] = nc.dram_tensor(name, list(shape), dt, kind="ExternalInput").ap()
        self.out = nc.dram_tensor("out", [NB, L, D], F32, kind="ExternalOutput").ap()
        self.tap_aps = {}
        self.XS = k.dram_t("XS", [NB, T, D], F32, nsub=NB * NT)
        self.KF = k.dram_t("KF", [32, 128, D], F32, nsub=32)
        self.KFC = k.dram_t("KFC", [4, 128, D], F32, nsub=4)
        self.X1T = k.dram_t("X1T", [NB, 8, 128, T], BF16, nsub=NB * 8)
        self.UT = k.dram_t("UT", [NB, 8, 128, T], BF16, nsub=NB * 8)
        self.GGD = k.dram_t("GGD", [2, 2, 3, 128, D], F32, nsub=12)
        self.ps = [Tile(nc.alloc_psum_tensor(f"ps{i}", [128, 512], F32), f"ps{i}") for i in range(8)]
        self.ps_rr = 0
        self.ps_pool = self.ps

    def tap(self, name, shape, dtype=F32):
        ap = self.k.nc.dram_tensor("tap_" + name, list(shape), dtype, kind="ExternalOutput").ap()
        self.tap_aps[name] = ap
        return ap

    def next_ps(self):
        pool = self.ps_pool
        p = pool[self.ps_rr % len(pool)]
        self.ps_rr += 1
        return p

    def load_consts(self):
        k, nc, io = self.k, self.k.nc, self.io
        self.colv = k.sb("colv", [128, COLV["_n"]], F32)
        k.dma("sp", self.colv[:], io["colv"], [], [self.colv.b], "c_grp0")
        self.ident = k.sb("ident", [128, 128], BF16)
        k.dma("pool", self.ident[:], io["ident"], [], [self.ident.b], "c_ident")
        self.ones = k.sb("ones", [128, 128], BF16)
        k.op("dve", [], [self.ones.b], lambda: nc.vector.memset(self.ones[:], 1.0))
        self.AS = k.sb("AS", [128, 2, 2, 2, 3, 8], F32)
        self.epsc = k.sb("epsc", [128, 1], F32)
        k.op("dve", [], [self.epsc.b], lambda: nc.vector.memset(self.epsc[:], EPS))

    def cv(self, name, n=1, off=0):
        o = COLV[name] + off
        return self.colv[:, o:o + n]

    def phase_mod_gen(self):
        k, nc, io = self.k, self.k.nc, self.io
        mTop = k.mark_top()
        cT = k.sbt("cT", [128, 24], F32)
        scT = k.sbt("scT", [128, 8, 3], BF16)
        k.dma("sp", cT[:], io["cT"].rearrange("p k r -> p (k r)"), [], [cT.b], "c_cT")
        k.op("act", [cT.b], [scT.b], lambda: nc.scalar.activation(
            out=scT[:].rearrange("p k r -> p (k r)"), in_=cT[:], func=AF.Silu))
        sel = k.sbt("sel3", [3, 3, 128], F32)
        k.dma("sp", sel[:], io["sel3"], [], [sel.b], "c_sel")
        modF = k.sbt("modF", [128, 2, 48, 3], F32)
        tmpr = k.sbt("tmpr", [3, 512], F32)
        W = [k.sbt(f"modW{i}", [128, 8, 512], BF16) for i in range(2)]
        GGs = [k.sbt(f"GGs{i}", [128, D], F32) for i in range(2)]
        R3 = k.sbt("R3", [3, 4, D], F32)
        ggrow = k.sbt("ggrow", [3, 2, D], F32)
        wi = 0
        gi = 0
        for l in range(2):
            for i, r in enumerate((ROW_MIX_POST[l], ROW_MLP_POST[l], ROW_MODB_G[l][0], ROW_MODB_G[l][1])):
                k.dma("sp", R3[:, i, :], io["rows"][r:r + 1, :].partition_broadcast(3), [], [R3.b], "c_R3")
            for cb in range(12):
                blk = cb // 2
                w = W[wi % 2]
                wi += 1
                k.dma("pool", w[:], io["mod_w"][l, :, cb * 512:(cb + 1) * 512].rearrange("(k p) n -> p k n", p=128),
                      [], [w.b], f"modW{wi % 2}")
                ps = self.next_ps()
                if blk in (2, 5):
                    sub = 0 if blk == 2 else 1
                    half = cb % 2

                    def mm_row(ps=ps, w=w):
                        last = None
                        for kk in range(8):
                            last = nc.tensor.matmul(ps[0:3, :], scT[:, kk, :], w[:, kk, :], start=(kk == 0), stop=(kk == 7))
                        return last
                    k.op("pe", [scT.b, w.b], [ps.b], mm_row)
                    rb = 2 + sub
                    rg = sub
                    k.op("dve", [ps.b, R3.b], [tmpr.b], lambda ps=ps, rb=rb, half=half: nc.vector.tensor_tensor(
                        out=tmpr[:], in0=ps[0:3, :], in1=R3[:, rb, half * 512:(half + 1) * 512], op=ALU.add))
                    k.op("dve", [tmpr.b, R3.b], [ggrow.b], lambda sub=sub, rg=rg, half=half: nc.vector.tensor_tensor(
                        out=ggrow[:, sub, half * 512:(half + 1) * 512], in0=tmpr[:],
                        in1=R3[:, rg, half * 512:(half + 1) * 512], op=ALU.mult))
                else:
                    def mm_f(ps=ps, w=w):
                        last = None
                        for q in range(4):
                            for kk in range(8):
                                last = nc.tensor.matmul(ps[:, q * 3:(q + 1) * 3], w[:, kk, q * 128:(q + 1) * 128], scT[:, kk, :],
                                                        start=(kk == 0), stop=(kk == 7))
                        return last
                    k.op("pe", [scT.b, w.b], [ps.b], mm_f)
                    bcol = self.cv(f"mod_b{l}", 4, cb * 4).unsqueeze(2).broadcast_to([128, 4, 3])
                    k.op("dve", [ps.b, self.colv.b], [modF.b], lambda ps=ps, l=l, cb=cb, bcol=bcol: nc.vector.tensor_tensor(
                        out=modF[:, l, cb * 4:(cb + 1) * 4, :], in0=ps[:, 0:12].rearrange("p (q r) -> p q r", r=3),
                        in1=bcol, op=ALU.add))
                yield
            for sub in range(2):
                sh0 = 0 if sub == 0 else 24
                sc0 = 8 if sub == 0 else 32
                gname = (f"mix_pre{l}" if sub == 0 else f"mlp_pre{l}")
                for r in range(3):
                    k.op("dve", [modF.b, self.colv.b], [self.AS.b], lambda l=l, sub=sub, r=r, sc0=sc0, gname=gname:
                         nc.vector.scalar_tensor_tensor(out=self.AS[:, l, sub, 0, r, :], in0=modF[:, l, sc0:sc0 + 8, r], scalar=1.0,
                                                        in1=self.cv(gname, 8), op0=ALU.add, op1=ALU.mult))
                    k.op("dve", [modF.b], [self.AS.b], lambda l=l, sub=sub, r=r, sh0=sh0:
                         nc.vector.tensor_copy(out=self.AS[:, l, sub, 1, r, :], in_=modF[:, l, sh0:sh0 + 8, r]))
            for sub in range(2):
                for r in range(3):
                    g = GGs[gi % 2]
                    gi += 1
                    for half in range(2):
                        ps = self.next_ps()
                        k.op("pe", [sel.b, ggrow.b], [ps.b], lambda ps=ps, sub=sub, r=r, half=half: nc.tensor.matmul(
                            ps[:, :], sel[:, r, :], ggrow[:, sub, half * 512:(half + 1) * 512], start=True, stop=True))
                        k.op("act", [ps.b], [g.b], lambda ps=ps, g=g, half=half: nc.scalar.copy(
                            out=g[:, half * 512:(half + 1) * 512], in_=ps[:, :]))
                    k.dma("act", self.GGD[l, sub, r], g[:], [g.b], [self.GGD.bufs[(l * 2 + sub) * 3 + r]], f"st_GG{(gi - 1) % 2}")
                yield
        k.release_top(mTop)

    def phase_mod(self):
        for _ in self.phase_mod_gen():
            pass

    def _range_reduce(self, ta, tn, csz):
        k, nc = self.k, self.k.nc
        MAGIC = 12582912.0
        k.op("dve", [ta.b], [tn.b], lambda: nc.vector.tensor_scalar(
            out=tn[:, 0:csz], in0=ta[:, 0:csz], scalar1=1.0 / (2.0 * math.pi), scalar2=MAGIC, op0=ALU.mult, op1=ALU.add))
        k.op("dve", [tn.b], [tn.b], lambda: nc.vector.tensor_scalar(
            out=tn[:, 0:csz], in0=tn[:, 0:csz], scalar1=-MAGIC, scalar2=-2.0 * math.pi, op0=ALU.add, op1=ALU.mult))
        k.op("dve", [tn.b, ta.b], [ta.b], lambda: nc.vector.tensor_tensor(
            out=ta[:, 0:csz], in0=ta[:, 0:csz], in1=tn[:, 0:csz], op=ALU.add))

    def phase_filter(self, inter=None):
        k, nc, io = self.k, self.k.nc, self.io
        m0 = k.mark()

        def tick():
            if inter is not None:
                next(inter, None)
        K2 = k.sb("K2", [128, 32, D], BF16, nsub=32)
        K2c = k.sb("K2c", [128, 4, D], BF16, nsub=4)
        mC = k.mark()
        zT = k.sb("zT", [33, 2 * L + 2 * C], F32)
        k.dma("sp", zT[:], io["zT"], [], [zT.b], "c_grp1")
        w1 = k.sb("fw1", [33, 64], F32)
        w2 = k.sb("fw2", [64, 64], F32)
        w3 = k.sb("fw3", [64, 2 * D], F32)
        k.dma("sp", w1[:], io["f_w1"], [], [w1.b], "c_grp1")
        k.dma("sp", w2[:], io["f_w2"], [], [w2.b], "c_grp1")
        k.dma("sp", w3[:], io["f_w3"], [], [w3.b], "c_grp1")
        tcol = k.sb("tcol", [128, 72], F32)
        k.dma("sp", tcol[:], io["tcol"], [], [tcol.b], "c_grp1")
        delta = k.sb("delta", [128, D], F32)
        k.dma("sp", delta[:], io["rows"][ROW_DELTA:ROW_DELTA + 1, :].partition_broadcast(128), [], [delta.b], "c_grp1")
        k.group_done("c_grp1", [zT, w1, w2, w3, tcol, delta])
        fb = k.sb("fb", [64, 2], F32)
        k.op("dve", [self.colv.b], [fb.b], lambda: nc.vector.tensor_tensor(
            out=fb[:, 0:1], in0=self.colv[0:64, COLV["f_f1"]:COLV["f_f1"] + 1], in1=self.colv[0:64, COLV["f_b1"]:COLV["f_b1"] + 1], op=ALU.mult))
        k.op("dve", [self.colv.b], [fb.b], lambda: nc.vector.tensor_tensor(
            out=fb[:, 1:2], in0=self.colv[0:64, COLV["f_f2"]:COLV["f_f2"] + 1], in1=self.colv[0:64, COLV["f_b2"]:COLV["f_b2"] + 1], op=ALU.mult))
        h1 = [k.sb(f"h1T{i}", [64, 512], F32) for i in range(2)]
        h2 = [k.sb(f"h2T{i}", [64, 512], F32) for i in range(2)]
        tA = [k.sb(f"targA{i}", [64, 512], F32) for i in range(2)]
        tB = [k.sb(f"targB{i}", [64, 512], F32) for i in range(2)]
        tnA = [k.sb(f"tnA{i}", [64, 512], F32) for i in range(2)]
        tnB = [k.sb(f"tnB{i}", [64, 512], F32) for i in range(2)]
        dec = [k.sb(f"dec{i}", [128, D], F32) for i in range(2)]
        jobs = [(0, L, K2, 0, 0, 0, 36), (L, L, K2, 16, D, 16, 52),
                (2 * L, C, K2c, 0, 0, 32, 68), (2 * L + C, C, K2c, 2, D, 34, 70)]
        chunks = []
        for (z0, Ls, KB_, t0, wc0, tc0, sg0) in jobs:
            csz = min(Ls, 512)
            for ch in range(Ls // csz):
                chunks.append((z0 + ch * csz, csz, KB_, t0 + ch * (csz // 128), wc0, tc0 + ch * (csz // 128), sg0 + ch * (csz // 128)))

        def s1(i):
            c0, csz = chunks[i][0], chunks[i][1]
            ps = self.next_ps()
            k.op("pe", [w1.b, zT.b], [ps.b], lambda: nc.tensor.matmul(
                ps[0:64, 0:csz], w1[:, :], zT[:, c0:c0 + csz], start=True, stop=True))
            ta = tA[i % 2]
            k.op("dve", [ps.b, self.colv.b, fb.b], [ta.b], lambda: nc.vector.tensor_scalar(
                out=ta[:, 0:csz], in0=ps[0:64, 0:csz], scalar1=self.colv[0:64, COLV["f_f1"]:COLV["f_f1"] + 1],
                scalar2=fb[:, 0:1], op0=ALU.mult, op1=ALU.add))
            self._range_reduce(ta, tnA[i % 2], csz)
            k.op("act", [ta.b], [h1[i % 2].b], lambda: nc.scalar.activation(
                out=h1[i % 2][:, 0:csz], in_=ta[:, 0:csz], func=AF.Sin))

        def s2(i):
            csz = chunks[i][1]
            ps2 = self.next_ps()
            k.op("pe", [w2.b, h1[i % 2].b], [ps2.b], lambda: nc.tensor.matmul(
                ps2[0:64, 0:csz], w2[:, :], h1[i % 2][:, 0:csz], start=True, stop=True))
            tb = tB[i % 2]
            k.op("dve", [ps2.b, self.colv.b, fb.b], [tb.b], lambda: nc.vector.tensor_scalar(
                out=tb[:, 0:csz], in0=ps2[0:64, 0:csz], scalar1=self.colv[0:64, COLV["f_f2"]:COLV["f_f2"] + 1],
                scalar2=fb[:, 1:2], op0=ALU.mult, op1=ALU.add))
            self._range_reduce(tb, tnB[i % 2], csz)
            k.op("act", [tb.b], [h2[i % 2].b], lambda: nc.scalar.activation(
                out=h2[i % 2][:, 0:csz], in_=tb[:, 0:csz], func=AF.Sin))

        dcnt = [0]

        def s3(i):
            (c0, csz, KB_, tile0, wc0, tcb, sgb) = chunks[i]
            hh = h2[i % 2]
            for tt in range(csz // 128):
                dc = dec[dcnt[0] % 2]
                dcnt[0] += 1
                k.op("act", [delta.b, tcol.b], [dc.b], lambda dc=dc, tt=tt: nc.scalar.activation(
                    out=dc[:], in_=delta[:], func=AF.Exp, scale=tcol[:, tcb + tt:tcb + tt + 1]))
                for cc_ in range(2):
                    ps3 = self.next_ps()
                    k.op("pe", [hh.b, w3.b], [ps3.b], lambda ps3=ps3, tt=tt, cc_=cc_: nc.tensor.matmul(
                        ps3[:, :], hh[:, tt * 128:(tt + 1) * 128], w3[:, wc0 + cc_ * 512:wc0 + (cc_ + 1) * 512],
                        start=True, stop=True))
                    kb = KB_.bufs[tile0 + tt]
                    k.op("dve", [ps3.b, dc.b, tcol.b], [kb], lambda ps3=ps3, dc=dc, tt=tt, cc_=cc_:
                         nc.vector.scalar_tensor_tensor(out=KB_[:, tile0 + tt, cc_ * 512:(cc_ + 1) * 512], in0=ps3[:, :],
                                                        scalar=tcol[:, sgb + tt:sgb + tt + 1],
                                                        in1=dc[:, cc_ * 512:(cc_ + 1) * 512], op0=ALU.mult, op1=ALU.mult))

        nch = len(chunks)
        for i in range(nch + 2):
            if i < nch:
                s1(i)
            if 0 <= i - 1 < nch:
                s2(i - 1)
            if 0 <= i - 2 < nch:
                s3(i - 2)
            tick()
            tick()
        k.release(mC)
        if "K2" in self.taps:
            k.dma("sp", self.tap("K2", [128, 32 * D], BF16), K2[:].rearrange("p a b -> p (a b)"), K2.bufs, [], "tap")
            k.dma("sp", self.tap("K2c", [128, 4 * D], BF16), K2c[:].rearrange("p a b -> p (a b)"), K2c.bufs, [], "tap")
        FK = [k.sb(f"FK{i}", [128, 32, 128], BF16) for i in range(2)]
        KFs = [k.sb(f"KFs{i}", [128, D], F32) for i in range(2)]
        for (nfc, ntile, src, KB_, dst, nm) in ((32, 32, io["fkt"], K2, self.KF, "m"), (4, 4, io["fktc"], K2c, self.KFC, "c")):
            for fc in range(nfc):
                fk = FK[fc % 2]
                k.dma("sp", fk[:, 0:ntile, :], src[fc], [], [fk.b], f"FK{fc % 2}")
                kf = KFs[fc % 2]
                for cc_ in range(2):
                    ps = self.next_ps()

                    def mmk(ps=ps, fk=fk, KB_=KB_, cc_=cc_, ntile=ntile):
                        last = None
                        for nt in range(ntile):
                            last = nc.tensor.matmul(ps[:, :], fk[:, nt, :], KB_[:, nt, cc_ * 512:(cc_ + 1) * 512],
                                                    start=(nt == 0), stop=(nt == ntile - 1))
                        return last
                    k.op("pe", [fk.b] + KB_.bufs, [ps.b], mmk)
                    k.op("act", [ps.b], [kf.b], lambda ps=ps, kf=kf, cc_=cc_: nc.scalar.copy(
                        out=kf[:, cc_ * 512:(cc_ + 1) * 512], in_=ps[:, :]))
                k.dma("act", dst[fc], kf[:], [kf.b], [dst.bufs[fc]], f"st_KF{fc % 2}")
                tick()
        if "KF" in self.taps:
            k.dma("sp", self.tap("KF", [32, 128, D]), self.KF[:], [self.KF.b], [], "tap")
            k.dma("sp", self.tap("KFC", [4, 128, D]), self.KFC[:], [self.KFC.b], [], "tap")
        if inter is not None:
            for _ in inter:
                pass
        k.release(m0)

    def psb(self, ps):
        return ps.h[:, :].bitcast(BF16)

    def alloc_norm_tmp(self, nslots=2):
        k = self.k
        self.ntmp = []
        for i in range(nslots):
            self.ntmp.append(dict(
                junk=k.sb(f"njunk{i}", [128, D], BF16), ss=k.sb(f"nss{i}", [128, 2], F32),
                rstd=k.sb(f"nrstd{i}", [128, 1], F32), xh=k.sb(f"nxh{i}", [128, D], BF16)))
        self.ntmp_i = 0

    def norm_stats(self, xt):
        k, nc = self.k, self.k.nc
        tm = self.ntmp[self.ntmp_i % len(self.ntmp)]
        self.ntmp_i += 1
        junk, ss, rstd, xh = tm["junk"], tm["ss"], tm["rstd"], tm["xh"]
        k.op("act", [xt.b], [junk.b, ss.b], lambda: nc.scalar.activation(
            out=junk[:], in_=xt[:], func=AF.Square, accum_out=ss[:, 0:1]))
        k.op("act", [ss.b, self.epsc.b], [rstd.b], lambda: nc.scalar.activation(
            out=rstd[:], in_=ss[:, 0:1], func=AF.Sqrt, scale=1.0 / D, bias=self.epsc[:, 0:1]))
        k.op("dve", [rstd.b], [rstd.b], lambda: nc.vector.reciprocal(out=rstd[:], in_=rstd[:]))
        k.op("dve", [xt.b, rstd.b], [xh.b], lambda: nc.vector.tensor_scalar(
            out=xh[:], in0=xt[:], scalar1=rstd[:, 0:1], scalar2=None, op0=ALU.mult))
        return tm

    def norm_transpose(self, tm, A, S, dst_fn, dst_bufs):
        k, nc = self.k, self.k.nc
        xh = tm["xh"]
        for half in range(2):
            ps = self.next_ps()
            pb = self.psb(ps)

            def tr(pb=pb, half=half):
                last = None
                for q in range(4):
                    kk = half * 4 + q
                    last = nc.tensor.transpose(pb[:, q * 128:(q + 1) * 128], xh[:, kk * 128:(kk + 1) * 128], self.ident[:])
                return last
            k.op("pe", [xh.b, self.ident.b], [ps.b], tr)
            for q in range(4):
                kk = half * 4 + q
                if half == 0:
                    k.op("act", [ps.b, self.AS.b], dst_bufs[0:1], lambda kk=kk, q=q, pb=pb: nc.scalar.activation(
                        out=dst_fn(kk), in_=pb[:, q * 128:(q + 1) * 128], func=AF.Identity,
                        scale=A[:, kk:kk + 1], bias=S[:, kk:kk + 1]))
                else:
                    k.op("dve", [ps.b, self.AS.b], dst_bufs[1:2], lambda kk=kk, q=q, pb=pb: nc.vector.tensor_scalar(
                        out=dst_fn(kk), in0=pb[:, q * 128:(q + 1) * 128], scalar1=A[:, kk:kk + 1], scalar2=S[:, kk:kk + 1],
                        op0=ALU.mult, op1=ALU.add))

    def norm_T(self, xt, A, S, dst_fn, dst_bufs):
        tm = self.norm_stats(xt)
        self.norm_transpose(tm, A, S, dst_fn, dst_bufs)

    def alloc_post_tmp(self):
        k = self.k
        self.ptmp = [dict(ss=k.sb(f"pss{i}", [128, 2], F32), rstd=k.sb(f"prstd{i}", [128, 1], F32),
                          junk=k.sb(f"pjunk{i}", [128, 512], BF16), tmp=k.sb(f"ptmp{i}", [128, 512], F32)) for i in range(2)]
        self.ptmp_i = 0

    def post_residual(self, ps2, xt, GG):
        k, nc = self.k, self.k.nc
        tm = self.ptmp[self.ptmp_i % 2]
        self.ptmp_i += 1
        ss, rstd, junk, tmp = tm["ss"], tm["rstd"], tm["junk"], tm["tmp"]
        for h in range(2):
            k.op("act", [ps2[h].b], [junk.b, ss.b], lambda h=h: nc.scalar.activation(
                out=junk[:], in_=ps2[h][:, :], func=AF.Square, accum_out=ss[:, h:h + 1]))
        k.op("dve", [ss.b], [rstd.b], lambda: nc.vector.tensor_tensor(out=rstd[:], in0=ss[:, 0:1], in1=ss[:, 1:2], op=ALU.add))
        k.op("act", [rstd.b, self.epsc.b], [rstd.b], lambda: nc.scalar.activation(
            out=rstd[:], in_=rstd[:], func=AF.Sqrt, scale=1.0 / D, bias=self.epsc[:, 0:1]))
        k.op("dve", [rstd.b], [rstd.b], lambda: nc.vector.reciprocal(out=rstd[:], in_=rstd[:]))
        for h in range(2):
            sl = slice(h * 512, (h + 1) * 512)
            k.op("dve", [ps2[h].b, rstd.b, GG.b], [tmp.b], lambda h=h, sl=sl: nc.vector.scalar_tensor_tensor(
                out=tmp[:], in0=ps2[h][:, :], scalar=rstd[:, 0:1], in1=GG[:, sl], op0=ALU.mult, op1=ALU.mult))
            k.op("dve", [tmp.b, xt.b], [xt.b], lambda sl=sl: nc.vector.tensor_tensor(
                out=xt[:, sl], in0=xt[:, sl], in1=tmp[:], op=ALU.add))

    def tok_src(self, b, tt, layer0):
        if layer0:
            if tt < NTX:
                return self.io["x"][b, tt * 128:(tt + 1) * 128, :]
            return self.io["ctx"][b, (tt - NTX) * 128:(tt - NTX + 1) * 128, :]
        return self.XS[b, tt * 128:(tt + 1) * 128, :]

    def phase_hyena(self, b):
        k, nc, io = self.k, self.k.nc, self.io
        mB = k.mark()
        U_off = k.sb_off
        U = k.sb("U", [128, NT, D], BF16, nsub=NT)
        mA = k.mark()
        hxT = k.sb("hxT", [128, 8, T], BF16, nsub=2 * NT)
        self.alloc_norm_tmp(3)
        xts = [k.sb(f"xt{i}", [128, D], F32) for i in range(2)]
        tms = {}
        for tt in range(NT + 1):
            if tt < NT:
                xt = xts[tt % 2]
                k.dma("sp", xt[:], self.tok_src(b, tt, True), [], [xt.b], f"xt{tt % 2}")
                tms[tt] = self.norm_stats(xt)
            if tt >= 1:
                t1 = tt - 1
                r = b if t1 < NTX else 2
                self.norm_transpose(tms.pop(t1), self.AS[:, 0, 0, 0, r, :], self.AS[:, 0, 0, 1, r, :],
                                    lambda kk, t1=t1: hxT[:, kk, t1 * 128:(t1 + 1) * 128], hxT.bufs[2 * t1:2 * t1 + 2])
        ZW = 2 + L + 2 + C
        zpad = [k.sb(f"zpad{s_}", [128, ZW], F32) for s_ in range(3)]
        for z in zpad:
            k.op("dve", [], [z.b], lambda z=z: nc.vector.memset(z[:], 0.0))
        acc = [k.sb(f"cacc{s_}", [128, T], F32) for s_ in range(3)]
        x1s = [k.sb(f"x1s{i}", [128, T], BF16) for i in range(2)]
        uTs = [k.sb(f"uTs{i}", [128, T], BF16) for i in range(2)]
        wj = [[k.sb(f"win{i}_{s_}", [128, 8, 128], BF16) for s_ in range(3)] for i in range(2)]
        chunks = [(1 + c * 512, c * 512, 512) for c in range(4)] + [(L + 3, L, C)]
        regions = [(0, 0, L), (L + 2, L, C)]
        def u_transposes(j):
            us_ = uTs[j % 2]
            for g0 in range(0, NT, 8):
                ng = min(8, NT - g0)
                ps = self.next_ps()
                pb = self.psb(ps)

                def tru(pb=pb, g0=g0, ng=ng):
                    last = None
                    for i in range(ng):
                        last = nc.tensor.transpose(pb[:, i * 128:(i + 1) * 128], us_[:, (g0 + i) * 128:(g0 + i + 1) * 128], self.ident[:])
                    return last
                k.op("pe", [us_.b, self.ident.b], [ps.b], tru)
                k.op("act", [ps.b], U.bufs[g0:g0 + ng], lambda pb=pb, g0=g0, ng=ng, j=j: nc.scalar.copy(
                    out=U[:, g0:g0 + ng, j * 128:(j + 1) * 128], in_=pb[:, 0:ng * 128].rearrange("p (a c) -> p a c", c=128)))

        def load_win(j):
            for s_ in range(3):
                c0 = s_ * D + j * 128
                k.dma("pool", wj[j % 2][s_][:], io["hy_w_in"][:, c0:c0 + 128].rearrange("(k p) n -> p k n", p=128),
                      [], [wj[j % 2][s_].b], f"win{j % 2}_{s_}")

        load_win(0)
        for j in range(8):
            ws = wj[j % 2]
            if j + 1 < 8:
                load_win(j + 1)
            for s_ in range(3):
                col = s_ * 8 + j
                for (zc, t0, n) in chunks:
                    ps = self.next_ps()

                    def mmz(ps=ps, w=ws[s_], t0=t0, n=n):
                        last = None
                        for kk in range(8):
                            last = nc.tensor.matmul(ps[:, 0:n], w[:, kk, :], hxT[:, kk, t0:t0 + n], start=(kk == 0), stop=(kk == 7))
                        return last
                    rb = hxT.bufs[2 * (t0 // 128):2 * (t0 // 128 + n // 128)]
                    k.op("pe", [ws[s_].b] + rb, [ps.b], mmz)
                    k.op("act", [ps.b, self.colv.b], [zpad[s_].b], lambda ps=ps, s_=s_, zc=zc, n=n, col=col: nc.scalar.activation(
                        out=zpad[s_][:, zc:zc + n], in_=ps[:, 0:n], func=AF.Identity,
                        bias=self.cv("hy_b_in", 1, col), scale=1.0))
            xs_ = x1s[j % 2]
            us_ = uTs[j % 2]
            for s_ in range(3):
                col = s_ * 8 + j
                eng = "dve"
                E = nc.vector
                for (zb, tb, n) in regions:
                    k.op(eng, [zpad[s_].b, self.colv.b], [acc[s_].b], lambda E=E, s_=s_, zb=zb, tb=tb, n=n, col=col: E.tensor_scalar(
                        out=acc[s_][:, tb:tb + n], in0=zpad[s_][:, zb:zb + n], scalar1=self.cv("hy_cw0", 1, col),
                        scalar2=self.cv("hy_cb", 1, col), op0=ALU.mult, op1=ALU.add))
                    k.op(eng, [zpad[s_].b, self.colv.b, acc[s_].b], [acc[s_].b], lambda E=E, s_=s_, zb=zb, tb=tb, n=n, col=col: E.scalar_tensor_tensor(
                        out=acc[s_][:, tb:tb + n], in0=zpad[s_][:, zb + 1:zb + 1 + n], scalar=self.cv("hy_cw1", 1, col),
                        in1=acc[s_][:, tb:tb + n], op0=ALU.mult, op1=ALU.add))
                    if s_ == 0:
                        k.op(eng, [zpad[s_].b, self.colv.b, acc[s_].b], [xs_.b], lambda E=E, s_=s_, zb=zb, tb=tb, n=n, col=col: E.scalar_tensor_tensor(
                            out=xs_[:, tb:tb + n], in0=zpad[s_][:, zb + 2:zb + 2 + n], scalar=self.cv("hy_cw2", 1, col),
                            in1=acc[s_][:, tb:tb + n], op0=ALU.mult, op1=ALU.add))
                    else:
                        k.op(eng, [zpad[s_].b, self.colv.b, acc[s_].b], [acc[s_].b], lambda E=E, s_=s_, zb=zb, tb=tb, n=n, col=col: E.scalar_tensor_tensor(
                            out=acc[s_][:, tb:tb + n], in0=zpad[s_][:, zb + 2:zb + 2 + n], scalar=self.cv("hy_cw2", 1, col),
                            in1=acc[s_][:, tb:tb + n], op0=ALU.mult, op1=ALU.add))
            k.op("dve", [acc[1].b, acc[2].b], [us_.b], lambda: nc.vector.tensor_tensor(
                out=us_[:], in0=acc[2][:], in1=acc[1][:], op=ALU.mult))
            k.dma("pool", self.X1T[b, j], xs_[:], [xs_.b], [self.X1T.bufs[b * 8 + j]], f"st_X1T{j % 2}")
            k.dma("pool", self.UT[b, j], us_[:], [us_.b], [self.UT.bufs[b * 8 + j]], f"st_UT{j % 2}")
            if j >= 1:
                u_transposes(j - 1)
        u_transposes(7)
        if "U" in self.taps and b == 0:
            k.dma("sp", self.tap("U", [128, NT * D], BF16), U[:].rearrange("p a c -> p (a c)"), U.bufs, [], "tap")
            k.dma("sp", self.tap("X1T", [8, 128, T], BF16), self.X1T[0], [self.X1T.b], [], "tap")
        k.release(mA)
        if self.stop_after == "HA":
            return
        mTop = k.mark_top()
        P = k.sbt("P", [128, 32, D], BF16, nsub=32)
        Pc = k.sbt("Pc", [128, 4, D], BF16, nsub=4)
        mU = k.mark()
        Fr = [k.sb(f"Fr{i}", [128, 16, 128], BF16) for i in range(2)]
        Fi = [k.sb(f"Fi{i}", [128, 16, 128], BF16) for i in range(2)]
        kre = [k.sb(f"kre{i}", [128, D], F32) for i in range(2)]
        kim = [k.sb(f"kim{i}", [128, D], F32) for i in range(2)]
        t1 = k.sb("pw1", [128, 512], F32)
        t2 = k.sb("pw2", [128, 512], F32)
        jobs = [(16, 16, io["fkt"], self.KF, P, 0, 16, "m"), (2, 2, io["fktc"], self.KFC, Pc, 16, 2, "c")]
        it = 0
        for (nfh, nk, fsrc, kfsrc, Pd, ut0, imoff, nm) in jobs:
            for i in range(nfh):
                fr, fi_, kr, ki = Fr[it % 2], Fi[it % 2], kre[it % 2], kim[it % 2]
                it += 1
                k.dma("sp", fr[:, 0:nk, :], fsrc[i, :, 0:nk, :], [], [fr.b], f"Fr{it % 2}")
                k.dma("sp", fi_[:, 0:nk, :], fsrc[imoff + i, :, 0:nk, :], [], [fi_.b], f"Fi{it % 2}")
                k.dma("sp", kr[:], kfsrc[i], [kfsrc.bufs[i]], [kr.b], f"kre{it % 2}")
                k.dma("sp", ki[:], kfsrc[imoff + i], [kfsrc.bufs[imoff + i]], [ki.b], f"kim{it % 2}")
                for ch in range(2):
                    sl = slice(ch * 512, (ch + 1) * 512)
                    psr = self.next_ps()
                    psi = self.next_ps()
                    ub = U.bufs[ut0:ut0 + nk]
                    for (ps_, f_) in ((psr, fr), (psi, fi_)):
                        def mmf(ps_=ps_, f_=f_, sl=sl, nk=nk, ut0=ut0):
                            last = None
                            for kk in range(nk):
                                last = nc.tensor.matmul(ps_[:, :], f_[:, kk, :], U[:, ut0 + kk, sl], start=(kk == 0), stop=(kk == nk - 1))
                            return last
                        k.op("pe", [f_.b] + ub, [ps_.b], mmf)
                    k.op("dve", [psr.b, kr.b], [t1.b], lambda psr=psr, kr=kr, sl=sl: nc.vector.tensor_tensor(
                        out=t1[:], in0=psr[:, :], in1=kr[:, sl], op=ALU.mult))
                    k.op("dve", [psi.b, ki.b], [t2.b], lambda psi=psi, ki=ki, sl=sl: nc.vector.tensor_tensor(
                        out=t2[:], in0=psi[:, :], in1=ki[:, sl], op=ALU.mult))
                    k.op("dve", [t1.b, t2.b], [Pd.bufs[i]], lambda Pd=Pd, i=i, sl=sl: nc.vector.tensor_tensor(
                        out=Pd[:, i, sl], in0=t1[:], in1=t2[:], op=ALU.subtract))
                    k.op("dve", [psr.b, ki.b], [t1.b], lambda psr=psr, ki=ki, sl=sl: nc.vector.tensor_tensor(
                        out=t1[:], in0=psr[:, :], in1=ki[:, sl], op=ALU.mult))
                    k.op("dve", [psi.b, kr.b], [t2.b], lambda psi=psi, kr=kr, sl=sl: nc.vector.tensor_tensor(
                        out=t2[:], in0=psi[:, :], in1=kr[:, sl], op=ALU.mult))
                    k.op("dve", [t1.b, t2.b], [Pd.bufs[imoff + i]], lambda Pd=Pd, i=i, imoff=imoff, sl=sl: nc.vector.tensor_tensor(
                        out=Pd[:, imoff + i, sl], in0=t1[:], in1=t2[:], op=ALU.add))
        k.release(mU)
        gT = k.sb_at("gatedT", [128, 8, T], BF16, U_off, alias_of=[U], nsub=NT)
        mI = k.mark()
        G = [k.sb(f"G{i}", [128, 32, 256], BF16) for i in range(2)]
        x1t = [k.sb(f"x1t{i}", [128, 8, 256], BF16) for i in range(2)]
        utt = [k.sb(f"utt{i}", [128, 8, 256], BF16) for i in range(2)]
        vt = [k.sb(f"vt{i}", [128, 256], F32) for i in range(2)]
        for tc in range(9):
            g, xa, ua = G[tc % 2], x1t[tc % 2], utt[tc % 2]
            if tc < 8:
                k.dma("sp", g[:], io["gt"][tc], [], [g.b], f"G{tc % 2}")
                nfc, Pd = 32, P
            else:
                k.dma("sp", g[:, 0:4, :], io["gtc"][0], [], [g.b], f"G{tc % 2}")
                nfc, Pd = 4, Pc
            tsl = slice(tc * 256, (tc + 1) * 256)
            k.dma("sp", xa[:], self.X1T[b, :, :, tsl].rearrange("j p t -> p j t"), self.X1T.bufs[b * 8:b * 8 + 8], [xa.b], f"x1t{tc % 2}")
            k.dma("sp", ua[:], self.UT[b, :, :, tsl].rearrange("j p t -> p j t"), self.UT.bufs[b * 8:b * 8 + 8], [ua.b], f"utt{tc % 2}")
            for j in range(8):
                ps = self.next_ps()

                def mmi(ps=ps, g=g, Pd=Pd, nfc=nfc, j=j):
                    last = None
                    for fc in range(nfc):
                        last = nc.tensor.matmul(ps[:, 0:256], Pd[:, fc, j * 128:(j + 1) * 128], g[:, fc, :], start=(fc == 0), stop=(fc == nfc - 1))
                    return last
                k.op("pe", [g.b] + Pd.bufs, [ps.b], mmi)
                v_ = vt[j % 2]
                k.op("dve", [ps.b, ua.b, self.colv.b], [v_.b], lambda ps=ps, ua=ua, v_=v_, j=j: nc.vector.scalar_tensor_tensor(
                    out=v_[:], in0=ua[:, j, :], scalar=self.cv("hy_fbias", 1, j), in1=ps[:, 0:256], op0=ALU.mult, op1=ALU.add))
                k.op("dve", [v_.b, xa.b], [gT.bufs[2 * tc], gT.bufs[2 * tc + 1]], lambda v_=v_, xa=xa, j=j, tsl=tsl: nc.vector.tensor_tensor(
                    out=gT[:, j, tsl], in0=v_[:], in1=xa[:, j, :], op=ALU.mult))
        if "gT" in self.taps and b == 0:
            k.dma("sp", self.tap("gT", [128, 8 * T], BF16), gT[:].rearrange("p a c -> p (a c)"), gT.bufs, [], "tap")
        k.release(mI)
        k.release_top(mTop)
        wo = k.sb("hy_wo", [128, 8, D], BF16)
        k.dma("pool", wo[:], io["hy_w_out"].rearrange("(k p) n -> p k n", p=128), [], [wo.b], "hy_wo")
        bo = k.sb("hy_bo", [1, D], BF16)
        k.dma("pool", bo[:], io["rows"][ROW_HY_B_OUT:ROW_HY_B_OUT + 1, :], [], [bo.b], "hy_bo")
        GGx = k.sb("GGx", [128, D], F32)
        GGc = k.sb("GGc", [128, D], F32)
        k.dma("sp", GGx[:], self.GGD[0, 0, b], [self.GGD.bufs[b]], [GGx.b], "GGx")
        k.dma("sp", GGc[:], self.GGD[0, 0, 2], [self.GGD.bufs[2]], [GGc.b], "GGc")
        self.alloc_post_tmp()
        xts = [k.sb(f"xo{i}", [128, D], F32) for i in range(3)]
        for tt in range(NT):
            xt = xts[tt % 3]
            k.dma("sp", xt[:], self.tok_src(b, tt, True), [], [xt.b], f"xo{tt % 3}")
            ps2 = [self.next_ps(), self.next_ps()]
            for h in range(2):
                def mmo(h=h, ps=ps2[h], tt=tt):
                    for j in range(8):
                        nc.tensor.matmul(ps[:, :], gT[:, j, tt * 128:(tt + 1) * 128], wo[:, j, h * 512:(h + 1) * 512], start=(j == 0), stop=False)
                    return nc.tensor.matmul(ps[:, :], self.ones[0:1, :], bo[:, h * 512:(h + 1) * 512], start=False, stop=True)
                k.op("pe", [gT.bufs[tt], wo.b, bo.b, self.ones.b], [ps2[h].b], mmo)
            self.post_residual(ps2, xt, GGx if tt < NTX else GGc)
            k.dma("pool", self.XS[b, tt * 128:(tt + 1) * 128, :], xt[:], [xt.b], [self.XS.bufs[b * NT + tt]], f"st_xo{tt % 3}")
        k.release(mB)

    def phase_mlp(self, l, final):
        k, nc, io = self.k, self.k.nc, self.io
        m0 = k.mark()
        mTop = k.mark_top()
        w1c, w2c = [], []
        for c in range(8):
            a = k.sbt(f"w1c{c}", [128, 8, 512], BF16)
            k.dma("pool", a[:], io["mlp_w1"][l, :, c * 512:(c + 1) * 512].rearrange("(k p) n -> p k n", p=128),
                  [], [a.b], f"w1c{c}")
            w1c.append(a)
            a2 = k.sbt(f"w2c{c}", [128, 4, D], BF16)
            k.dma("pool", a2[:], io["mlp_w2"][l, c * 512:(c + 1) * 512, :].rearrange("(f p) n -> p f n", p=128),
                  [], [a2.b], f"w2c{c}")
            w2c.append(a2)
        GG = []
        for r in range(3):
            g = k.sb(f"GGm{r}", [128, D], F32)
            k.dma("sp", g[:], self.GGD[l, 1, r], [self.GGD.bufs[(l * 2 + 1) * 3 + r]], [g.b], f"GGm{r}")
            GG.append(g)
        self.alloc_norm_tmp(2)
        self.alloc_post_tmp()
        self.pjunk = k.sb("pjunk", [128, D], BF16)
        ntt = NTX if final else NT
        tiles = [(b, tt) for b in range(NB) for tt in range(ntt)]
        blocks = [tiles[i:i + 2] for i in range(0, len(tiles), 2)]
        xts = [k.sb(f"xm{i}", [128, D], F32) for i in range(4)]
        hxT = [k.sb(f"hxTm{i}", [128, 8, 256], BF16, nsub=4) for i in range(2)]
        hT = [k.sb(f"hTm{i}", [128, 256], BF16) for i in range(4)]
        rl = [k.sb(f"rlm{i}", [128, 256], F32) for i in range(2)]
        ysb = [k.sb(f"ysb{i}", [128, D], F32) for i in range(4)]
        out_banks = self.ps[0:4]
        self.ps_pool = self.ps[4:8]
        xi = 0
        blk_x = {}

        def load_block(n):
            nonlocal xi
            xs = []
            for (b, tt) in blocks[n]:
                xt = xts[xi % 4]
                k.dma("sp", xt[:], self.tok_src(b, tt, False), [self.XS.bufs[b * NT + tt]], [xt.b], f"xm{xi % 4}")
                xi += 1
                xs.append(xt)
            blk_x[n] = xs

        blk_tm = {}

        def stats_block(n):
            blk_tm[n] = [self.norm_stats(blk_x[n][i]) for i in range(len(blocks[n]))]

        def tr_block(n):
            hx = hxT[n % 2]
            for i, (b, tt) in enumerate(blocks[n]):
                r = b if tt < NTX else 2
                self.norm_transpose(blk_tm[n][i], self.AS[:, l, 1, 0, r, :], self.AS[:, l, 1, 1, r, :],
                                    lambda kk, hx=hx, i=i: hx[:, kk, i * 128:(i + 1) * 128], hx.bufs[2 * i:2 * i + 2])

        def norm_block(n):
            stats_block(n)
            tr_block(n)

        def post_tile(n, i):
            b, tt = blocks[n][i]
            y = ysb[(2 * n + i) % 4]
            xt = blk_x[n][i]
            r = b if tt < NTX else 2
            self.post_residual_sb(y, xt, GG[r])
            if final:
                k.dma("pool", self.out[b, tt * 128:(tt + 1) * 128, :], xt[:], [xt.b], [], f"st_xm{(2 * n + i) % 4}")
            else:
                k.dma("pool", self.XS[b, tt * 128:(tt + 1) * 128, :], xt[:], [xt.b], [self.XS.bufs[b * NT + tt]], f"st_xm{(2 * n + i) % 4}")

        load_block(0)
        load_block(1)
        norm_block(0)
        SK = 3
        f_ctr = 0
        for n in range(len(blocks)):
            hx = hxT[n % 2]
            for f in range(32 + SK):
                if f < 32:
                    ps1 = self.next_ps()
                    h_ = hT[f_ctr % 4]
                    r_ = rl[f_ctr % 2]
                    f_ctr += 1

                    def mm1(ps1=ps1, f=f, hx=hx):
                        last = None
                        for kk in range(8):
                            last = nc.tensor.matmul(ps1[:, 0:256], w1c[f // 4][:, kk, (f % 4) * 128:(f % 4 + 1) * 128], hx[:, kk, :], start=(kk == 0), stop=(kk == 7))
                        return last
                    k.op("pe", [w1c[f // 4].b] + hx.bufs, [ps1.b], mm1)
                    k.op("act", [ps1.b], [r_.b], lambda ps1=ps1, r_=r_: nc.scalar.activation(out=r_[:], in_=ps1[:, 0:256], func=AF.Relu))
                    k.op("dve", [r_.b], [h_.b], lambda r_=r_, h_=h_: nc.vector.tensor_tensor(out=h_[:], in0=r_[:], in1=r_[:], op=ALU.mult))
                if f >= SK:
                    f2 = f - SK
                    h2 = hT[(f_ctr - (min(f, 31) - f2) - 1) % 4] if False else None
                    h2 = hT[(n * 32 + f2) % 4]

                    def mm2(h2=h2, f2=f2):
                        last = None
                        for i in range(2):
                            for hf in range(2):
                                last = nc.tensor.matmul(out_banks[2 * i + hf][:, :], h2[:, i * 128:(i + 1) * 128],
                                                        w2c[f2 // 4][:, f2 % 4, hf * 512:(hf + 1) * 512], start=(f2 == 0), stop=(f2 == 31))
                        return last
                    k.op("pe", [h2.b, w2c[f2 // 4].b], [ob.b for ob in out_banks], mm2)
                if n >= 1 and f in (3, 7):
                    post_tile(n - 1, 0 if f == 3 else 1)
                if f == 8 and n + 1 < len(blocks) and n >= 1:
                    load_block(n + 1)
                if f == 12 and n + 1 < len(blocks):
                    stats_block(n + 1)
                if f == 26 and n + 1 < len(blocks):
                    tr_block(n + 1)
            for i, (b, tt) in enumerate(blocks[n]):
                y = ysb[(2 * n + i) % 4]
                for hf in range(2):
                    k.op("act", [out_banks[2 * i + hf].b], [y.b], lambda y=y, i=i, hf=hf: nc.scalar.copy(
                        out=y[:, hf * 512:(hf + 1) * 512], in_=out_banks[2 * i + hf][:, :]))
        post_tile(len(blocks) - 1, 0)
        post_tile(len(blocks) - 1, 1)
        self.ps_pool = self.ps
        k.release(m0)
        k.release_top(mTop)

    def post_residual_sb(self, y, xt, GG):
        k, nc = self.k, self.k.nc
        tm = self.ptmp[self.ptmp_i % 2]
        self.ptmp_i += 1
        ss, rstd = tm["ss"], tm["rstd"]
        junk = self.pjunk
        k.op("act", [y.b], [junk.b, ss.b], lambda: nc.scalar.activation(
            out=junk[:], in_=y[:], func=AF.Square, accum_out=ss[:, 0:1]))
        k.op("act", [ss.b, self.epsc.b], [rstd.b], lambda: nc.scalar.activation(
            out=rstd[:], in_=ss[:, 0:1], func=AF.Sqrt, scale=1.0 / D, bias=self.epsc[:, 0:1]))
        k.op("dve", [rstd.b], [rstd.b], lambda: nc.vector.reciprocal(out=rstd[:], in_=rstd[:]))
        k.op("dve", [y.b, rstd.b, GG.b], [y.b], lambda: nc.vector.scalar_tensor_tensor(
            out=y[:], in0=y[:], scalar=rstd[:, 0:1], in1=GG[:], op0=ALU.mult, op1=ALU.mult))
        k.op("dve", [y.b, xt.b], [xt.b], lambda: nc.vector.tensor_tensor(out=xt[:], in0=xt[:], in1=y[:], op=ALU.add))

    def phase_attn(self, b):
        k, nc, io = self.k, self.k.nc, self.io
        l = 1
        m0 = k.mark()
        QT = k.sb("QT", [128, NH, L], BF16, nsub=NTX)
        KT = k.sb("KT", [128, NKV, T], BF16, nsub=NT)
        V = k.sb("V", [128, NT, NKV, HD], BF16, nsub=NT)
        mA = k.mark()
        hxT = k.sb("hxTa", [128, 8, T], BF16, nsub=2 * NT)
        wqkv = k.sb("wqkv", [128, 8, 1536], BF16)
        k.dma("pool", wqkv[:], io["attn_w_qkv"].rearrange("(k p) n -> p k n", p=128), [], [wqkv.b], "wqkv")
        ropec = k.sb("ropec", [128, NTX, 64], F32)
        ropes = k.sb("ropes", [128, NTX, 64], F32)
        k.dma("sp", ropec[:], io["ropec"], [], [ropec.b], "c_grp2")
        k.dma("sp", ropes[:], io["ropes"], [], [ropes.b], "c_grp2")
        qg = k.sb("qg", [128, HD], F32)
        kg = k.sb("kg", [128, HD], F32)
        k.dma("sp", qg[:], io["rows"][ROW_QN:ROW_QN + 1, 0:HD].partition_broadcast(128), [], [qg.b], "c_grp2")
        k.dma("sp", kg[:], io["rows"][ROW_KN:ROW_KN + 1, 0:HD].partition_broadcast(128), [], [kg.b], "c_grp2")
        k.group_done("c_grp2", [ropec, ropes, qg, kg])
        self.alloc_norm_tmp(3)
        xts = [k.sb(f"xa{i}", [128, D], F32) for i in range(3)]
        sq = [k.sb(f"sq{i}", [128, 10, HD], F32) for i in range(2)]
        qn = [k.sb(f"qn{i}", [128, 10, HD], F32) for i in range(2)]
        qr = [k.sb(f"qr{i}", [128, 10, HD], BF16) for i in range(2)]
        ss10 = [k.sb(f"ss10_{i}", [128, 10], F32) for i in range(2)]
        ra = [k.sb(f"ra{i}", [128, 10, 64], F32) for i in range(2)]
        rb_ = [k.sb(f"rb{i}", [128, 10, 64], F32) for i in range(2)]
        rc_ = [k.sb(f"rc{i}", [128, 10, 64], F32) for i in range(2)]
        rd_ = [k.sb(f"rd{i}", [128, 10, 64], F32) for i in range(2)]
        POOL = nc.gpsimd

        def stA(tt):
            xt = xts[tt % 3]
            k.dma("sp", xt[:], self.tok_src(b, tt, False), [self.XS.bufs[b * NT + tt]], [xt.b], f"xa{tt % 3}")
            return self.norm_stats(xt)

        def stB(tt, tm):
            r = b if tt < NTX else 2
            self.norm_transpose(tm, self.AS[:, l, 0, 0, r, :], self.AS[:, l, 0, 1, r, :],
                                lambda kk, tt=tt: hxT[:, kk, tt * 128:(tt + 1) * 128], hxT.bufs[2 * tt:2 * tt + 2])

        def stC(tt):
            isx = tt < NTX
            h0 = 0 if isx else 8
            banks = []
            if isx:
                for c in range(2):
                    ps = self.next_ps()

                    def mmq(ps=ps, c=c, tt=tt):
                        last = None
                        for kk in range(8):
                            last = nc.tensor.matmul(ps[:, :], hxT[:, kk, tt * 128:(tt + 1) * 128], wqkv[:, kk, c * 512:(c + 1) * 512],
                                                    start=(kk == 0), stop=(kk == 7))
                        return last
                    k.op("pe", hxT.bufs[2 * tt:2 * tt + 2] + [wqkv.b], [ps.b], mmq)
                    banks.append(ps)
            pskv = self.next_ps()

            def mmkv(ps=pskv, tt=tt):
                last = None
                for kk in range(8):
                    last = nc.tensor.matmul(ps[:, :], hxT[:, kk, tt * 128:(tt + 1) * 128], wqkv[:, kk, 1024:1536],
                                            start=(kk == 0), stop=(kk == 7))
                return last
            k.op("pe", hxT.bufs[2 * tt:2 * tt + 2] + [wqkv.b], [pskv.b], mmkv)
            sq_, qn_, qr_, ss_ = sq[tt % 2], qn[tt % 2], qr[tt % 2], ss10[tt % 2]
            ra_, rbb, rcc, rdd = ra[tt % 2], rb_[tt % 2], rc_[tt % 2], rd_[tt % 2]
            k.op("act", [pskv.b], [V.bufs[tt]], lambda pskv=pskv, tt=tt: nc.scalar.copy(
                out=V[:, tt, :, :], in_=pskv[:, 256:512].rearrange("p (g d) -> p g d", d=HD)))
            srcs = []
            if isx:
                srcs += [(banks[0], 0, 512, 0, 4), (banks[1], 0, 512, 4, 4)]
            srcs.append((pskv, 0, 256, 8, 2))
            for (ps, c0, ncol, hs, nhh) in srcs:
                k.op("act", [ps.b], [sq_.b], lambda ps=ps, c0=c0, ncol=ncol, hs=hs, nhh=nhh: nc.scalar.activation(
                    out=sq_[:, hs:hs + nhh, :], in_=ps[:, c0:c0 + ncol].rearrange("p (h d) -> p h d", d=HD), func=AF.Square))
            k.op("dve", [sq_.b], [ss_.b], lambda: nc.vector.tensor_reduce(
                out=ss_[:, h0:10], in_=sq_[:, h0:10, :], axis=AX.X, op=ALU.add))
            k.op("act", [ss_.b, self.epsc.b], [ss_.b], lambda: nc.scalar.activation(
                out=ss_[:, h0:10], in_=ss_[:, h0:10], func=AF.Sqrt, scale=1.0 / HD, bias=self.epsc[:, 0:1]))
            k.op("dve", [ss_.b], [ss_.b], lambda: nc.vector.reciprocal(out=ss_[:, h0:10], in_=ss_[:, h0:10]))
            if isx:
                for (ps, c0, ncol, hs, nhh) in srcs:
                    for hh in range(nhh):
                        gt_ = qg if hs + hh < 8 else kg
                        k.op("dve", [ps.b, ss_.b, gt_.b], [qn_.b], lambda ps=ps, c0=c0, hs=hs, hh=hh, gt_=gt_: nc.vector.scalar_tensor_tensor(
                            out=qn_[:, hs + hh, :], in0=ps[:, c0 + hh * HD:c0 + (hh + 1) * HD], scalar=ss_[:, hs + hh:hs + hh + 1],
                            in1=gt_[:, :], op0=ALU.mult, op1=ALU.mult))
                x0 = qn_[:, :, :].rearrange("p h (i two) -> p h i two", two=2)[:, :, :, 0]
                x1 = qn_[:, :, :].rearrange("p h (i two) -> p h i two", two=2)[:, :, :, 1]
                o0 = qr_[:, :, :].rearrange("p h (i two) -> p h i two", two=2)[:, :, :, 0]
                o1 = qr_[:, :, :].rearrange("p h (i two) -> p h i two", two=2)[:, :, :, 1]
                cosb = ropec[:, tt, :].unsqueeze(1).broadcast_to([128, 10, 64])
                sinb = ropes[:, tt, :].unsqueeze(1).broadcast_to([128, 10, 64])
                k.op("dve", [qn_.b, ropec.b], [ra_.b], lambda: nc.vector.tensor_tensor(out=ra_[:], in0=x0, in1=cosb, op=ALU.mult))
                k.op("dve", [qn_.b, ropes.b], [rbb.b], lambda: nc.vector.tensor_tensor(out=rbb[:], in0=x1, in1=sinb, op=ALU.mult))
                k.op("dve", [ra_.b, rbb.b], [qr_.b], lambda: nc.vector.tensor_tensor(out=o0, in0=ra_[:], in1=rbb[:], op=ALU.subtract))
                k.op("dve", [qn_.b, ropes.b], [rcc.b], lambda: nc.vector.tensor_tensor(out=rcc[:], in0=x0, in1=sinb, op=ALU.mult))
                k.op("dve", [qn_.b, ropec.b], [rdd.b], lambda: nc.vector.tensor_tensor(out=rdd[:], in0=x1, in1=cosb, op=ALU.mult))
                k.op("dve", [rcc.b, rdd.b], [qr_.b], lambda: nc.vector.tensor_tensor(out=o1, in0=rcc[:], in1=rdd[:], op=ALU.add))
            else:
                for hh in range(2):
                    k.op("dve", [pskv.b, ss_.b, kg.b], [qr_.b], lambda hh=hh: nc.vector.scalar_tensor_tensor(
                        out=qr_[:, 8 + hh, :], in0=pskv[:, hh * HD:(hh + 1) * HD], scalar=ss_[:, 8 + hh:9 + hh],
                        in1=kg[:, :], op0=ALU.mult, op1=ALU.mult))

        def stD(tt):
            isx = tt < NTX
            h0 = 0 if isx else 8
            qr_ = qr[tt % 2]
            ps = self.next_ps()
            pb = self.psb(ps)

            def trq(pb=pb, qr_=qr_, h0=h0):
                last = None
                for hh in range(h0, 8 if h0 == 0 else 10):
                    last = nc.tensor.transpose(pb[:, (hh - h0) * 128:(hh - h0 + 1) * 128], qr_[:, hh, :], self.ident[:])
                return last
            k.op("pe", [qr_.b, self.ident.b], [ps.b], trq)
            if isx:
                k.op("act", [ps.b], [QT.bufs[tt]], lambda pb=pb, tt=tt: nc.scalar.copy(
                    out=QT[:, :, tt * 128:(tt + 1) * 128], in_=pb[:, 0:1024].rearrange("p (h t) -> p h t", t=128)))
                ps2 = self.next_ps()
                pb2 = self.psb(ps2)
                k.op("pe", [qr_.b, self.ident.b], [ps2.b], lambda pb2=pb2, qr_=qr_: [nc.tensor.transpose(
                    pb2[:, g * 128:(g + 1) * 128], qr_[:, 8 + g, :], self.ident[:]) for g in range(2)][-1])
                k.op("act", [ps2.b], [KT.bufs[tt]], lambda pb2=pb2, tt=tt: nc.scalar.copy(
                    out=KT[:, :, tt * 128:(tt + 1) * 128], in_=pb2[:, 0:256].rearrange("p (g t) -> p g t", t=128)))
            else:
                k.op("act", [ps.b], [KT.bufs[tt]], lambda pb=pb, tt=tt: nc.scalar.copy(
                    out=KT[:, :, tt * 128:(tt + 1) * 128], in_=pb[:, 0:256].rearrange("p (g t) -> p g t", t=128)))

        tms = {}
        for i in range(NT + 3):
            if i < NT:
                tms[i] = stA(i)
            if 0 <= i - 1 < NT:
                stB(i - 1, tms.pop(i - 1))
            if 0 <= i - 2 < NT:
                stC(i - 2)
            if 0 <= i - 3 < NT:
                stD(i - 3)
        if "QT" in self.taps and b == 0:
            k.dma("sp", self.tap("QT", [128, NH * L], BF16), QT[:].rearrange("p a c -> p (a c)"), QT.bufs, [], "tap")
            k.dma("sp", self.tap("KT", [128, NKV * T], BF16), KT[:].rearrange("p a c -> p (a c)"), KT.bufs, [], "tap")
            k.dma("sp", self.tap("V", [128, NT * NKV * HD], BF16), V[:].rearrange("p a c d -> p (a c d)"), V.bufs, [], "tap")
        k.release(mA)
        OT = k.sb("OT", [128, NH, L], BF16, nsub=NTX)
        PT = [k.sb(f"PT{i}", [128, 512], BF16) for i in range(4)]
        rec = [k.sb(f"rec{i}", [128, 512], F32) for i in range(2)]
        s_banks = self.ps[0:4]
        o_banks = self.ps[4:6]
        d_banks = self.ps[6:8]
        SCALE = float(HD) ** -0.5
        SK = 2
        it = 0
        sc = 0
        for g in range(NKV):
            for hq in range(NH // NKV):
                h = g * (NH // NKV) + hq
                for qc in range(L // 512):
                    qsl = slice(qc * 512, (qc + 1) * 512)
                    qbufs = QT.bufs[qc * 4:qc * 4 + 4]
                    pso = o_banks[it % 2]
                    psd = d_banks[it % 2]
                    base = sc
                    for kc in range(NT + SK):
                        if kc < NT:
                            pss = s_banks[sc % 4]
                            pt = PT[sc % 4]
                            sc += 1
                            k.op("pe", [KT.bufs[kc]] + qbufs, [pss.b], lambda pss=pss, kc=kc, g=g, h=h, qsl=qsl: nc.tensor.matmul(
                                pss[:, :], KT[:, g, kc * 128:(kc + 1) * 128], QT[:, h, qsl], start=True, stop=True))
                            k.op("act", [pss.b], [pt.b], lambda pss=pss, pt=pt: nc.scalar.activation(
                                out=pt[:], in_=pss[:, :], func=AF.Exp, scale=SCALE))
                        if kc >= SK:
                            k2_ = kc - SK
                            pt2 = PT[(base + k2_) % 4]

                            def mmpv(pt2=pt2, k2_=k2_, g=g, pso=pso, psd=psd):
                                nc.tensor.matmul(pso[:, :], V[:, k2_, g, :], pt2[:], start=(k2_ == 0), stop=(k2_ == NT - 1))
                                return nc.tensor.matmul(psd[:, :], self.ones[:, :], pt2[:], start=(k2_ == 0), stop=(k2_ == NT - 1))
                            k.op("pe", [pt2.b, V.bufs[k2_], self.ones.b], [pso.b, psd.b], mmpv)
                    rc = rec[it % 2]
                    k.op("dve", [psd.b], [rc.b], lambda psd=psd, rc=rc: nc.vector.reciprocal(out=rc[:], in_=psd[:, :]))
                    k.op("dve", [pso.b, rc.b], OT.bufs[qc * 4:qc * 4 + 4], lambda pso=pso, rc=rc, h=h, qsl=qsl: nc.vector.tensor_tensor(
                        out=OT[:, h, qsl], in0=pso[:, :], in1=rc[:], op=ALU.mult))
                    it += 1
        if "OT" in self.taps and b == 0:
            k.dma("sp", self.tap("OT", [128, NH * L], BF16), OT[:].rearrange("p a c -> p (a c)"), OT.bufs, [], "tap")
        wo = k.sb("at_wo", [128, NH, D], BF16)
        k.dma("pool", wo[:], io["attn_w_o"].rearrange("(h p) n -> p h n", p=128), [], [wo.b], "at_wo")
        GGx = k.sb("GGa", [128, D], F32)
        k.dma("sp", GGx[:], self.GGD[l, 0, b], [self.GGD.bufs[(l * 2) * 3 + b]], [GGx.b], "GGa")
        self.alloc_post_tmp()
        xts = [k.sb(f"xao{i}", [128, D], F32) for i in range(3)]
        for tt in range(NTX):
            xt = xts[tt % 3]
            k.dma("sp", xt[:], self.tok_src(b, tt, False), [self.XS.bufs[b * NT + tt]], [xt.b], f"xao{tt % 3}")
            ps2 = [self.next_ps(), self.next_ps()]
            for hf in range(2):
                def mmo(hf=hf, ps=ps2[hf], tt=tt):
                    last = None
                    for h in range(NH):
                        last = nc.tensor.matmul(ps[:, :], OT[:, h, tt * 128:(tt + 1) * 128], wo[:, h, hf * 512:(hf + 1) * 512],
                                                start=(h == 0), stop=(h == NH - 1))
                    return last
                k.op("pe", [OT.bufs[tt], wo.b], [ps2[hf].b], mmo)
            self.post_residual(ps2, xt, GGx)
            k.dma("pool", self.XS[b, tt * 128:(tt + 1) * 128, :], xt[:], [xt.b], [self.XS.bufs[b * NT + tt]], f"st_xao{tt % 3}")
        k.release(m0)

    def finish(self):
        self.k.final_wait()
        return self.k.nc


def make_in_maps(inp, ncores=8):
    cc = get_consts()
    f32 = np.float32
    shared = {
        "mod_w": np.ascontiguousarray(inp["mod_w"], f32),
        "colv": build_colv(inp),
        "rows": build_rows(inp, cc),
        "mlp_w1": np.ascontiguousarray(inp["mlp_w1"], f32),
        "mlp_w2": np.ascontiguousarray(inp["mlp_w2"], f32),
        "hy_w_in": np.ascontiguousarray(inp["hy_w_in"][0], f32),
        "hy_w_out": np.ascontiguousarray(inp["hy_w_out"][0], f32),
        "attn_w_qkv": np.ascontiguousarray(inp["attn_w_qkv"][0], f32),
        "attn_w_o": np.ascontiguousarray(inp["attn_w_o"][0], f32),
        "f_w1": np.ascontiguousarray(inp["hy_filt_w1"][0], f32),
        "f_w2": np.ascontiguousarray(inp["hy_filt_w2"][0], f32),
        "f_w3": np.ascontiguousarray(inp["hy_filt_w3"][0], f32),
        "fkt": cc["fkt"], "gt": cc["gt"], "fktc": cc["fktc"], "gtc": cc["gtc"],
        "zT": cc["zT"], "tcol": cc["tcol"], "ident": cc["ident"], "sel3": cc["sel3"],
        "ropec": cc["ropec"], "ropes": cc["ropes"],
    }
    maps = []
    for i in range(ncores):
        b0 = i * NB
        cT = np.stack([inp["c"][b0], inp["c"][b0 + 1], inp["c_ctx"]], axis=-1)
        cT = np.ascontiguousarray(cT.reshape(8, 128, 3).transpose(1, 0, 2), f32)
        m = dict(shared)
        m["x"] = np.ascontiguousarray(inp["x"][b0:b0 + NB], f32)
        m["ctx"] = np.ascontiguousarray(inp["ctx"][b0:b0 + NB], f32)
        m["cT"] = cT
        maps.append(m)
    return maps


_PROG_CACHE = {}


def build_program():
    p = Prog()
    p.load_consts()
    p.phase_filter(p.phase_mod_gen())
    for b in range(NB):
        p.phase_hyena(b)
    p.phase_mlp(0, False)
    for b in range(NB):
        p.phase_attn(b)
    p.phase_mlp(1, True)
    return p.finish()


def kernel(**inputs):
    inp = {k_: np.asarray(v) for k_, v in inputs.items()}
    n_cores = 8
    maps = make_in_maps(inp, n_cores)
    nc = build_program()
    res = run_bass_kernel_spmd(nc, maps, core_ids=list(range(n_cores)))
    outs = [np.asarray(res.results[i]["out"], dtype=np.float32) for i in range(n_cores)]
    return np.concatenate(outs, axis=0)
```

```python
import math
import numpy as np
import ml_dtypes
import concourse.bass as bass
import concourse.mybir as mybir
from concourse.bass_utils import run_bass_kernel_spmd

F32 = mybir.dt.float32
BF16 = mybir.dt.bfloat16
AF = mybir.ActivationFunctionType
ALU = mybir.AluOpType
AX = mybir.AxisListType

D = 1024
L = 2048
C = 256
T = L + C
NB = 2
DFF = 4096
EPS = 1e-6
NT = T // 128
NTX = L // 128
HD = 128
NH = 8
NKV = 2
N_DFT = 2 * L
N_DFTC = 2 * C


class Buf:
    __slots__ = ("name", "w", "r")

    def __init__(self, name):
        self.name = name
        self.w = {}
        self.r = []


class Tile:
    def __init__(self, h, name, nsub=1):
        self.h = h
        self.name = name
        self.bufs = [Buf(f"{name}.{i}") for i in range(nsub)]

    def __getitem__(self, idx):
        return self.h[idx]

    @property
    def b(self):
        return self.bufs[0]


class KB:
    def __init__(self):
        nc = self.nc = bass.Bass("TRN2", target_bir_lowering=False)
        self.E = {"pe": nc.tensor, "act": nc.scalar, "dve": nc.vector, "pool": nc.gpsimd, "sp": nc.sync}
        self.sem = {e: nc.alloc_semaphore("s_" + e) for e in ("pe", "act", "dve", "pool")}
        self.tick = {e: 0 for e in self.sem}
        self.seen = {e: {} for e in self.E}
        self.dsem = {}
        self.all_tokens = {}
        self.sb_off = (nc.sbuf_base + 31) // 32 * 32
        self.sb_top = nc.sbuf_top // 32 * 32
        self.top_off = self.sb_top
        self.rel = []
        self.sb_peak = 0
        self.n_names = 0
        self.dram = {}

    def _inherit(self, lo, hi, tile):
        toks = {}
        for (a, b_, tl) in self.rel:
            if a < hi and lo < b_:
                for tok in tl:
                    if tok[0] not in toks or toks[tok[0]][1] < tok[1]:
                        toks[tok[0]] = tok
        if toks:
            for buf in tile.bufs:
                buf.r = list(toks.values())

    @staticmethod
    def _per(shape, dtype):
        per = 4 if dtype == F32 else 2
        for s_ in shape[1:]:
            per *= s_
        return (per + 31) // 32 * 32

    def sb(self, name, shape, dtype, nsub=1):
        per = self._per(shape, dtype)
        off = self.sb_off
        assert off + per <= self.top_off, f"SBUF overflow allocating {name}: {off}+{per} > {self.top_off}"
        self.n_names += 1
        h = self.nc.alloc_sbuf_tensor_at(f"{name}_{self.n_names}", list(shape), dtype, offset=off)
        self.sb_off = off + per
        self.sb_peak = max(self.sb_peak, self.sb_off + (self.sb_top - self.top_off))
        t = Tile(h, name, nsub)
        self._inherit(off, off + per, t)
        return t

    def sbt(self, name, shape, dtype, nsub=1):
        per = self._per(shape, dtype)
        off = self.top_off - per
        assert off >= self.sb_off, f"SBUF overflow (top) allocating {name}: {off} < {self.sb_off}"
        self.n_names += 1
        h = self.nc.alloc_sbuf_tensor_at(f"{name}_{self.n_names}", list(shape), dtype, offset=off)
        self.top_off = off
        self.sb_peak = max(self.sb_peak, self.sb_off + (self.sb_top - self.top_off))
        t = Tile(h, name, nsub)
        self._inherit(off, off + per, t)
        return t

    def sb_at(self, name, shape, dtype, off, alias_of=(), nsub=1):
        self.n_names += 1
        h = self.nc.alloc_sbuf_tensor_at(f"{name}_{self.n_names}", list(shape), dtype, offset=off)
        t = Tile(h, name, nsub)
        toks = {}
        for a in alias_of:
            for buf in a.bufs:
                for tok in list(buf.w.values()) + list(buf.r):
                    if tok[0] not in toks or toks[tok[0]][1] < tok[1]:
                        toks[tok[0]] = tok
        for buf in t.bufs:
            buf.r = list(toks.values())
        return t

    def mark(self):
        return self.sb_off

    def release(self, mark):
        if self.sb_off > mark:
            self.rel.append((mark, self.sb_off, list(self.all_tokens.values())))
        self.sb_off = mark

    def mark_top(self):
        return self.top_off

    def release_top(self, mark):
        if mark > self.top_off:
            self.rel.append((self.top_off, mark, list(self.all_tokens.values())))
        self.top_off = mark

    def dram_t(self, name, shape, dtype, kind="Internal", nsub=1):
        h = self.nc.dram_tensor(name, list(shape), dtype, kind=kind)
        t = Tile(h.ap(), name, nsub)
        self.dram[name] = t
        return t

    def _collect(self, eng, reads, writes, is_dma):
        need = {}

        def add(tok, raw):
            if tok is None:
                return
            s, v, src = tok
            if (not is_dma) and src == eng and not raw and eng == "pe":
                return
            if need.get(s, 0) < v:
                need[s] = v

        for b in reads:
            for t in b.w.values():
                add(t, True)
        for b in writes:
            for t in b.w.values():
                add(t, False)
            for t in b.r:
                add(t, False)
        seen = self.seen[eng]
        out = []
        for s, v in need.items():
            if seen.get(s, 0) >= v:
                continue
            seen[s] = v
            out.append((s, v))
        return out

    def _emit_waits(self, eng, waits):
        e = self.E[eng]
        for s, v in waits:
            e.wait_ge(s, v)

    def _finish(self, tok, reads, writes):
        for b in writes:
            b.w[tok[0]] = tok
            b.r = []
        for b in reads:
            b.r.append(tok)
            if len(b.r) > 64:
                mx = {}
                for (s, v, src) in b.r:
                    if s not in mx or mx[s][1] < v:
                        mx[s] = (s, v, src)
                b.r = list(mx.values())
        self.all_tokens[tok[0]] = tok

    def op(self, eng, reads, writes, fn):
        waits = self._collect(eng, reads, writes, False)
        self._emit_waits(eng, waits)
        last = fn()
        self.tick[eng] += 1
        last.then_inc(self.sem[eng], 1)
        tok = (self.sem[eng], self.tick[eng], eng)
        self._finish(tok, reads, writes)
        return tok

    def dma(self, q, out_ap, in_ap, reads, writes, key):
        waits = self._collect(q, reads, writes, True)
        self._emit_waits(q, waits)
        if key not in self.dsem:
            self.dsem[key] = [self.nc.alloc_semaphore("d_" + key), 0]
        ds = self.dsem[key]
        ins = self.E[q].dma_start(out=out_ap, in_=in_ap)
        ds[1] += 16
        ins.then_inc(ds[0], 16)
        tok = (ds[0], ds[1], "dma")
        self._finish(tok, reads, writes)
        return tok

    def group_done(self, key, tiles):
        ds = self.dsem[key]
        tok = (ds[0], ds[1], "dma")
        for t in tiles:
            for b in t.bufs:
                b.w[ds[0]] = tok

    def barrier(self):
        toks = list(self.all_tokens.values())
        for eng in self.E:
            seen = self.seen[eng]
            for (s, v, src) in toks:
                if seen.get(s, 0) >= v:
                    continue
                seen[s] = v
                self.E[eng].wait_ge(s, v)

    def final_wait(self):
        toks = list(self.all_tokens.values())
        seen = self.seen["sp"]
        for (s, v, src) in toks:
            if seen.get(s, 0) >= v:
                continue
            seen[s] = v
            self.E["sp"].wait_ge(s, v)


_CONST_CACHE = {}


def _bf(a):
    return np.ascontiguousarray(a.astype(ml_dtypes.bfloat16))


def _dft_consts(Ls):
    N = 2 * Ls
    half = N // 2
    n = np.arange(N, dtype=np.float64)
    f = np.arange(half, dtype=np.float64) + 0.5
    k2 = (np.arange(half, dtype=np.int64) * 2 + 1)[:, None] * np.arange(N, dtype=np.int64)[None, :]
    k2 = k2 % (2 * N)
    th = k2.astype(np.float64) * (math.pi / N)
    Fre = np.cos(th)
    Fim = -np.sin(th)
    Fall = np.concatenate([Fre, Fim], axis=0)
    nfc = N // 128
    ntile = N // 128
    FKT = Fall.reshape(nfc, 128, ntile, 128).transpose(0, 3, 2, 1)
    Gre = (2.0 / N) * np.cos(th[:, :Ls])
    Gim = -(2.0 / N) * np.sin(th[:, :Ls])
    Gall = np.concatenate([Gre, Gim], axis=0)
    tch = min(Ls, 256)
    ntch = Ls // tch
    GT = Gall.reshape(nfc, 128, ntch, tch).transpose(2, 1, 0, 3)
    return _bf(FKT), _bf(GT)


def _filter_pos(Ls):
    t = np.arange(Ls, dtype=np.float32) / np.float32(Ls)
    bands = np.linspace(1e-4, 16 - 1, 16, dtype=np.float32)
    ang = (np.float32(2.0 * math.pi) * t[:, None] * bands[None, :]).astype(np.float32)
    z = np.concatenate([t[:, None], np.cos(ang), np.sin(ang)], axis=-1).astype(np.float32)
    idx = (Ls - np.arange(Ls)) % Ls
    zr = z[idx]
    tr = t[idx]
    sgn_f = np.ones(Ls, np.float32)
    sgn_r = -np.ones(Ls, np.float32)
    sgn_r[0] = 0.0
    return z, zr, t, tr, sgn_f, sgn_r


def get_consts():
    if _CONST_CACHE:
        return _CONST_CACHE
    cc = {}
    cc["fkt"], cc["gt"] = _dft_consts(L)
    cc["fktc"], cc["gtc"] = _dft_consts(C)
    z, zr, t, tr, sf, sr = _filter_pos(L)
    zc, zrc, tc_, trc, sfc, src_ = _filter_pos(C)
    cc["zT"] = np.ascontiguousarray(np.concatenate([z.T, zr.T, zc.T, zrc.T], axis=1))
    def cols(v):
        return v.reshape(-1, 128).T
    cc["tcol"] = np.ascontiguousarray(np.concatenate(
        [cols(-t), cols(-tr), cols(-tc_), cols(-trc), cols(sf), cols(sr), cols(sfc), cols(src_)], axis=1)).astype(np.float32)
    deltas = np.abs(np.linspace(math.log(1e-2) / 1.5, math.log(1e-2) / 0.3, D, dtype=np.float32)).astype(np.float32)
    cc["delta"] = deltas.reshape(1, D)
    cc["ident"] = np.eye(128, dtype=np.float32)
    sel = np.zeros((3, 3, 128), np.float32)
    for r in range(3):
        sel[r, r, :] = 1.0
    cc["sel3"] = sel.transpose(1, 0, 2).copy()
    rows = L // 64
    row = np.repeat(np.arange(rows, dtype=np.float32), 64)
    col = np.tile(np.arange(64, dtype=np.float32), rows)
    inv = (np.float32(10000.0) ** (-np.arange(32, dtype=np.float32) / np.float32(32))).astype(np.float32)
    ang = np.concatenate([row[:, None] * inv[None, :], col[:, None] * inv[None, :]], axis=-1).astype(np.float32)
    cc["ropec"] = np.ascontiguousarray(np.cos(ang).astype(np.float32).reshape(NTX, 128, 64).transpose(1, 0, 2))
    cc["ropes"] = np.ascontiguousarray(np.sin(ang).astype(np.float32).reshape(NTX, 128, 64).transpose(1, 0, 2))
    _CONST_CACHE.update(cc)
    return cc


def _colv_layout():
    lay = {}
    off = 0

    def add(name, n):
        nonlocal off
        lay[name] = off
        off += n

    for l in range(2):
        add(f"mix_pre{l}", 8)
        add(f"mlp_pre{l}", 8)
    add("hy_b_in", 24)
    add("hy_cw0", 24)
    add("hy_cw1", 24)
    add("hy_cw2", 24)
    add("hy_cb", 24)
    add("hy_fbias", 8)
    add("f_b1", 1)
    add("f_f1", 1)
    add("f_b2", 1)
    add("f_f2", 1)
    for l in range(2):
        add(f"mod_b{l}", 48)
    lay["_n"] = off
    return lay


COLV = _colv_layout()


def _pcol(v):
    return np.asarray(v, np.float32).reshape(-1, 128).T


def build_colv(inp):
    a = np.zeros((128, COLV["_n"]), np.float32)

    def put(name, v):
        m = _pcol(v)
        a[:, COLV[name]:COLV[name] + m.shape[1]] = m

    for l in range(2):
        put(f"mix_pre{l}", inp["mix_norm_pre"][l])
        put(f"mlp_pre{l}", inp["mlp_norm_pre"][l])
        put(f"mod_b{l}", inp["mod_b"][l])
    put("hy_b_in", inp["hy_b_in"][0])
    for i in range(3):
        put(f"hy_cw{i}", inp["hy_conv_w"][0, i])
    put("hy_cb", inp["hy_conv_b"][0])
    put("hy_fbias", inp["hy_filt_bias"][0])
    for nm, key in (("f_b1", "hy_filt_b1"), ("f_f1", "hy_filt_freq1"), ("f_b2", "hy_filt_b2"), ("f_f2", "hy_filt_freq2")):
        a[:64, COLV[nm]] = inp[key][0]
    return a


ROW_MIX_POST = (0, 2)
ROW_MLP_POST = (1, 3)
ROW_HY_B_OUT = 4
ROW_DELTA = 5
ROW_QN = 6
ROW_KN = 7
ROW_MODB_G = ((8, 9), (10, 11))
N_ROWS = 12


def build_rows(inp, cc):
    r = np.zeros((N_ROWS, D), np.float32)
    r[0] = inp["mix_norm_post"][0]
    r[1] = inp["mlp_norm_post"][0]
    r[2] = inp["mix_norm_post"][1]
    r[3] = inp["mlp_norm_post"][1]
    r[4] = inp["hy_b_out"][0]
    r[5] = cc["delta"][0]
    r[6, :HD] = inp["attn_q_norm"][0]
    r[7, :HD] = inp["attn_k_norm"][0]
    for l in range(2):
        r[8 + 2 * l] = inp["mod_b"][l, 2048:3072]
        r[9 + 2 * l] = inp["mod_b"][l, 5120:6144]
    return r


INPUT_SPECS = [
    ("x", [NB, L, D], F32), ("ctx", [NB, C, D], F32), ("cT", [128, 8, 3], F32),
    ("mod_w", [2, D, 6 * D], F32), ("colv", [128, COLV["_n"]], F32), ("rows", [N_ROWS, D], F32),
    ("mlp_w1", [2, D, DFF], F32), ("mlp_w2", [2, DFF, D], F32),
    ("hy_w_in", [D, 3 * D], F32), ("hy_w_out", [D, D], F32),
    ("attn_w_qkv", [D, 1536], F32), ("attn_w_o", [D, D], F32),
    ("f_w1", [33, 64], F32), ("f_w2", [64, 64], F32), ("f_w3", [64, 2 * D], F32),
    ("fkt", [32, 128, 32, 128], BF16), ("gt", [8, 128, 32, 256], BF16),
    ("fktc", [4, 128, 4, 128], BF16), ("gtc", [1, 128, 4, 256], BF16),
    ("zT", [33, 2 * L + 2 * C], F32), ("tcol", [128, 72], F32),
    ("ident", [128, 128], F32), ("sel3", [3, 3, 128], F32),
    ("ropec", [128, NTX, 64], F32), ("ropes", [128, NTX, 64], F32),
]


class Prog:
    def __init__(self, taps=(), stop_after=None):
        self.k = k = KB()
        nc = k.nc
        self.taps = set(taps)
        self.stop_after = stop_after
        self.io = {}
        for name, shape, dt in INPUT_SPECS:
            self.io[name] = nc.dram_tensor(name, list(shape), dt, kind="ExternalInput").ap()
        self.out = nc.dram_tensor("out", [NB, L, D], F32, kind="ExternalOutput").ap()
        self.tap_aps = {}
        self.XS = k.dram_t("XS", [NB, T, D], F32, nsub=NB * NT)
        self.KF = k.dram_t("KF", [32, 128, D], F32, nsub=32)
        self.KFC = k.dram_t("KFC", [4, 128, D], F32, nsub=4)
        self.X1T = k.dram_t("X1T", [NB, 8, 128, T], BF16, nsub=NB * 8)
        self.UT = k.dram_t("UT", [NB, 8, 128, T], BF16, nsub=NB * 8)
        self.GGD = k.dram_t("GGD", [2, 2, 3, 128, D], F32, nsub=12)
        self.ps = [Tile(nc.alloc_psum_tensor(f"ps{i}", [128, 512], F32), f"ps{i}") for i in range(8)]
        self.ps_rr = 0
        self.ps_pool = self.ps

    def tap(self, name, shape, dtype=F32):
        ap = self.k.nc.dram_tensor("tap_" + name, list(shape), dtype, kind="ExternalOutput").ap()
        self.tap_aps[name] = ap
        return ap

    def next_ps(self):
        pool = self.ps_pool
        p = pool[self.ps_rr % len(pool)]
        self.ps_rr += 1
        return p

    def load_consts(self):
        k, nc, io = self.k, self.k.nc, self.io
        self.colv = k.sb("colv", [128, COLV["_n"]], F32)
        k.dma("sp", self.colv[:], io["colv"], [], [self.colv.b], "c_grp0")
        self.ident = k.sb("ident", [128, 128], BF16)
        k.dma("pool", self.ident[:], io["ident"], [], [self.ident.b], "c_ident")
        self.ones = k.sb("ones", [128, 128], BF16)
        k.op("dve", [], [self.ones.b], lambda: nc.vector.memset(self.ones[:], 1.0))
        self.AS = k.sb("AS", [128, 2, 2, 2, 3, 8], F32)
        self.epsc = k.sb("epsc", [128, 1], F32)
        k.op("dve", [], [self.epsc.b], lambda: nc.vector.memset(self.epsc[:], EPS))

    def cv(self, name, n=1, off=0):
        o = COLV[name] + off
        return self.colv[:, o:o + n]

    def phase_mod_gen(self):
        k, nc, io = self.k, self.k.nc, self.io
        mTop = k.mark_top()
        cT = k.sbt("cT", [128, 24], F32)
        scT = k.sbt("scT", [128, 8, 3], BF16)
        k.dma("sp", cT[:], io["cT"].rearrange("p k r -> p (k r)"), [], [cT.b], "c_cT")
        k.op("act", [cT.b], [scT.b], lambda: nc.scalar.activation(
            out=scT[:].rearrange("p k r -> p (k r)"), in_=cT[:], func=AF.Silu))
        sel = k.sbt("sel3", [3, 3, 128], F32)
        k.dma("sp", sel[:], io["sel3"], [], [sel.b], "c_sel")
        modF = k.sbt("modF", [128, 2, 48, 3], F32)
        tmpr = k.sbt("tmpr", [3, 512], F32)
        W = [k.sbt(f"modW{i}", [128, 8, 512], BF16) for i in range(2)]
        GGs = [k.sbt(f"GGs{i}", [128, D], F32) for i in range(2)]
        R3 = k.sbt("R3", [3, 4, D], F32)
        ggrow = k.sbt("ggrow", [3, 2, D], F32)
        wi = 0
        gi = 0
        for l in range(2):
            for i, r in enumerate((ROW_MIX_POST[l], ROW_MLP_POST[l], ROW_MODB_G[l][0], ROW_MODB_G[l][1])):
                k.dma("sp", R3[:, i, :], io["rows"][r:r + 1, :].partition_broadcast(3), [], [R3.b], "c_R3")
            for cb in range(12):
                blk = cb // 2
                w = W[wi % 2]
                wi += 1
                k.dma("pool", w[:], io["mod_w"][l, :, cb * 512:(cb + 1) * 512].rearrange("(k p) n -> p k n", p=128),
                      [], [w.b], f"modW{wi % 2}")
                ps = self.next_ps()
                if blk in (2, 5):
                    sub = 0 if blk == 2 else 1
                    half = cb % 2

                    def mm_row(ps=ps, w=w):
                        last = None
                        for kk in range(8):
                            last = nc.tensor.matmul(ps[0:3, :], scT[:, kk, :], w[:, kk, :], start=(kk == 0), stop=(kk == 7))
                        return last
                    k.op("pe", [scT.b, w.b], [ps.b], mm_row)
                    rb = 2 + sub
                    rg = sub
                    k.op("dve", [ps.b, R3.b], [tmpr.b], lambda ps=ps, rb=rb, half=half: nc.vector.tensor_tensor(
                        out=tmpr[:], in0=ps[0:3, :], in1=R3[:, rb, half * 512:(half + 1) * 512], op=ALU.add))
                    k.op("dve", [tmpr.b, R3.b], [ggrow.b], lambda sub=sub, rg=rg, half=half: nc.vector.tensor_tensor(
                        out=ggrow[:, sub, half * 512:(half + 1) * 512], in0=tmpr[:],
                        in1=R3[:, rg, half * 512:(half + 1) * 512], op=ALU.mult))
                else:
                    def mm_f(ps=ps, w=w):
                        last = None
                        for q in range(4):
                            for kk in range(8):
                                last = nc.tensor.matmul(ps[:, q * 3:(q + 1) * 3], w[:, kk, q * 128:(q + 1) * 128], scT[:, kk, :],
                                                        start=(kk == 0), stop=(kk == 7))
                        return last
                    k.op("pe", [scT.b, w.b], [ps.b], mm_f)
                    bcol = self.cv(f"mod_b{l}", 4, cb * 4).unsqueeze(2).broadcast_to([128, 4, 3])
                    k.op("dve", [ps.b, self.colv.b], [modF.b], lambda ps=ps, l=l, cb=cb, bcol=bcol: nc.vector.tensor_tensor(
                        out=modF[:, l, cb * 4:(cb + 1) * 4, :], in0=ps[:, 0:12].rearrange("p (q r) -> p q r", r=3),
                        in1=bcol, op=ALU.add))
                yield
            for sub in range(2):
                sh0 = 0 if sub == 0 else 24
                sc0 = 8 if sub == 0 else 32
                gname = (f"mix_pre{l}" if sub == 0 else f"mlp_pre{l}")
                for r in range(3):
                    k.op("dve", [modF.b, self.colv.b], [self.AS.b], lambda l=l, sub=sub, r=r, sc0=sc0, gname=gname:
                         nc.vector.scalar_tensor_tensor(out=self.AS[:, l, sub, 0, r, :], in0=modF[:, l, sc0:sc0 + 8, r], scalar=1.0,
                                                        in1=self.cv(gname, 8), op0=ALU.add, op1=ALU.mult))
                    k.op("dve", [modF.b], [self.AS.b], lambda l=l, sub=sub, r=r, sh0=sh0:
                         nc.vector.tensor_copy(out=self.AS[:, l, sub, 1, r, :], in_=modF[:, l, sh0:sh0 + 8, r]))
            for sub in range(2):
                for r in range(3):
                    g = GGs[gi % 2]
                    gi += 1
                    for half in range(2):
                        ps = self.next_ps()
                        k.op("pe", [sel.b, ggrow.b], [ps.b], lambda ps=ps, sub=sub, r=r, half=half: nc.tensor.matmul(
                            ps[:, :], sel[:, r, :], ggrow[:, sub, half * 512:(half + 1) * 512], start=True, stop=True))
                        k.op("act", [ps.b], [g.b], lambda ps=ps, g=g, half=half: nc.scalar.copy(
                            out=g[:, half * 512:(half + 1) * 512], in_=ps[:, :]))
                    k.dma("act", self.GGD[l, sub, r], g[:], [g.b], [self.GGD.bufs[(l * 2 + sub) * 3 + r]], f"st_GG{(gi - 1) % 2}")
                yield
        k.release_top(mTop)

    def phase_mod(self):
        for _ in self.phase_mod_gen():
            pass

    def _range_reduce(self, ta, tn, csz):
        k, nc = self.k, self.k.nc
        MAGIC = 12582912.0
        k.op("dve", [ta.b], [tn.b], lambda: nc.vector.tensor_scalar(
            out=tn[:, 0:csz], in0=ta[:, 0:csz], scalar1=1.0 / (2.0 * math.pi), scalar2=MAGIC, op0=ALU.mult, op1=ALU.add))
        k.op("dve", [tn.b], [tn.b], lambda: nc.vector.tensor_scalar(
            out=tn[:, 0:csz], in0=tn[:, 0:csz], scalar1=-MAGIC, scalar2=-2.0 * math.pi, op0=ALU.add, op1=ALU.mult))
        k.op("dve", [tn.b, ta.b], [ta.b], lambda: nc.vector.tensor_tensor(
            out=ta[:, 0:csz], in0=ta[:, 0:csz], in1=tn[:, 0:csz], op=ALU.add))

    def phase_filter(self, inter=None):
        k, nc, io = self.k, self.k.nc, self.io
        m0 = k.mark()

        def tick():
            if inter is not None:
                next(inter, None)
        K2 = k.sb("K2", [128, 32, D], BF16, nsub=32)
        K2c = k.sb("K2c", [128, 4, D], BF16, nsub=4)
        mC = k.mark()
        zT = k.sb("zT", [33, 2 * L + 2 * C], F32)
        k.dma("sp", zT[:], io["zT"], [], [zT.b], "c_grp1")
        w1 = k.sb("fw1", [33, 64], F32)
        w2 = k.sb("fw2", [64, 64], F32)
        w3 = k.sb("fw3", [64, 2 * D], F32)
        k.dma("sp", w1[:], io["f_w1"], [], [w1.b], "c_grp1")
        k.dma("sp", w2[:], io["f_w2"], [], [w2.b], "c_grp1")
        k.dma("sp", w3[:], io["f_w3"], [], [w3.b], "c_grp1")
        tcol = k.sb("tcol", [128, 72], F32)
        k.dma("sp", tcol[:], io["tcol"], [], [tcol.b], "c_grp1")
        delta = k.sb("delta", [128, D], F32)
        k.dma("sp", delta[:], io["rows"][ROW_DELTA:ROW_DELTA + 1, :].partition_broadcast(128), [], [delta.b], "c_grp1")
        k.group_done("c_grp1", [zT, w1, w2, w3, tcol, delta])
        fb = k.sb("fb", [64, 2], F32)
        k.op("dve", [self.colv.b], [fb.b], lambda: nc.vector.tensor_tensor(
            out=fb[:, 0:1], in0=self.colv[0:64, COLV["f_f1"]:COLV["f_f1"] + 1], in1=self.colv[0:64, COLV["f_b1"]:COLV["f_b1"] + 1], op=ALU.mult))
        k.op("dve", [self.colv.b], [fb.b], lambda: nc.vector.tensor_tensor(
            out=fb[:, 1:2], in0=self.colv[0:64, COLV["f_f2"]:COLV["f_f2"] + 1], in1=self.colv[0:64, COLV["f_b2"]:COLV["f_b2"] + 1], op=ALU.mult))
        h1 = [k.sb(f"h1T{i}", [64, 512], F32) for i in range(2)]
        h2 = [k.sb(f"h2T{i}", [64, 512], F32) for i in range(2)]
        tA = [k.sb(f"targA{i}", [64, 512], F32) for i in range(2)]
        tB = [k.sb(f"targB{i}", [64, 512], F32) for i in range(2)]
        tnA = [k.sb(f"tnA{i}", [64, 512], F32) for i in range(2)]
        tnB = [k.sb(f"tnB{i}", [64, 512], F32) for i in range(2)]
        dec = [k.sb(f"dec{i}", [128, D], F32) for i in range(2)]
        jobs = [(0, L, K2, 0, 0, 0, 36), (L, L, K2, 16, D, 16, 52),
                (2 * L, C, K2c, 0, 0, 32, 68), (2 * L + C, C, K2c, 2, D, 34, 70)]
        chunks = []
        for (z0, Ls, KB_, t0, wc0, tc0, sg0) in jobs:
            csz = min(Ls, 512)
            for ch in range(Ls // csz):
                chunks.append((z0 + ch * csz, csz, KB_, t0 + ch * (csz // 128), wc0, tc0 + ch * (csz // 128), sg0 + ch * (csz // 128)))

        def s1(i):
            c0, csz = chunks[i][0], chunks[i][1]
            ps = self.next_ps()
            k.op("pe", [w1.b, zT.b], [ps.b], lambda: nc.tensor.matmul(
                ps[0:64, 0:csz], w1[:, :], zT[:, c0:c0 + csz], start=True, stop=True))
            ta = tA[i % 2]
            k.op("dve", [ps.b, self.colv.b, fb.b], [ta.b], lambda: nc.vector.tensor_scalar(
                out=ta[:, 0:csz], in0=ps[0:64, 0:csz], scalar1=self.colv[0:64, COLV["f_f1"]:COLV["f_f1"] + 1],
                scalar2=fb[:, 0:1], op0=ALU.mult, op1=ALU.add))
            self._range_reduce(ta, tnA[i % 2], csz)
            k.op("act", [ta.b], [h1[i % 2].b], lambda: nc.scalar.activation(
                out=h1[i % 2][:, 0:csz], in_=ta[:, 0:csz], func=AF.Sin))

        def s2(i):
            csz = chunks[i][1]
            ps2 = self.next_ps()
            k.op("pe", [w2.b, h1[i % 2].b], [ps2.b], lambda: nc.tensor.matmul(
                ps2[0:64, 0:csz], w2[:, :], h1[i % 2][:, 0:csz], start=True, stop=True))
            tb = tB[i % 2]
            k.op("dve", [ps2.b, self.colv.b, fb.b], [tb.b], lambda: nc.vector.tensor_scalar(
                out=tb[:, 0:csz], in0=ps2[0:64, 0:csz], scalar1=self.colv[0:64, COLV["f_f2"]:COLV["f_f2"] + 1],
                scalar2=fb[:, 1:2], op0=ALU.mult, op1=ALU.add))
            self._range_reduce(tb, tnB[i % 2], csz)
            k.op("act", [tb.b], [h2[i % 2].b], lambda: nc.scalar.activation(
                out=h2[i % 2][:, 0:csz], in_=tb[:, 0:csz], func=AF.Sin))

        dcnt = [0]

        def s3(i):
            (c0, csz, KB_, tile0, wc0, tcb, sgb) = chunks[i]
            hh = h2[i % 2]
            for tt in range(csz // 128):
                dc = dec[dcnt[0] % 2]
                dcnt[0] += 1
                k.op("act", [delta.b, tcol.b], [dc.b], lambda dc=dc, tt=tt: nc.scalar.activation(
                    out=dc[:], in_=delta[:], func=AF.Exp, scale=tcol[:, tcb + tt:tcb + tt + 1]))
                for cc_ in range(2):
                    ps3 = self.next_ps()
                    k.op("pe", [hh.b, w3.b], [ps3.b], lambda ps3=ps3, tt=tt, cc_=cc_: nc.tensor.matmul(
                        ps3[:, :], hh[:, tt * 128:(tt + 1) * 128], w3[:, wc0 + cc_ * 512:wc0 + (cc_ + 1) * 512],
                        start=True, stop=True))
                    kb = KB_.bufs[tile0 + tt]
                    k.op("dve", [ps3.b, dc.b, tcol.b], [kb], lambda ps3=ps3, dc=dc, tt=tt, cc_=cc_:
                         nc.vector.scalar_tensor_tensor(out=KB_[:, tile0 + tt, cc_ * 512:(cc_ + 1) * 512], in0=ps3[:, :],
                                                        scalar=tcol[:, sgb + tt:sgb + tt + 1],
                                                        in1=dc[:, cc_ * 512:(cc_ + 1) * 512], op0=ALU.mult, op1=ALU.mult))

        nch = len(chunks)
        for i in range(nch + 2):
            if i < nch:
                s1(i)
            if 0 <= i - 1 < nch:
                s2(i - 1)
            if 0 <= i - 2 < nch:
                s3(i - 2)
            tick()
            tick()
        k.release(mC)
        if "K2" in self.taps:
            k.dma("sp", self.tap("K2", [128, 32 * D], BF16), K2[:].rearrange("p a b -> p (a b)"), K2.bufs, [], "tap")
            k.dma("sp", self.tap("K2c", [128, 4 * D], BF16), K2c[:].rearrange("p a b -> p (a b)"), K2c.bufs, [], "tap")
        FK = [k.sb(f"FK{i}", [128, 32, 128], BF16) for i in range(2)]
        KFs = [k.sb(f"KFs{i}", [128, D], F32) for i in range(2)]
        for (nfc, ntile, src, KB_, dst, nm) in ((32, 32, io["fkt"], K2, self.KF, "m"), (4, 4, io["fktc"], K2c, self.KFC, "c")):
            for fc in range(nfc):
                fk = FK[fc % 2]
                k.dma("sp", fk[:, 0:ntile, :], src[fc], [], [fk.b], f"FK{fc % 2}")
                kf = KFs[fc % 2]
                for cc_ in range(2):
                    ps = self.next_ps()

                    def mmk(ps=ps, fk=fk, KB_=KB_, cc_=cc_, ntile=ntile):
                        last = None
                        for nt in range(ntile):
                            last = nc.tensor.matmul(ps[:, :], fk[:, nt, :], KB_[:, nt, cc_ * 512:(cc_ + 1) * 512],
                                                    start=(nt == 0), stop=(nt == ntile - 1))
                        return last
                    k.op("pe", [fk.b] + KB_.bufs, [ps.b], mmk)
                    k.op("act", [ps.b], [kf.b], lambda ps=ps, kf=kf, cc_=cc_: nc.scalar.copy(
                        out=kf[:, cc_ * 512:(cc_ + 1) * 512], in_=ps[:, :]))
                k.dma("act", dst[fc], kf[:], [kf.b], [dst.bufs[fc]], f"st_KF{fc % 2}")
                tick()
        if "KF" in self.taps:
            k.dma("sp", self.tap("KF", [32, 128, D]), self.KF[:], [self.KF.b], [], "tap")
            k.dma("sp", self.tap("KFC", [4, 128, D]), self.KFC[:], [self.KFC.b], [], "tap")
        if inter is not None:
            for _ in inter:
                pass
        k.release(m0)

    def psb(self, ps):
        return ps.h[:, :].bitcast(BF16)

    def alloc_norm_tmp(self, nslots=2):
        k = self.k
        self.ntmp = []
        for i in range(nslots):
            self.ntmp.append(dict(
                junk=k.sb(f"njunk{i}", [128, D], BF16), ss=k.sb(f"nss{i}", [128, 2], F32),
                rstd=k.sb(f"nrstd{i}", [128, 1], F32), xh=k.sb(f"nxh{i}", [128, D], BF16)))
        self.ntmp_i = 0

    def norm_stats(self, xt):
        k, nc = self.k, self.k.nc
        tm = self.ntmp[self.ntmp_i % len(self.ntmp)]
        self.ntmp_i += 1
        junk, ss, rstd, xh = tm["junk"], tm["ss"], tm["rstd"], tm["xh"]
        k.op("act", [xt.b], [junk.b, ss.b], lambda: nc.scalar.activation(
            out=junk[:], in_=xt[:], func=AF.Square, accum_out=ss[:, 0:1]))
        k.op("act", [ss.b, self.epsc.b], [rstd.b], lambda: nc.scalar.activation(
            out=rstd[:], in_=ss[:, 0:1], func=AF.Sqrt, scale=1.0 / D, bias=self.epsc[:, 0:1]))
        k.op("dve", [rstd.b], [rstd.b], lambda: nc.vector.reciprocal(out=rstd[:], in_=rstd[:]))
        k.op("dve", [xt.b, rstd.b], [xh.b], lambda: nc.vector.tensor_scalar(
            out=xh[:], in0=xt[:], scalar1=rstd[:, 0:1], scalar2=None, op0=ALU.mult))
        return tm

    def norm_transpose(self, tm, A, S, dst_fn, dst_bufs):
        k, nc = self.k, self.k.nc
        xh = tm["xh"]
        for half in range(2):
            ps = self.next_ps()
            pb = self.psb(ps)

            def tr(pb=pb, half=half):
                last = None
                for q in range(4):
                    kk = half * 4 + q
                    last = nc.tensor.transpose(pb[:, q * 128:(q + 1) * 128], xh[:, kk * 128:(kk + 1) * 128], self.ident[:])
                return last
            k.op("pe", [xh.b, self.ident.b], [ps.b], tr)
            for q in range(4):
                kk = half * 4 + q
                if half == 0:
                    k.op("act", [ps.b, self.AS.b], dst_bufs[0:1], lambda kk=kk, q=q, pb=pb: nc.scalar.activation(
                        out=dst_fn(kk), in_=pb[:, q * 128:(q + 1) * 128], func=AF.Identity,
                        scale=A[:, kk:kk + 1], bias=S[:, kk:kk + 1]))
                else:
                    k.op("dve", [ps.b, self.AS.b], dst_bufs[1:2], lambda kk=kk, q=q, pb=pb: nc.vector.tensor_scalar(
                        out=dst_fn(kk), in0=pb[:, q * 128:(q + 1) * 128], scalar1=A[:, kk:kk + 1], scalar2=S[:, kk:kk + 1],
                        op0=ALU.mult, op1=ALU.add))

    def norm_T(self, xt, A, S, dst_fn, dst_bufs):
        tm = self.norm_stats(xt)
        self.norm_transpose(tm, A, S, dst_fn, dst_bufs)

    def alloc_post_tmp(self):
        k = self.k
        self.ptmp = [dict(ss=k.sb(f"pss{i}", [128, 2], F32), rstd=k.sb(f"prstd{i}", [128, 1], F32),
                          junk=k.sb(f"pjunk{i}", [128, 512], BF16), tmp=k.sb(f"ptmp{i}", [128, 512], F32)) for i in range(2)]
        self.ptmp_i = 0

    def post_residual(self, ps2, xt, GG):
        k, nc = self.k, self.k.nc
        tm = self.ptmp[self.ptmp_i % 2]
        self.ptmp_i += 1
        ss, rstd, junk, tmp = tm["ss"], tm["rstd"], tm["junk"], tm["tmp"]
        for h in range(2):
            k.op("act", [ps2[h].b], [junk.b, ss.b], lambda h=h: nc.scalar.activation(
                out=junk[:], in_=ps2[h][:, :], func=AF.Square, accum_out=ss[:, h:h + 1]))
        k.op("dve", [ss.b], [rstd.b], lambda: nc.vector.tensor_tensor(out=rstd[:], in0=ss[:, 0:1], in1=ss[:, 1:2], op=ALU.add))
        k.op("act", [rstd.b, self.epsc.b], [rstd.b], lambda: nc.scalar.activation(
            out=rstd[:], in_=rstd[:], func=AF.Sqrt, scale=1.0 / D, bias=self.epsc[:, 0:1]))
        k.op("dve", [rstd.b], [rstd.b], lambda: nc.vector.reciprocal(out=rstd[:], in_=rstd[:]))
        for h in range(2):
            sl = slice(h * 512, (h + 1) * 512)
            k.op("dve", [ps2[h].b, rstd.b, GG.b], [tmp.b], lambda h=h, sl=sl: nc.vector.scalar_tensor_tensor(
                out=tmp[:], in0=ps2[h][:, :], scalar=rstd[:, 0:1], in1=GG[:, sl], op0=ALU.mult, op1=ALU.mult))
            k.op("dve", [tmp.b, xt.b], [xt.b], lambda sl=sl: nc.vector.tensor_tensor(
                out=xt[:, sl], in0=xt[:, sl], in1=tmp[:], op=ALU.add))

    def tok_src(self, b, tt, layer0):
        if layer0:
            if tt < NTX:
                return self.io["x"][b, tt * 128:(tt + 1) * 128, :]
            return self.io["ctx"][b, (tt - NTX) * 128:(tt - NTX + 1) * 128, :]
        return self.XS[b, tt * 128:(tt + 1) * 128, :]

    def phase_hyena(self, b):
        k, nc, io = self.k, self.k.nc, self.io
        mB = k.mark()
        U_off = k.sb_off
        U = k.sb("U", [128, NT, D], BF16, nsub=NT)
        mA = k.mark()
        hxT = k.sb("hxT", [128, 8, T], BF16, nsub=2 * NT)
        self.alloc_norm_tmp(3)
        xts = [k.sb(f"xt{i}", [128, D], F32) for i in range(2)]
        tms = {}
        for tt in range(NT + 1):
            if tt < NT:
                xt = xts[tt % 2]
                k.dma("sp", xt[:], self.tok_src(b, tt, True), [], [xt.b], f"xt{tt % 2}")
                tms[tt] = self.norm_stats(xt)
            if tt >= 1:
                t1 = tt - 1
                r = b if t1 < NTX else 2
                self.norm_transpose(tms.pop(t1), self.AS[:, 0, 0, 0, r, :], self.AS[:, 0, 0, 1, r, :],
                                    lambda kk, t1=t1: hxT[:, kk, t1 * 128:(t1 + 1) * 128], hxT.bufs[2 * t1:2 * t1 + 2])
        ZW = 2 + L + 2 + C
        zpad = [k.sb(f"zpad{s_}", [128, ZW], F32) for s_ in range(3)]
        for z in zpad:
            k.op("dve", [], [z.b], lambda z=z: nc.vector.memset(z[:], 0.0))
        acc = [k.sb(f"cacc{s_}", [128, T], F32) for s_ in range(3)]
        x1s = [k.sb(f"x1s{i}", [128, T], BF16) for i in range(2)]
        uTs = [k.sb(f"uTs{i}", [128, T], BF16) for i in range(2)]
        wj = [[k.sb(f"win{i}_{s_}", [128, 8, 128], BF16) for s_ in range(3)] for i in range(2)]
        chunks = [(1 + c * 512, c * 512, 512) for c in range(4)] + [(L + 3, L, C)]
        regions = [(0, 0, L), (L + 2, L, C)]
        def u_transposes(j):
            us_ = uTs[j % 2]
            for g0 in range(0, NT, 8):
                ng = min(8, NT - g0)
                ps = self.next_ps()
                pb = self.psb(ps)

                def tru(pb=pb, g0=g0, ng=ng):
                    last = None
                    for i in range(ng):
                        last = nc.tensor.transpose(pb[:, i * 128:(i + 1) * 128], us_[:, (g0 + i) * 128:(g0 + i + 1) * 128], self.ident[:])
                    return last
                k.op("pe", [us_.b, self.ident.b], [ps.b], tru)
                k.op("act", [ps.b], U.bufs[g0:g0 + ng], lambda pb=pb, g0=g0, ng=ng, j=j: nc.scalar.copy(
                    out=U[:, g0:g0 + ng, j * 128:(j + 1) * 128], in_=pb[:, 0:ng * 128].rearrange("p (a c) -> p a c", c=128)))

        def load_win(j):
            for s_ in range(3):
                c0 = s_ * D + j * 128
                k.dma("pool", wj[j % 2][s_][:], io["hy_w_in"][:, c0:c0 + 128].rearrange("(k p) n -> p k n", p=128),
                      [], [wj[j % 2][s_].b], f"win{j % 2}_{s_}")

        load_win(0)
        for j in range(8):
            ws = wj[j % 2]
            if j + 1 < 8:
                load_win(j + 1)
            for s_ in range(3):
                col = s_ * 8 + j
                for (zc, t0, n) in chunks:
                    ps = self.next_ps()

                    def mmz(ps=ps, w=ws[s_], t0=t0, n=n):
                        last = None
                        for kk in range(8):
                            last = nc.tensor.matmul(ps[:, 0:n], w[:, kk, :], hxT[:, kk, t0:t0 + n], start=(kk == 0), stop=(kk == 7))
                        return last
                    rb = hxT.bufs[2 * (t0 // 128):2 * (t0 // 128 + n // 128)]
                    k.op("pe", [ws[s_].b] + rb, [ps.b], mmz)
                    k.op("act", [ps.b, self.colv.b], [zpad[s_].b], lambda ps=ps, s_=s_, zc=zc, n=n, col=col: nc.scalar.activation(
                        out=zpad[s_][:, zc:zc + n], in_=ps[:, 0:n], func=AF.Identity,
                        bias=self.cv("hy_b_in", 1, col), scale=1.0))
            xs_ = x1s[j % 2]
            us_ = uTs[j % 2]
            for s_ in range(3):
                col = s_ * 8 + j
                eng = "dve"
                E = nc.vector
                for (zb, tb, n) in regions:
                    k.op(eng, [zpad[s_].b, self.colv.b], [acc[s_].b], lambda E=E, s_=s_, zb=zb, tb=tb, n=n, col=col: E.tensor_scalar(
                        out=acc[s_][:, tb:tb + n], in0=zpad[s_][:, zb:zb + n], scalar1=self.cv("hy_cw0", 1, col),
                        scalar2=self.cv("hy_cb", 1, col), op0=ALU.mult, op1=ALU.add))
                    k.op(eng, [zpad[s_].b, self.colv.b, acc[s_].b], [acc[s_].b], lambda E=E, s_=s_, zb=zb, tb=tb, n=n, col=col: E.scalar_tensor_tensor(
                        out=acc[s_][:, tb:tb + n], in0=zpad[s_][:, zb + 1:zb + 1 + n], scalar=self.cv("hy_cw1", 1, col),
                        in1=acc[s_][:, tb:tb + n], op0=ALU.mult, op1=ALU.add))
                    if s_ == 0:
                        k.op(eng, [zpad[s_].b, self.colv.b, acc[s_].b], [xs_.b], lambda E=E, s_=s_, zb=zb, tb=tb, n=n, col=col: E.scalar_tensor_tensor(
                            out=xs_[:, tb:tb + n], in0=zpad[s_][:, zb + 2:zb + 2 + n], scalar=self.cv("hy_cw2", 1, col),
                            in1=acc[s_][:, tb:tb + n], op0=ALU.mult, op1=ALU.add))
                    else:
                        k.op(eng, [zpad[s_].b, self.colv.b, acc[s_].b], [acc[s_].b], lambda E=E, s_=s_, zb=zb, tb=tb, n=n, col=col: E.scalar_tensor_tensor(
                            out=acc[s_][:, tb:tb + n], in0=zpad[s_][:, zb + 2:zb + 2 + n], scalar=self.cv("hy_cw2", 1, col),
                            in1=acc[s_][:, tb:tb + n], op0=ALU.mult, op1=ALU.add))
            k.op("dve", [acc[1].b, acc[2].b], [us_.b], lambda: nc.vector.tensor_tensor(
                out=us_[:], in0=acc[2][:], in1=acc[1][:], op=ALU.mult))
            k.dma("pool", self.X1T[b, j], xs_[:], [xs_.b], [self.X1T.bufs[b * 8 + j]], f"st_X1T{j % 2}")
            k.dma("pool", self.UT[b, j], us_[:], [us_.b], [self.UT.bufs[b * 8 + j]], f"st_UT{j % 2}")
            if j >= 1:
                u_transposes(j - 1)
        u_transposes(7)
        if "U" in self.taps and b == 0:
            k.dma("sp", self.tap("U", [128, NT * D], BF16), U[:].rearrange("p a c -> p (a c)"), U.bufs, [], "tap")
            k.dma("sp", self.tap("X1T", [8, 128, T], BF16), self.X1T[0], [self.X1T.b], [], "tap")
        k.release(mA)
        if self.stop_after == "HA":
            return
        wo = k.sb("hy_wo", [128, 8, D], BF16)
        k.dma("pool", wo[:], io["hy_w_out"].rearrange("(k p) n -> p k n", p=128), [], [wo.b], "hy_wo")
        bo = k.sb("hy_bo", [1, D], BF16)
        k.dma("pool", bo[:], io["rows"][ROW_HY_B_OUT:ROW_HY_B_OUT + 1, :], [], [bo.b], "hy_bo")
        mI = k.mark()
        G = [k.sb(f"G{i}", [128, 32, 256], BF16) for i in range(2)]
        mTop = k.mark_top()
        P = k.sbt("P", [128, 32, D], BF16, nsub=32)
        Pc = k.sbt("Pc", [128, 4, D], BF16, nsub=4)
        mU = k.mark()
        Fr = [k.sb(f"Fr{i}", [128, 16, 128], BF16) for i in range(2)]
        Fi = [k.sb(f"Fi{i}", [128, 16, 128], BF16) for i in range(2)]
        kre = [k.sb(f"kre{i}", [128, D], F32) for i in range(2)]
        kim = [k.sb(f"kim{i}", [128, D], F32) for i in range(2)]
        t1 = k.sb("pw1", [128, 512], F32)
        t2 = k.sb("pw2", [128, 512], F32)
        jobs = [(16, 16, io["fkt"], self.KF, P, 0, 16, "m"), (2, 2, io["fktc"], self.KFC, Pc, 16, 2, "c")]
        it = 0
        for (nfh, nk, fsrc, kfsrc, Pd, ut0, imoff, nm) in jobs:
            for i in range(nfh):
                fr, fi_, kr, ki = Fr[it % 2], Fi[it % 2], kre[it % 2], kim[it % 2]
                it += 1
                k.dma("sp", fr[:, 0:nk, :], fsrc[i, :, 0:nk, :], [], [fr.b], f"Fr{it % 2}")
                k.dma("sp", fi_[:, 0:nk, :], fsrc[imoff + i, :, 0:nk, :], [], [fi_.b], f"Fi{it % 2}")
                k.dma("sp", kr[:], kfsrc[i], [kfsrc.bufs[i]], [kr.b], f"kre{it % 2}")
                k.dma("sp", ki[:], kfsrc[imoff + i], [kfsrc.bufs[imoff + i]], [ki.b], f"kim{it % 2}")
                for ch in range(2):
                    sl = slice(ch * 512, (ch + 1) * 512)
                    psr = self.next_ps()
                    psi = self.next_ps()
                    ub = U.bufs[ut0:ut0 + nk]
                    for (ps_, f_) in ((psr, fr), (psi, fi_)):
                        def mmf(ps_=ps_, f_=f_, sl=sl, nk=nk, ut0=ut0):
                            last = None
                            for kk in range(nk):
                                last = nc.tensor.matmul(ps_[:, :], f_[:, kk, :], U[:, ut0 + kk, sl], start=(kk == 0), stop=(kk == nk - 1))
                            return last
                        k.op("pe", [f_.b] + ub, [ps_.b], mmf)
                    k.op("dve", [psr.b, kr.b], [t1.b], lambda psr=psr, kr=kr, sl=sl: nc.vector.tensor_tensor(
                        out=t1[:], in0=psr[:, :], in1=kr[:, sl], op=ALU.mult))
                    k.op("dve", [psi.b, ki.b], [t2.b], lambda psi=psi, ki=ki, sl=sl: nc.vector.tensor_tensor(
                        out=t2[:], in0=psi[:, :], in1=ki[:, sl], op=ALU.mult))
                    k.op("dve", [t1.b, t2.b], [Pd.bufs[i]], lambda Pd=Pd, i=i, sl=sl: nc.vector.tensor_tensor(
                        out=Pd[:, i, sl], in0=t1[:], in1=t2[:], op=ALU.subtract))
                    k.op("dve", [psr.b, ki.b], [t1.b], lambda psr=psr, ki=ki, sl=sl: nc.vector.tensor_tensor(
                        out=t1[:], in0=psr[:, :], in1=ki[:, sl], op=ALU.mult))
                    k.op("dve", [psi.b, kr.b], [t2.b], lambda psi=psi, kr=kr, sl=sl: nc.vector.tensor_tensor(
                        out=t2[:], in0=psi[:, :], in1=kr[:, sl], op=ALU.mult))
                    k.op("dve", [t1.b, t2.b], [Pd.bufs[imoff + i]], lambda Pd=Pd, i=i, imoff=imoff, sl=sl: nc.vector.tensor_tensor(
                        out=Pd[:, imoff + i, sl], in0=t1[:], in1=t2[:], op=ALU.add))
        k.release(mU)
        gT = k.sb_at("gatedT", [128, 8, T], BF16, U_off, alias_of=[U], nsub=NT)
        x1t = [k.sb(f"x1t{i}", [128, 8, 256], BF16) for i in range(2)]
        utt = [k.sb(f"utt{i}", [128, 8, 256], BF16) for i in range(2)]
        vt = [k.sb(f"vt{i}", [128, 256], F32) for i in range(2)]
        for tc in range(9):
            g, xa, ua = G[tc % 2], x1t[tc % 2], utt[tc % 2]
            if tc < 8:
                k.dma("sp", g[:], io["gt"][tc], [], [g.b], f"G{tc % 2}")
                nfc, Pd = 32, P
            else:
                k.dma("sp", g[:, 0:4, :], io["gtc"][0], [], [g.b], f"G{tc % 2}")
                nfc, Pd = 4, Pc
            tsl = slice(tc * 256, (tc + 1) * 256)
            k.dma("sp", xa[:], self.X1T[b, :, :, tsl].rearrange("j p t -> p j t"), self.X1T.bufs[b * 8:b * 8 + 8], [xa.b], f"x1t{tc % 2}")
            k.dma("sp", ua[:], self.UT[b, :, :, tsl].rearrange("j p t -> p j t"), self.UT.bufs[b * 8:b * 8 + 8], [ua.b], f"utt{tc % 2}")
            for j in range(8):
                ps = self.next_ps()

                def mmi(ps=ps, g=g, Pd=Pd, nfc=nfc, j=j):
                    last = None
                    for fc in range(nfc):
                        last = nc.tensor.matmul(ps[:, 0:256], Pd[:, fc, j * 128:(j + 1) * 128], g[:, fc, :], start=(fc == 0), stop=(fc == nfc - 1))
                    return last
                k.op("pe", [g.b] + Pd.bufs, [ps.b], mmi)
                v_ = vt[j % 2]
                k.op("dve", [ps.b, ua.b, self.colv.b], [v_.b], lambda ps=ps, ua=ua, v_=v_, j=j: nc.vector.scalar_tensor_tensor(
                    out=v_[:], in0=ua[:, j, :], scalar=self.cv("hy_fbias", 1, j), in1=ps[:, 0:256], op0=ALU.mult, op1=ALU.add))
                k.op("dve", [v_.b, xa.b], [gT.bufs[2 * tc], gT.bufs[2 * tc + 1]], lambda v_=v_, xa=xa, j=j, tsl=tsl: nc.vector.tensor_tensor(
                    out=gT[:, j, tsl], in0=v_[:], in1=xa[:, j, :], op=ALU.mult))
        if "gT" in self.taps and b == 0:
            k.dma("sp", self.tap("gT", [128, 8 * T], BF16), gT[:].rearrange("p a c -> p (a c)"), gT.bufs, [], "tap")
        k.release(mI)
        k.release_top(mTop)
        GGx = k.sb("GGx", [128, D], F32)
        GGc = k.sb("GGc", [128, D], F32)
        k.dma("sp", GGx[:], self.GGD[0, 0, b], [self.GGD.bufs[b]], [GGx.b], "GGx")
        k.dma("sp", GGc[:], self.GGD[0, 0, 2], [self.GGD.bufs[2]], [GGc.b], "GGc")
        self.alloc_post_tmp()
        xts = [k.sb(f"xo{i}", [128, D], F32) for i in range(3)]
        for tt in range(NT):
            xt = xts[tt % 3]
            k.dma("sp", xt[:], self.tok_src(b, tt, True), [], [xt.b], f"xo{tt % 3}")
            ps2 = [self.next_ps(), self.next_ps()]
            for h in range(2):
                def mmo(h=h, ps=ps2[h], tt=tt):
                    for j in range(8):
                        nc.tensor.matmul(ps[:, :], gT[:, j, tt * 128:(tt + 1) * 128], wo[:, j, h * 512:(h + 1) * 512], start=(j == 0), stop=False)
                    return nc.tensor.matmul(ps[:, :], self.ones[0:1, :], bo[:, h * 512:(h + 1) * 512], start=False, stop=True)
                k.op("pe", [gT.bufs[tt], wo.b, bo.b, self.ones.b], [ps2[h].b], mmo)
            self.post_residual(ps2, xt, GGx if tt < NTX else GGc)
            k.dma("pool", self.XS[b, tt * 128:(tt + 1) * 128, :], xt[:], [xt.b], [self.XS.bufs[b * NT + tt]], f"st_xo{tt % 3}")
        k.release(mB)

    def phase_mlp(self, l, final):
        k, nc, io = self.k, self.k.nc, self.io
        m0 = k.mark()
        mTop = k.mark_top()
        w1c, w2c = [], []
        for c in range(8):
            a = k.sbt(f"w1c{c}", [128, 8, 512], BF16)
            k.dma("pool", a[:], io["mlp_w1"][l, :, c * 512:(c + 1) * 512].rearrange("(k p) n -> p k n", p=128),
                  [], [a.b], f"w1c{c}")
            w1c.append(a)
            a2 = k.sbt(f"w2c{c}", [128, 4, D], BF16)
            k.dma("pool", a2[:], io["mlp_w2"][l, c * 512:(c + 1) * 512, :].rearrange("(f p) n -> p f n", p=128),
                  [], [a2.b], f"w2c{c}")
            w2c.append(a2)
        GG = []
        for r in range(3):
            g = k.sb(f"GGm{r}", [128, D], F32)
            k.dma("sp", g[:], self.GGD[l, 1, r], [self.GGD.bufs[(l * 2 + 1) * 3 + r]], [g.b], f"GGm{r}")
            GG.append(g)
        self.alloc_norm_tmp(2)
        self.alloc_post_tmp()
        self.pjunk = k.sb("pjunk", [128, D], BF16)
        ntt = NTX if final else NT
        tiles = [(b, tt) for b in range(NB) for tt in range(ntt)]
        blocks = [tiles[i:i + 2] for i in range(0, len(tiles), 2)]
        xts = [k.sb(f"xm{i}", [128, D], F32) for i in range(4)]
        hxT = [k.sb(f"hxTm{i}", [128, 8, 256], BF16, nsub=4) for i in range(2)]
        hT = [k.sb(f"hTm{i}", [128, 256], BF16) for i in range(4)]
        rl = [k.sb(f"rlm{i}", [128, 256], F32) for i in range(2)]
        ysb = [k.sb(f"ysb{i}", [128, D], F32) for i in range(4)]
        out_banks = self.ps[0:4]
        self.ps_pool = self.ps[4:8]
        xi = 0
        blk_x = {}

        def load_block(n):
            nonlocal xi
            xs = []
            for (b, tt) in blocks[n]:
                xt = xts[xi % 4]
                k.dma("sp", xt[:], self.tok_src(b, tt, False), [self.XS.bufs[b * NT + tt]], [xt.b], f"xm{xi % 4}")
                xi += 1
                xs.append(xt)
            blk_x[n] = xs

        blk_tm = {}

        def stats_block(n):
            blk_tm[n] = [self.norm_stats(blk_x[n][i]) for i in range(len(blocks[n]))]

        def tr_block(n):
            hx = hxT[n % 2]
            for i, (b, tt) in enumerate(blocks[n]):
                r = b if tt < NTX else 2
                self.norm_transpose(blk_tm[n][i], self.AS[:, l, 1, 0, r, :], self.AS[:, l, 1, 1, r, :],
                                    lambda kk, hx=hx, i=i: hx[:, kk, i * 128:(i + 1) * 128], hx.bufs[2 * i:2 * i + 2])

        def norm_block(n):
            stats_block(n)
            tr_block(n)

        def post_tile(n, i):
            b, tt = blocks[n][i]
            y = ysb[(2 * n + i) % 4]
            xt = blk_x[n][i]
            r = b if tt < NTX else 2
            self.post_residual_sb(y, xt, GG[r])
            if final:
                k.dma("pool", self.out[b, tt * 128:(tt + 1) * 128, :], xt[:], [xt.b], [], f"st_xm{(2 * n + i) % 4}")
            else:
                k.dma("pool", self.XS[b, tt * 128:(tt + 1) * 128, :], xt[:], [xt.b], [self.XS.bufs[b * NT + tt]], f"st_xm{(2 * n + i) % 4}")

        load_block(0)
        load_block(1)
        norm_block(0)
        SK = 3
        f_ctr = 0
        for n in range(len(blocks)):
            hx = hxT[n % 2]
            for f in range(32 + SK):
                if f < 32:
                    ps1 = self.next_ps()
                    h_ = hT[f_ctr % 4]
                    r_ = rl[f_ctr % 2]
                    f_ctr += 1

                    def mm1(ps1=ps1, f=f, hx=hx):
                        last = None
                        for kk in range(8):
                            last = nc.tensor.matmul(ps1[:, 0:256], w1c[f // 4][:, kk, (f % 4) * 128:(f % 4 + 1) * 128], hx[:, kk, :], start=(kk == 0), stop=(kk == 7))
                        return last
                    k.op("pe", [w1c[f // 4].b] + hx.bufs, [ps1.b], mm1)
                    k.op("act", [ps1.b], [r_.b], lambda ps1=ps1, r_=r_: nc.scalar.activation(out=r_[:], in_=ps1[:, 0:256], func=AF.Relu))
                    k.op("dve", [r_.b], [h_.b], lambda r_=r_, h_=h_: nc.vector.tensor_tensor(out=h_[:], in0=r_[:], in1=r_[:], op=ALU.mult))
                if f >= SK:
                    f2 = f - SK
                    h2 = hT[(f_ctr - (min(f, 31) - f2) - 1) % 4] if False else None
                    h2 = hT[(n * 32 + f2) % 4]

                    def mm2(h2=h2, f2=f2):
                        last = None
                        for i in range(2):
                            for hf in range(2):
                                last = nc.tensor.matmul(out_banks[2 * i + hf][:, :], h2[:, i * 128:(i + 1) * 128],
                                                        w2c[f2 // 4][:, f2 % 4, hf * 512:(hf + 1) * 512], start=(f2 == 0), stop=(f2 == 31))
                        return last
                    k.op("pe", [h2.b, w2c[f2 // 4].b], [ob.b for ob in out_banks], mm2)
                if n >= 1 and f in (3, 7):
                    post_tile(n - 1, 0 if f == 3 else 1)
                if f == 8 and n + 1 < len(blocks) and n >= 1:
                    load_block(n + 1)
                if f == 12 and n + 1 < len(blocks):
                    stats_block(n + 1)
                if f == 26 and n + 1 < len(blocks):
                    tr_block(n + 1)
            for i, (b, tt) in enumerate(blocks[n]):
                y = ysb[(2 * n + i) % 4]
                for hf in range(2):
                    k.op("act", [out_banks[2 * i + hf].b], [y.b], lambda y=y, i=i, hf=hf: nc.scalar.copy(
                        out=y[:, hf * 512:(hf + 1) * 512], in_=out_banks[2 * i + hf][:, :]))
        post_tile(len(blocks) - 1, 0)
        post_tile(len(blocks) - 1, 1)
        self.ps_pool = self.ps
        k.release(m0)
        k.release_top(mTop)

    def post_residual_sb(self, y, xt, GG):
        k, nc = self.k, self.k.nc
        tm = self.ptmp[self.ptmp_i % 2]
        self.ptmp_i += 1
        ss, rstd = tm["ss"], tm["rstd"]
        junk = self.pjunk
        k.op("act", [y.b], [junk.b, ss.b], lambda: nc.scalar.activation(
            out=junk[:], in_=y[:], func=AF.Square, accum_out=ss[:, 0:1]))
        k.op("act", [ss.b, self.epsc.b], [rstd.b], lambda: nc.scalar.activation(
            out=rstd[:], in_=ss[:, 0:1], func=AF.Sqrt, scale=1.0 / D, bias=self.epsc[:, 0:1]))
        k.op("dve", [rstd.b], [rstd.b], lambda: nc.vector.reciprocal(out=rstd[:], in_=rstd[:]))
        k.op("dve", [y.b, rstd.b, GG.b], [y.b], lambda: nc.vector.scalar_tensor_tensor(
            out=y[:], in0=y[:], scalar=rstd[:, 0:1], in1=GG[:], op0=ALU.mult, op1=ALU.mult))
        k.op("dve", [y.b, xt.b], [xt.b], lambda: nc.vector.tensor_tensor(out=xt[:], in0=xt[:], in1=y[:], op=ALU.add))

    def phase_attn(self, b):
        k, nc, io = self.k, self.k.nc, self.io
        l = 1
        m0 = k.mark()
        QT = k.sb("QT", [128, NH, L], BF16, nsub=NTX)
        KT = k.sb("KT", [128, NKV, T], BF16, nsub=NT)
        V = k.sb("V", [128, NT, NKV, HD], BF16, nsub=NT)
        mA = k.mark()
        hxT = k.sb("hxTa", [128, 8, T], BF16, nsub=2 * NT)
        wqkv = k.sb("wqkv", [128, 8, 1536], BF16)
        k.dma("pool", wqkv[:], io["attn_w_qkv"].rearrange("(k p) n -> p k n", p=128), [], [wqkv.b], "wqkv")
        ropec = k.sb("ropec", [128, NTX, 64], F32)
        ropes = k.sb("ropes", [128, NTX, 64], F32)
        k.dma("sp", ropec[:], io["ropec"], [], [ropec.b], "c_grp2")
        k.dma("sp", ropes[:], io["ropes"], [], [ropes.b], "c_grp2")
        qg = k.sb("qg", [128, HD], F32)
        kg = k.sb("kg", [128, HD], F32)
        k.dma("sp", qg[:], io["rows"][ROW_QN:ROW_QN + 1, 0:HD].partition_broadcast(128), [], [qg.b], "c_grp2")
        k.dma("sp", kg[:], io["rows"][ROW_KN:ROW_KN + 1, 0:HD].partition_broadcast(128), [], [kg.b], "c_grp2")
        k.group_done("c_grp2", [ropec, ropes, qg, kg])
        self.alloc_norm_tmp(3)
        xts = [k.sb(f"xa{i}", [128, D], F32) for i in range(3)]
        sq = [k.sb(f"sq{i}", [128, 10, HD], F32) for i in range(2)]
        qn = [k.sb(f"qn{i}", [128, 10, HD], F32) for i in range(2)]
        qr = [k.sb(f"qr{i}", [128, 10, HD], BF16) for i in range(2)]
        ss10 = [k.sb(f"ss10_{i}", [128, 10], F32) for i in range(2)]
        ra = [k.sb(f"ra{i}", [128, 10, 64], F32) for i in range(2)]
        rb_ = [k.sb(f"rb{i}", [128, 10, 64], F32) for i in range(2)]
        rc_ = [k.sb(f"rc{i}", [128, 10, 64], F32) for i in range(2)]
        rd_ = [k.sb(f"rd{i}", [128, 10, 64], F32) for i in range(2)]
        POOL = nc.gpsimd

        def stA(tt):
            xt = xts[tt % 3]
            k.dma("sp", xt[:], self.tok_src(b, tt, False), [self.XS.bufs[b * NT + tt]], [xt.b], f"xa{tt % 3}")
            return self.norm_stats(xt)

        def stB(tt, tm):
            r = b if tt < NTX else 2
            self.norm_transpose(tm, self.AS[:, l, 0, 0, r, :], self.AS[:, l, 0, 1, r, :],
                                lambda kk, tt=tt: hxT[:, kk, tt * 128:(tt + 1) * 128], hxT.bufs[2 * tt:2 * tt + 2])

        def stC(tt):
            isx = tt < NTX
            h0 = 0 if isx else 8
            banks = []
            if isx:
                for c in range(2):
                    ps = self.next_ps()

                    def mmq(ps=ps, c=c, tt=tt):
                        last = None
                        for kk in range(8):
                            last = nc.tensor.matmul(ps[:, :], hxT[:, kk, tt * 128:(tt + 1) * 128], wqkv[:, kk, c * 512:(c + 1) * 512],
                                                    start=(kk == 0), stop=(kk == 7))
                        return last
                    k.op("pe", hxT.bufs[2 * tt:2 * tt + 2] + [wqkv.b], [ps.b], mmq)
                    banks.append(ps)
            pskv = self.next_ps()

            def mmkv(ps=pskv, tt=tt):
                last = None
                for kk in range(8):
                    last = nc.tensor.matmul(ps[:, :], hxT[:, kk, tt * 128:(tt + 1) * 128], wqkv[:, kk, 1024:1536],
                                            start=(kk == 0), stop=(kk == 7))
                return last
            k.op("pe", hxT.bufs[2 * tt:2 * tt + 2] + [wqkv.b], [pskv.b], mmkv)
            sq_, qn_, qr_, ss_ = sq[tt % 2], qn[tt % 2], qr[tt % 2], ss10[tt % 2]
            ra_, rbb, rcc, rdd = ra[tt % 2], rb_[tt % 2], rc_[tt % 2], rd_[tt % 2]
            k.op("act", [pskv.b], [V.bufs[tt]], lambda pskv=pskv, tt=tt: nc.scalar.copy(
                out=V[:, tt, :, :], in_=pskv[:, 256:512].rearrange("p (g d) -> p g d", d=HD)))
            srcs = []
            if isx:
                srcs += [(banks[0], 0, 512, 0, 4), (banks[1], 0, 512, 4, 4)]
            srcs.append((pskv, 0, 256, 8, 2))
            for (ps, c0, ncol, hs, nhh) in srcs:
                k.op("act", [ps.b], [sq_.b], lambda ps=ps, c0=c0, ncol=ncol, hs=hs, nhh=nhh: nc.scalar.activation(
                    out=sq_[:, hs:hs + nhh, :], in_=ps[:, c0:c0 + ncol].rearrange("p (h d) -> p h d", d=HD), func=AF.Square))
            k.op("dve", [sq_.b], [ss_.b], lambda: nc.vector.tensor_reduce(
                out=ss_[:, h0:10], in_=sq_[:, h0:10, :], axis=AX.X, op=ALU.add))
            k.op("act", [ss_.b, self.epsc.b], [ss_.b], lambda: nc.scalar.activation(
                out=ss_[:, h0:10], in_=ss_[:, h0:10], func=AF.Sqrt, scale=1.0 / HD, bias=self.epsc[:, 0:1]))
            k.op("dve", [ss_.b], [ss_.b], lambda: nc.vector.reciprocal(out=ss_[:, h0:10], in_=ss_[:, h0:10]))
            if isx:
                for (ps, c0, ncol, hs, nhh) in srcs:
                    for hh in range(nhh):
                        gt_ = qg if hs + hh < 8 else kg
                        k.op("dve", [ps.b, ss_.b, gt_.b], [qn_.b], lambda ps=ps, c0=c0, hs=hs, hh=hh, gt_=gt_: nc.vector.scalar_tensor_tensor(
                            out=qn_[:, hs + hh, :], in0=ps[:, c0 + hh * HD:c0 + (hh + 1) * HD], scalar=ss_[:, hs + hh:hs + hh + 1],
                            in1=gt_[:, :], op0=ALU.mult, op1=ALU.mult))
                x0 = qn_[:, :, :].rearrange("p h (i two) -> p h i two", two=2)[:, :, :, 0]
                x1 = qn_[:, :, :].rearrange("p h (i two) -> p h i two", two=2)[:, :, :, 1]
                o0 = qr_[:, :, :].rearrange("p h (i two) -> p h i two", two=2)[:, :, :, 0]
                o1 = qr_[:, :, :].rearrange("p h (i two) -> p h i two", two=2)[:, :, :, 1]
                cosb = ropec[:, tt, :].unsqueeze(1).broadcast_to([128, 10, 64])
                sinb = ropes[:, tt, :].unsqueeze(1).broadcast_to([128, 10, 64])
                k.op("dve", [qn_.b, ropec.b], [ra_.b], lambda: nc.vector.tensor_tensor(out=ra_[:], in0=x0, in1=cosb, op=ALU.mult))
                k.op("dve", [qn_.b, ropes.b], [rbb.b], lambda: nc.vector.tensor_tensor(out=rbb[:], in0=x1, in1=sinb, op=ALU.mult))
                k.op("dve", [ra_.b, rbb.b], [qr_.b], lambda: nc.vector.tensor_tensor(out=o0, in0=ra_[:], in1=rbb[:], op=ALU.subtract))
                k.op("dve", [qn_.b, ropes.b], [rcc.b], lambda: nc.vector.tensor_tensor(out=rcc[:], in0=x0, in1=sinb, op=ALU.mult))
                k.op("dve", [qn_.b, ropec.b], [rdd.b], lambda: nc.vector.tensor_tensor(out=rdd[:], in0=x1, in1=cosb, op=ALU.mult))
                k.op("dve", [rcc.b, rdd.b], [qr_.b], lambda: nc.vector.tensor_tensor(out=o1, in0=rcc[:], in1=rdd[:], op=ALU.add))
            else:
                for hh in range(2):
                    k.op("dve", [pskv.b, ss_.b, kg.b], [qr_.b], lambda hh=hh: nc.vector.scalar_tensor_tensor(
                        out=qr_[:, 8 + hh, :], in0=pskv[:, hh * HD:(hh + 1) * HD], scalar=ss_[:, 8 + hh:9 + hh],
                        in1=kg[:, :], op0=ALU.mult, op1=ALU.mult))

        def stD(tt):
            isx = tt < NTX
            h0 = 0 if isx else 8
            qr_ = qr[tt % 2]
            ps = self.next_ps()
            pb = self.psb(ps)

            def trq(pb=pb, qr_=qr_, h0=h0):
                last = None
                for hh in range(h0, 8 if h0 == 0 else 10):
                    last = nc.tensor.transpose(pb[:, (hh - h0) * 128:(hh - h0 + 1) * 128], qr_[:, hh, :], self.ident[:])
                return last
            k.op("pe", [qr_.b, self.ident.b], [ps.b], trq)
            if isx:
                k.op("act", [ps.b], [QT.bufs[tt]], lambda pb=pb, tt=tt: nc.scalar.copy(
                    out=QT[:, :, tt * 128:(tt + 1) * 128], in_=pb[:, 0:1024].rearrange("p (h t) -> p h t", t=128)))
                ps2 = self.next_ps()
                pb2 = self.psb(ps2)
                k.op("pe", [qr_.b, self.ident.b], [ps2.b], lambda pb2=pb2, qr_=qr_: [nc.tensor.transpose(
                    pb2[:, g * 128:(g + 1) * 128], qr_[:, 8 + g, :], self.ident[:]) for g in range(2)][-1])
                k.op("act", [ps2.b], [KT.bufs[tt]], lambda pb2=pb2, tt=tt: nc.scalar.copy(
                    out=KT[:, :, tt * 128:(tt + 1) * 128], in_=pb2[:, 0:256].rearrange("p (g t) -> p g t", t=128)))
            else:
                k.op("act", [ps.b], [KT.bufs[tt]], lambda pb=pb, tt=tt: nc.scalar.copy(
                    out=KT[:, :, tt * 128:(tt + 1) * 128], in_=pb[:, 0:256].rearrange("p (g t) -> p g t", t=128)))

        tms = {}
        for i in range(NT + 3):
            if i < NT:
                tms[i] = stA(i)
            if 0 <= i - 1 < NT:
                stB(i - 1, tms.pop(i - 1))
            if 0 <= i - 2 < NT:
                stC(i - 2)
            if 0 <= i - 3 < NT:
                stD(i - 3)
        if "QT" in self.taps and b == 0:
            k.dma("sp", self.tap("QT", [128, NH * L], BF16), QT[:].rearrange("p a c -> p (a c)"), QT.bufs, [], "tap")
            k.dma("sp", self.tap("KT", [128, NKV * T], BF16), KT[:].rearrange("p a c -> p (a c)"), KT.bufs, [], "tap")
            k.dma("sp", self.tap("V", [128, NT * NKV * HD], BF16), V[:].rearrange("p a c d -> p (a c d)"), V.bufs, [], "tap")
        k.release(mA)
        OT = k.sb("OT", [128, NH, L], BF16, nsub=NTX)
        PT = [k.sb(f"PT{i}", [128, 512], BF16) for i in range(4)]
        rec = [k.sb(f"rec{i}", [128, 512], F32) for i in range(2)]
        s_banks = self.ps[0:4]
        o_banks = self.ps[4:6]
        d_banks = self.ps[6:8]
        SCALE = float(HD) ** -0.5
        SK = 2
        it = 0
        sc = 0
        for g in range(NKV):
            for hq in range(NH // NKV):
                h = g * (NH // NKV) + hq
                for qc in range(L // 512):
                    qsl = slice(qc * 512, (qc + 1) * 512)
                    qbufs = QT.bufs[qc * 4:qc * 4 + 4]
                    pso = o_banks[it % 2]
                    psd = d_banks[it % 2]
                    base = sc
                    for kc in range(NT + SK):
                        if kc < NT:
                            pss = s_banks[sc % 4]
                            pt = PT[sc % 4]
                            sc += 1
                            k.op("pe", [KT.bufs[kc]] + qbufs, [pss.b], lambda pss=pss, kc=kc, g=g, h=h, qsl=qsl: nc.tensor.matmul(
                                pss[:, :], KT[:, g, kc * 128:(kc + 1) * 128], QT[:, h, qsl], start=True, stop=True))
                            k.op("act", [pss.b], [pt.b], lambda pss=pss, pt=pt: nc.scalar.activation(
                                out=pt[:], in_=pss[:, :], func=AF.Exp, scale=SCALE))
                        if kc >= SK:
                            k2_ = kc - SK
                            pt2 = PT[(base + k2_) % 4]

                            def mmpv(pt2=pt2, k2_=k2_, g=g, pso=pso, psd=psd):
                                nc.tensor.matmul(pso[:, :], V[:, k2_, g, :], pt2[:], start=(k2_ == 0), stop=(k2_ == NT - 1))
                                return nc.tensor.matmul(psd[:, :], self.ones[:, :], pt2[:], start=(k2_ == 0), stop=(k2_ == NT - 1))
                            k.op("pe", [pt2.b, V.bufs[k2_], self.ones.b], [pso.b, psd.b], mmpv)
                    rc = rec[it % 2]
                    k.op("dve", [psd.b], [rc.b], lambda psd=psd, rc=rc: nc.vector.reciprocal(out=rc[:], in_=psd[:, :]))
                    k.op("dve", [pso.b, rc.b], OT.bufs[qc * 4:qc * 4 + 4], lambda pso=pso, rc=rc, h=h, qsl=qsl: nc.vector.tensor_tensor(
                        out=OT[:, h, qsl], in0=pso[:, :], in1=rc[:], op=ALU.mult))
                    it += 1
        if "OT" in self.taps and b == 0:
            k.dma("sp", self.tap("OT", [128, NH * L], BF16), OT[:].rearrange("p a c -> p (a c)"), OT.bufs, [], "tap")
        wo = k.sb("at_wo", [128, NH, D], BF16)
        k.dma("pool", wo[:], io["attn_w_o"].rearrange("(h p) n -> p h n", p=128), [], [wo.b], "at_wo")
        GGx = k.sb("GGa", [128, D], F32)
        k.dma("sp", GGx[:], self.GGD[l, 0, b], [self.GGD.bufs[(l * 2) * 3 + b]], [GGx.b], "GGa")
        self.alloc_post_tmp()
        xts = [k.sb(f"xao{i}", [128, D], F32) for i in range(3)]
        for tt in range(NTX):
            xt = xts[tt % 3]
            k.dma("sp", xt[:], self.tok_src(b, tt, False), [self.XS.bufs[b * NT + tt]], [xt.b], f"xao{tt % 3}")
            ps2 = [self.next_ps(), self.next_ps()]
            for hf in range(2):
                def mmo(hf=hf, ps=ps2[hf], tt=tt):
                    last = None
                    for h in range(NH):
                        last = nc.tensor.matmul(ps[:, :], OT[:, h, tt * 128:(tt + 1) * 128], wo[:, h, hf * 512:(hf + 1) * 512],
                                                start=(h == 0), stop=(h == NH - 1))
                    return last
                k.op("pe", [OT.bufs[tt], wo.b], [ps2[hf].b], mmo)
            self.post_residual(ps2, xt, GGx)
            k.dma("pool", self.XS[b, tt * 128:(tt + 1) * 128, :], xt[:], [xt.b], [self.XS.bufs[b * NT + tt]], f"st_xao{tt % 3}")
        k.release(m0)

    def finish(self):
        self.k.final_wait()
        return self.k.nc


def make_in_maps(inp, ncores=8):
    cc = get_consts()
    f32 = np.float32
    shared = {
        "mod_w": np.ascontiguousarray(inp["mod_w"], f32),
        "colv": build_colv(inp),
        "rows": build_rows(inp, cc),
        "mlp_w1": np.ascontiguousarray(inp["mlp_w1"], f32),
        "mlp_w2": np.ascontiguousarray(inp["mlp_w2"], f32),
        "hy_w_in": np.ascontiguousarray(inp["hy_w_in"][0], f32),
        "hy_w_out": np.ascontiguousarray(inp["hy_w_out"][0], f32),
        "attn_w_qkv": np.ascontiguousarray(inp["attn_w_qkv"][0], f32),
        "attn_w_o": np.ascontiguousarray(inp["attn_w_o"][0], f32),
        "f_w1": np.ascontiguousarray(inp["hy_filt_w1"][0], f32),
        "f_w2": np.ascontiguousarray(inp["hy_filt_w2"][0], f32),
        "f_w3": np.ascontiguousarray(inp["hy_filt_w3"][0], f32),
        "fkt": cc["fkt"], "gt": cc["gt"], "fktc": cc["fktc"], "gtc": cc["gtc"],
        "zT": cc["zT"], "tcol": cc["tcol"], "ident": cc["ident"], "sel3": cc["sel3"],
        "ropec": cc["ropec"], "ropes": cc["ropes"],
    }
    maps = []
    for i in range(ncores):
        b0 = i * NB
        cT = np.stack([inp["c"][b0], inp["c"][b0 + 1], inp["c_ctx"]], axis=-1)
        cT = np.ascontiguousarray(cT.reshape(8, 128, 3).transpose(1, 0, 2), f32)
        m = dict(shared)
        m["x"] = np.ascontiguousarray(inp["x"][b0:b0 + NB], f32)
        m["ctx"] = np.ascontiguousarray(inp["ctx"][b0:b0 + NB], f32)
        m["cT"] = cT
        maps.append(m)
    return maps


_PROG_CACHE = {}


def build_program():
    p = Prog()
    p.load_consts()
    p.phase_filter(p.phase_mod_gen())
    for b in range(NB):
        p.phase_hyena(b)
    p.phase_mlp(0, False)
    for b in range(NB):
        p.phase_attn(b)
    p.phase_mlp(1, True)
    return p.finish()


def kernel(**inputs):
    inp = {k_: np.asarray(v) for k_, v in inputs.items()}
    n_cores = 8
    maps = make_in_maps(inp, n_cores)
    nc = build_program()
    res = run_bass_kernel_spmd(nc, maps, core_ids=list(range(n_cores)))
    outs = [np.asarray(res.results[i]["out"], dtype=np.float32) for i in range(n_cores)]
    return np.concatenate(outs, axis=0)
```
